# Optimizing a Trainium2 kernel written in Bass

```python
import jax, jax.numpy as jnp
from jax import lax
import numpy as np

D_MODEL = 2048
BATCH = 2
SEQ = 16384
DEPTH = 2

CHUNK = 64
N_GROUPS = 4
W_G = D_MODEL // N_GROUPS
D_MIX = N_GROUPS * W_G
GMLP_BLOCK = 128
GMLP_HEADS = 4
GMLP_HEAD_DIM = W_G // GMLP_HEADS
MLA_NOPE = 128
MLA_ROPE = 64
MLA_V = 128
MLA_HEADS = W_G // MLA_V
Q_LORA = 3 * D_MODEL // 16
KV_LORA = D_MODEL // 8
ROPE_THETA = 10000.0
Q_BLOCK = 128
SCONV_K = 3
CONF_K = 31
D_FF = 11 * D_MODEL // 4
FFN_K = 3
LN_EPS = 1e-5
RMS_EPS = 1e-6
ALPHA = (2.0 * DEPTH) ** 0.25
BETA = (8.0 * DEPTH) ** -0.25
IN_WIDTHS = (2 * W_G, Q_LORA, KV_LORA, MLA_ROPE, W_G, W_G, W_G, 2 * W_G)
IN_COLS = sum(IN_WIDTHS)
IN_SPLITS = tuple(sum(IN_WIDTHS[:i + 1]) for i in range(len(IN_WIDTHS) - 1))

kernel_name = "hybrid_headgroup_streaming_encoder"


def layer_norm(x, gain=None, bias=None, eps=LN_EPS):
    xf = x.astype(jnp.float32)
    mu = jnp.mean(xf, axis=-1, keepdims=True)
    var = jnp.mean(jnp.square(xf - mu), axis=-1, keepdims=True)
    y = (xf - mu) * lax.rsqrt(var + eps)
    if gain is not None:
        y = y * gain.astype(jnp.float32) + bias.astype(jnp.float32)
    return y.astype(x.dtype)


def rms_norm(x, gain, eps=RMS_EPS):
    xf = x.astype(jnp.float32)
    y = xf * lax.rsqrt(jnp.mean(jnp.square(xf), axis=-1, keepdims=True) + eps)
    return (y * gain.astype(jnp.float32)).astype(x.dtype)


def causal_dwconv(x, w, b=None):
    k = w.shape[0]
    y = lax.conv_general_dilated(
        x, w[:, None, :].astype(x.dtype), window_strides=(1,), padding=[(k - 1, 0)],
        dimension_numbers=("NWC", "WIO", "NWC"), feature_group_count=x.shape[-1])
    if b is not None:
        y = y + b.astype(x.dtype)
    return y


def apply_rope(x, cos, sin):
    x1, x2 = jnp.split(x, 2, axis=-1)
    return jnp.concatenate([x1 * cos - x2 * sin, x2 * cos + x1 * sin], axis=-1)


def gmlp_mixer(uv, ln_g, ln_b, w_s, b_s):
    z = jax.nn.gelu(uv)
    u, v = jnp.split(z, 2, axis=-1)
    v = layer_norm(v, ln_g, ln_b)
    bn, s, _ = v.shape
    vb = v.reshape(bn, s // GMLP_BLOCK, GMLP_BLOCK, GMLP_HEADS, GMLP_HEAD_DIM)
    cpos = jnp.arange(GMLP_BLOCK) // CHUNK
    mask = cpos[None, :] <= cpos[:, None]
    w = jnp.where(mask[None], w_s, 0.0).astype(v.dtype)
    mixed = jnp.einsum("hij,bnjhd->bnihd", w, vb) + b_s.T.astype(v.dtype)[:, :, None]
    return u * mixed.reshape(bn, s, W_G)


def mla_mixer(q_c, kv_c, k_r, q_norm_g, w_uq, kv_norm_g, w_ukv, cos, sin):
    bn, s, _ = q_c.shape
    q = (rms_norm(q_c, q_norm_g) @ w_uq).reshape(bn, s, MLA_HEADS, MLA_NOPE + MLA_ROPE)
    q_nope, q_rope = q[..., :MLA_NOPE], q[..., MLA_NOPE:]
    q_rope = apply_rope(q_rope, cos[:, None, :], sin[:, None, :])
    kv = (rms_norm(kv_c, kv_norm_g) @ w_ukv).reshape(bn, s, MLA_HEADS, MLA_NOPE + MLA_V)
    k_nope, v = kv[..., :MLA_NOPE], kv[..., MLA_NOPE:]
    k_rope = apply_rope(k_r, cos, sin)
    scale = (MLA_NOPE + MLA_ROPE) ** -0.5
    nqb = s // Q_BLOCK
    qn_b = q_nope.reshape(bn, nqb, Q_BLOCK, MLA_HEADS, MLA_NOPE).transpose(1, 0, 2, 3, 4)
    qr_b = q_rope.reshape(bn, nqb, Q_BLOCK, MLA_HEADS, MLA_ROPE).transpose(1, 0, 2, 3, 4)
    k_chunk = jnp.arange(s) // CHUNK
    qc_b = k_chunk.reshape(nqb, Q_BLOCK)

    def block(args):
        qn, qr, qc = args
        sc = (jnp.einsum("bqhd,bkhd->bhqk", qn, k_nope)
              + jnp.einsum("bqhr,bkr->bhqk", qr, k_rope)).astype(jnp.float32) * scale
        mask = k_chunk[None, :] <= qc[:, None]
        sc = jnp.where(mask[None, None], sc, -jnp.inf)
        p = jax.nn.softmax(sc, axis=-1).astype(v.dtype)
        return jnp.einsum("bhqk,bkhd->bqhd", p, v)

    o = lax.map(block, (qn_b, qr_b, qc_b))
    return o.transpose(1, 0, 2, 3, 4).reshape(bn, s, MLA_HEADS * MLA_V)


def short_conv_mixer(b_gate, c_gate, h, w_conv):
    return b_gate * causal_dwconv(c_gate * h, w_conv)


def conformer_conv(ag, w_dw, b_dw, ln_g, ln_b):
    a, g = jnp.split(ag, 2, axis=-1)
    y = a * jax.nn.sigmoid(g)
    y = causal_dwconv(y, w_dw, b_dw)
    return jax.nn.silu(layer_norm(y, ln_g, ln_b))


def conv_ffn(h, w_up, w_conv, b_conv, w_down):
    up = h @ w_up
    gate, val = jnp.split(up, 2, axis=-1)
    gate = causal_dwconv(gate, w_conv, b_conv)
    return (jax.nn.silu(gate) * val) @ w_down


def setup_inputs(seed: int = 0) -> dict:
    key = jax.random.key(seed)
    ks = jax.random.split(key, 32)
    L = DEPTH

    def nrm(k, shape, scale):
        return jax.random.normal(k, shape, jnp.float32) * scale

    return {
        "x": nrm(ks[0], (BATCH, SEQ, D_MODEL), 1.0),
        "c": nrm(ks[1], (BATCH, D_MODEL), 1.0),
        "w_mod": nrm(ks[2], (L, D_MODEL, 6 * D_MODEL), 0.5 * D_MODEL ** -0.5),
        "b_mod": nrm(ks[3], (L, 6 * D_MODEL), 0.01),
        "w_in": nrm(ks[4], (L, D_MODEL, IN_COLS), D_MODEL ** -0.5),
        "gmlp_ln_g": 1.0 + nrm(ks[5], (L, W_G), 0.02),
        "gmlp_ln_b": nrm(ks[6], (L, W_G), 0.02),
        "gmlp_w_s": nrm(ks[7], (L, GMLP_HEADS, GMLP_BLOCK, GMLP_BLOCK), GMLP_BLOCK ** -0.5),
        "gmlp_b_s": nrm(ks[8], (L, GMLP_HEADS, GMLP_BLOCK), 0.02),
        "mla_q_norm": 1.0 + nrm(ks[9], (L, Q_LORA), 0.02),
        "mla_w_uq": nrm(ks[10], (L, Q_LORA, MLA_HEADS * (MLA_NOPE + MLA_ROPE)), Q_LORA ** -0.5),
        "mla_kv_norm": 1.0 + nrm(ks[11], (L, KV_LORA), 0.02),
        "mla_w_ukv": nrm(ks[12], (L, KV_LORA, MLA_HEADS * (MLA_NOPE + MLA_V)), KV_LORA ** -0.5),
        "sconv_w": nrm(ks[13], (L, SCONV_K, W_G), SCONV_K ** -0.5),
        "conf_w_dw": nrm(ks[14], (L, CONF_K, W_G), CONF_K ** -0.5),
        "conf_b_dw": nrm(ks[15], (L, W_G), 0.02),
        "conf_ln_g": 1.0 + nrm(ks[16], (L, W_G), 0.02),
        "conf_ln_b": nrm(ks[17], (L, W_G), 0.02),
        "w_out": nrm(ks[18], (L, D_MIX, D_MODEL), BETA * D_MIX ** -0.5),
        "post_mix_g": 1.0 + nrm(ks[19], (L, D_MODEL), 0.02),
        "post_mix_b": nrm(ks[20], (L, D_MODEL), 0.02),
        "ffn_w_up": nrm(ks[21], (L, D_MODEL, 2 * D_FF), D_MODEL ** -0.5),
        "ffn_w_conv": nrm(ks[22], (L, FFN_K, D_FF), FFN_K ** -0.5),
        "ffn_b_conv": nrm(ks[23], (L, D_FF), 0.02),
        "ffn_w_down": nrm(ks[24], (L, D_FF, D_MODEL), BETA * D_FF ** -0.5),
        "post_ffn_g": 1.0 + nrm(ks[25], (L, D_MODEL), 0.02),
        "post_ffn_b": nrm(ks[26], (L, D_MODEL), 0.02),
    }


def reference(x, c, w_mod, b_mod, w_in, gmlp_ln_g, gmlp_ln_b, gmlp_w_s, gmlp_b_s,
              mla_q_norm, mla_w_uq, mla_kv_norm, mla_w_ukv, sconv_w, conf_w_dw, conf_b_dw,
              conf_ln_g, conf_ln_b, w_out, post_mix_g, post_mix_b, ffn_w_up, ffn_w_conv,
              ffn_b_conv, ffn_w_down, post_ffn_g, post_ffn_b):
    bn, s, d = x.shape
    pos = jnp.arange(s, dtype=jnp.float32)
    inv_freq = ROPE_THETA ** (-jnp.arange(0, MLA_ROPE, 2, dtype=jnp.float32) / MLA_ROPE)
    ang = pos[:, None] * inv_freq[None, :]
    cos = jnp.cos(ang).astype(x.dtype)
    sin = jnp.sin(ang).astype(x.dtype)
    c_act = jax.nn.silu(c)

    for l in range(DEPTH):
        mod = (c_act @ w_mod[l] + b_mod[l]).reshape(bn, 6, d)
        shift_m, scale_m, gate_m = mod[:, 0, None], mod[:, 1, None], mod[:, 2, None]
        shift_f, scale_f, gate_f = mod[:, 3, None], mod[:, 4, None], mod[:, 5, None]

        h = layer_norm(x) * (1.0 + scale_m) + shift_m
        proj = h @ w_in[l]
        uv, q_c, kv_c, k_r, sb, sc, sh, ag = jnp.split(proj, IN_SPLITS, axis=-1)
        y_a = gmlp_mixer(uv, gmlp_ln_g[l], gmlp_ln_b[l], gmlp_w_s[l], gmlp_b_s[l])
        y_b = mla_mixer(q_c, kv_c, k_r, mla_q_norm[l], mla_w_uq[l], mla_kv_norm[l],
                        mla_w_ukv[l], cos, sin)
        y_c = short_conv_mixer(sb, sc, sh, sconv_w[l])
        y_d = conformer_conv(ag, conf_w_dw[l], conf_b_dw[l], conf_ln_g[l], conf_ln_b[l])
        y = jnp.concatenate([y_a, y_b, y_c, y_d], axis=-1) @ w_out[l]
        x = layer_norm(ALPHA * x + (1.0 + gate_m) * y, post_mix_g[l], post_mix_b[l])

        h = layer_norm(x) * (1.0 + scale_f) + shift_f
        y = conv_ffn(h, ffn_w_up[l], ffn_w_conv[l], ffn_b_conv[l], ffn_w_down[l])
        x = layer_norm(ALPHA * x + (1.0 + gate_f) * y, post_ffn_g[l], post_ffn_b[l])
    return x
```

```python
import numpy as np
import ml_dtypes
import concourse.bass as bass
import concourse.mybir as mybir
from concourse.bass_utils import run_bass_kernel_spmd

F32 = mybir.dt.float32
BF16 = mybir.dt.bfloat16
AF = mybir.ActivationFunctionType
ALU = mybir.AluOpType

D = 2048
NCH = 16
DFF = 5632
NFF = 44
HALO = 128
ALPHA = (2.0 * 2) ** 0.25
LN_EPS = 1e-5
RMS_EPS = 1e-6
ATT_SCALE = 192.0 ** -0.5
NEG = -30000.0

ENGS = ("pe", "act", "dve", "pool", "sp")
EPOCH = 30000
NDMA = {"sp": 40, "pool": 24, "act": 1}


class Op:
    __slots__ = ("eng", "fn", "deps", "dma", "idx", "signal", "sem", "val")

    def __init__(self, eng, fn, deps, dma, idx):
        self.eng = eng
        self.fn = fn
        self.deps = deps
        self.dma = dma
        self.idx = idx
        self.signal = False
        self.sem = None
        self.val = 0


class Prog:
    def __init__(self, nc):
        self.nc = nc
        self.ops = []
        self.lastw = {}
        self.readers = {}
        self.rot = {}

    def op(self, eng, fn, reads=(), writes=(), dma=False):
        if getattr(self, "deferring", False):
            self.defer(lambda: self._op(eng, fn, reads, writes, dma))
            return None
        return self._op(eng, fn, reads, writes, dma)

    def _op(self, eng, fn, reads=(), writes=(), dma=False):
        extra = [b for b in reads if isinstance(b, tuple) and b[0] == "ps" and b not in writes]
        if extra and eng != "pe":
            writes = list(writes) + extra
        deps = set()
        for b in reads:
            w = self.lastw.get(b)
            if w is not None:
                deps.add(w)
        for b in writes:
            w = self.lastw.get(b)
            if w is not None:
                deps.add(w)
            for r in self.readers.get(b, ()):
                deps.add(r)
        idx = len(self.ops)
        self.ops.append(Op(eng, fn, deps, dma, idx))
        for b in reads:
            self.readers.setdefault(b, []).append(idx)
        for b in writes:
            self.lastw[b] = idx
            self.readers[b] = []
        return idx

    def dma(self, eng, out, in_, reads, writes):
        self.op(eng, lambda e, o=out, i=in_: e.dma_start(out=o, in_=i), reads, writes, dma=True)

    def mm(self, out, lhsT, rhs, start, stop, reads, writes):
        self.op("pe", lambda e, o=out, l=lhsT, r=rhs, s=start, t=stop: e.matmul(o, lhsT=l, rhs=r, start=s, stop=t),
                reads, writes)

    def act(self, out, in_, func, reads, writes, bias=None, scale=None):
        kw = {}
        if bias is not None:
            kw["bias"] = bias
        if scale is not None:
            kw["scale"] = scale
        self.op("act", lambda e, o=out, i=in_, f=func, k=kw: e.activation(out=o, in_=i, func=f, **k), reads, writes)

    def tt(self, eng, out, in0, in1, op, reads, writes):
        self.op(eng, lambda e, o=out, a=in0, b=in1, p=op: e.tensor_tensor(out=o, in0=a, in1=b, op=p), reads, writes)

    def ts(self, eng, out, in0, s1, s2, op0, op1, reads, writes):
        if s2 is None:
            self.op(eng, lambda e, o=out, a=in0, x=s1, p=op0: e.tensor_scalar(out=o, in0=a, scalar1=x, scalar2=None, op0=p),
                    reads, writes)
        else:
            self.op(eng, lambda e, o=out, a=in0, x=s1, y=s2, p=op0, q=op1: e.tensor_scalar(
                out=o, in0=a, scalar1=x, scalar2=y, op0=p, op1=q), reads, writes)

    def stt(self, eng, out, in0, scalar, in1, op0, op1, reads, writes):
        self.op(eng, lambda e, o=out, a=in0, s=scalar, b=in1, p=op0, q=op1: e.scalar_tensor_tensor(
            out=o, in0=a, scalar=s, in1=b, op0=p, op1=q), reads, writes)

    def cp(self, eng, out, in_, reads, writes):
        self.op(eng, lambda e, o=out, i=in_: e.tensor_copy(out=o, in_=i), reads, writes)

    def ms(self, eng, ap, val, writes):
        self.op(eng, lambda e, a=ap, v=val: e.memset(a, v), (), writes)

    def rcp(self, out, in_, reads, writes):
        self.op("dve", lambda e, o=out, i=in_: e.reciprocal(out=o, in_=i), reads, writes)

    def defer(self, thunk):
        if not hasattr(self, "deferred"):
            self.deferred = []
        self.deferred.append(thunk)

    def flush(self, n=None):
        q = getattr(self, "deferred", [])
        k = len(q) if n is None else min(n, len(q))
        for _ in range(k):
            q.pop(0)()

    def bg_flush(self, n=None):
        q = getattr(self, "bgq", [])
        k = len(q) if n is None else min(n, len(q))
        for _ in range(k):
            q.pop(0)()

    def nxt(self, name, n):
        k = self.rot.get(name, 0)
        self.rot[name] = k + 1
        return k % n

    def emit(self):
        nc = self.nc
        ops = self.ops
        for o in ops:
            keep = set()
            for d in o.deps:
                po = ops[d]
                if po.eng == "pe" and o.eng == "pe" and not po.dma and not o.dma:
                    continue
                keep.add(d)
            o.deps = keep
            for d in keep:
                ops[d].signal = True
        ops[-1].signal = True
        for o in ops:
            if o.dma:
                o.signal = True
        sems = {}
        for e in ENGS:
            n = sum(1 for o in ops if o.eng == e and not o.dma and o.signal)
            sems[e] = [nc.alloc_semaphore(name=f"s_{e}_{i}") for i in range(max(1, -(-n // EPOCH)))]
        dsems = {e: [nc.alloc_semaphore(name=f"d_{e}_{i}") for i in range(NDMA[e])] for e in NDMA}
        cnt = {e: 0 for e in ENGS}
        dcnt = {e: 0 for e in NDMA}
        for o in ops:
            if not o.signal:
                continue
            if o.dma:
                k = dcnt[o.eng]
                dcnt[o.eng] += 1
                o.sem = dsems[o.eng][k % NDMA[o.eng]]
                o.val = 16 * (k // NDMA[o.eng] + 1)
            else:
                k = cnt[o.eng]
                cnt[o.eng] += 1
                o.sem = sems[o.eng][k // EPOCH]
                o.val = (k % EPOCH) + 1
        per_eng = {e: [o for o in ops if o.eng == e] for e in ENGS}
        final = ops[-1]

        def run_engine(e, eng):
            waited = {}
            for o in per_eng[e]:
                need = {}
                for d in o.deps:
                    po = ops[d]
                    key = id(po.sem)
                    if need.get(key, (None, 0))[1] < po.val:
                        need[key] = (po.sem, po.val)
                for key, (sem, val) in need.items():
                    if waited.get(key, 0) >= val:
                        continue
                    eng.wait_ge(sem, val)
                    waited[key] = val
                ins = o.fn(eng)
                if o.signal:
                    ins.then_inc(o.sem, 16 if o.dma else 1)
            if e == final.eng:
                eng.wait_ge(final.sem, final.val)

        with nc.Block() as block:
            @block.tensor
            def _(eng):
                run_engine("pe", eng)

            @block.scalar
            def _(eng):
                run_engine("act", eng)

            @block.vector
            def _(eng):
                run_engine("dve", eng)

            @block.gpsimd
            def _(eng):
                run_engine("pool", eng)

            @block.sync
            def _(eng):
                run_engine("sp", eng)


class Ctx:
    def __init__(self, nc):
        self.nc = nc
        self.P = Prog(nc)
        self.ps = [nc.alloc_psum_tensor(f"ps{i}", [128, 512], F32) for i in range(8)]
        self.ones = nc.alloc_sbuf_tensor("ones", [128, 128], BF16)
        self.eps_ln = nc.alloc_sbuf_tensor("eps_ln", [128, 1], F32)
        self.eps_rms = nc.alloc_sbuf_tensor("eps_rms", [128, 1], F32)
        self.P.ms("pool", self.ones[:], 1.0, ["ones"])
        self.P.ms("pool", self.eps_ln[:], LN_EPS, ["eps"])
        self.P.ms("pool", self.eps_rms[:], RMS_EPS, ["eps"])
        self.lnb = nc.alloc_sbuf_tensor("lnb", [128, 4, 512], BF16)
        self.ln_m = nc.alloc_sbuf_tensor("ln_m", [128, 512], F32)
        self.ln_t = nc.alloc_sbuf_tensor("ln_t", [128, 512], F32)
        self.ln_r = nc.alloc_sbuf_tensor("ln_r", [128, 512], F32)
        self.ln_x = nc.alloc_sbuf_tensor("ln_x", [128, 2, 512], F32)
        self.cst = nc.alloc_sbuf_tensor("cst", [128, 2, 512], F32)
        self.cstb = nc.alloc_sbuf_tensor("cstb", [128, 2, 512], BF16)

    def psn(self, lo=0, hi=8):
        return lo + self.P.nxt(("ps", lo, hi), hi - lo)

    def load(self, name, shape, src, dtype=F32, eng="sp"):
        t = self.nc.alloc_sbuf_tensor(name, shape, dtype)
        self.P.dma(eng, t[:], src, [], [name])
        return t

    def cast_weights(self, dst2d, src2d, rows, cols):
        P = self.P
        for r0 in range(0, rows, 128):
            for c0 in range(0, cols, 512):
                w = min(512, cols - c0)
                k = P.nxt("cst", 2)
                P.dma("sp", self.cst[:, k, :w], src2d[r0:r0 + 128, c0:c0 + w], [], [("cst", k)])
                P.cp("pool", self.cstb[:, k, :w], self.cst[:, k, :w], [("cst", k)], [("cstb", k)])
                P.dma("pool", dst2d[r0:r0 + 128, c0:c0 + w], self.cstb[:, k, :w], [("cstb", k)], [("wdram", id(dst2d))])

    def ln_stream(self, xview, xkey, t0, nch, G, nfeat, eps_tile, scale_fn, bias_fn, dst, dkeys, func=AF.Identity, banks=None):
        P = self.P
        if banks is None:
            s1 = self.psn(0, 4)
            s2 = self.psn(0, 4)
        else:
            s1, s2 = banks
        ps1 = self.ps[s1][:, :G]
        ps2 = self.ps[s2][:, :G]
        for c in range(nch):
            kx = P.nxt("ln_x", 2)
            x = self.ln_x[:, kx, :G]
            P.dma("pool", x, xview[:, c, t0:t0 + G], xkey, [("ln_x", kx)])
            k = P.nxt("lnb", 2)
            P.cp("dve", self.lnb[:, 2 * k, :G], x, [("ln_x", kx)], [("lnb", 2 * k)])
            P.act(self.lnb[:, 2 * k + 1, :G], x, AF.Square, [("ln_x", kx)], [("lnb", 2 * k + 1)])
            P.mm(ps1, self.ones[:], self.lnb[:, 2 * k, :G], c == 0, c == nch - 1, ["ones", ("lnb", 2 * k)], [("ps", s1)])
            P.mm(ps2, self.ones[:], self.lnb[:, 2 * k + 1, :G], c == 0, c == nch - 1, ["ones", ("lnb", 2 * k + 1)], [("ps", s2)])
        m = self.ln_m[:, :G]
        t = self.ln_t[:, :G]
        r = self.ln_r[:, :G]
        P.act(m, ps1, AF.Copy, [("ps", s1)], ["ln_m"], scale=1.0 / nfeat)
        P.tt("dve", t, m, m, ALU.mult, ["ln_m"], ["ln_t"])
        P.stt("dve", t, ps2, 1.0 / nfeat, t, ALU.mult, ALU.subtract, [("ps", s2), "ln_t"], ["ln_t"])
        P.act(t, t, AF.Sqrt, ["ln_t", "eps"], ["ln_t"], bias=eps_tile[:], scale=1.0)
        P.rcp(r, t, ["ln_t"], ["ln_r"])
        for c in range(nch):
            kx = P.nxt("ln_x", 2)
            x = self.ln_x[:, kx, :G]
            P.dma("pool", x, xview[:, c, t0:t0 + G], xkey, [("ln_x", kx)])
            P.tt("dve", x, x, m, ALU.subtract, [("ln_x", kx), "ln_m"], [("ln_x", kx)])
            P.tt("dve", x, x, r, ALU.mult, [("ln_x", kx), "ln_r"], [("ln_x", kx)])
            P.act(dst(c), x, func, [("ln_x", kx), "consts"], [dkeys(c)], bias=bias_fn(c), scale=scale_fn(c))

    def cast_weights_bg(self, dst2d, src2d, rows, cols):
        P = self.P
        if not hasattr(P, "bgq"):
            P.bgq = []
        steps = [(r0, c0, min(512, cols - c0)) for r0 in range(0, rows, 128) for c0 in range(0, cols, 512)]
        key = ("wdram", id(dst2d))

        def t_in(i):
            r0, c0, w = steps[i]
            k = i % 2
            P.dma("pool", self.cst[:, k, :w], src2d[r0:r0 + 128, c0:c0 + w], [], [("cst", k)])

        def t_unit(i):
            r0, c0, w = steps[i]
            k = i % 2
            P.cp("pool", self.cstb[:, k, :w], self.cst[:, k, :w], [("cst", k)], [("cstb", k)])
            P.dma("pool", dst2d[r0:r0 + 128, c0:c0 + w], self.cstb[:, k, :w], [("cstb", k)], [key])
            if i + 2 < len(steps):
                t_in(i + 2)

        P.bgq.append(lambda: t_in(0))
        if len(steps) > 1:
            P.bgq.append(lambda: t_in(1))
        for i in range(len(steps)):
            P.bgq.append(lambda i=i: t_unit(i))

    def ln_fm(self, src, skeys, nch, G, nfeat, eps_tile, scale_fn, bias_fn, dst, dkeys, func=AF.Identity):
        P = self.P
        s1 = self.psn(0, 4)
        s2 = self.psn(0, 4)
        ps1 = self.ps[s1][:, :G]
        ps2 = self.ps[s2][:, :G]
        for c in range(nch):
            k = P.nxt("lnb", 2)
            P.cp("dve", self.lnb[:, 2 * k, :G], src(c), [skeys(c)], [("lnb", 2 * k)])
            P.act(self.lnb[:, 2 * k + 1, :G], src(c), AF.Square, [skeys(c)], [("lnb", 2 * k + 1)])
            P.mm(ps1, self.ones[:], self.lnb[:, 2 * k, :G], c == 0, c == nch - 1, ["ones", ("lnb", 2 * k)], [("ps", s1)])
            P.mm(ps2, self.ones[:], self.lnb[:, 2 * k + 1, :G], c == 0, c == nch - 1, ["ones", ("lnb", 2 * k + 1)], [("ps", s2)])
        m = self.ln_m[:, :G]
        t = self.ln_t[:, :G]
        r = self.ln_r[:, :G]
        P.act(m, ps1, AF.Copy, [("ps", s1)], ["ln_m"], scale=1.0 / nfeat)
        P.tt("dve", t, m, m, ALU.mult, ["ln_m"], ["ln_t"])
        P.stt("dve", t, ps2, 1.0 / nfeat, t, ALU.mult, ALU.subtract, [("ps", s2), "ln_t"], ["ln_t"])
        P.act(t, t, AF.Sqrt, ["ln_t", "eps"], ["ln_t"], bias=eps_tile[:], scale=1.0)
        P.rcp(r, t, ["ln_t"], ["ln_r"])
        for c in range(nch):
            k = P.nxt("ln_x", 2)
            x = self.ln_x[:, k, :G]
            P.tt("dve", x, src(c), m, ALU.subtract, [skeys(c), "ln_m"], [("ln_x", k)])
            P.tt("dve", x, x, r, ALU.mult, [("ln_x", k), "ln_r"], [("ln_x", k)])
            P.act(dst(c), x, func, [("ln_x", k), "consts"], [dkeys(c)], bias=bias_fn(c), scale=scale_fn(c))


def build_mod():
    nc = bass.Bass("TRN2", target_bir_lowering=False)
    cT = nc.dram_tensor("cT", [128, 32], F32, kind="ExternalInput").ap()
    wm = nc.dram_tensor("wm", [24, 128, 2048], F32, kind="ExternalInput").ap()
    bm = nc.dram_tensor("bm", [128, 24], F32, kind="ExternalInput").ap()
    mo = nc.dram_tensor("mo", [128, 48], F32, kind="ExternalOutput").ap()
    P = Prog(nc)
    ca = nc.alloc_sbuf_tensor("ca", [128, 32], F32)
    bmt = nc.alloc_sbuf_tensor("bmt", [128, 24], F32)
    res = nc.alloc_sbuf_tensor("res", [128, 48], F32)
    wb = nc.alloc_sbuf_tensor("wb", [128, 2, 2048], F32)
    ps = nc.alloc_psum_tensor("ps", [128, 48], F32)
    P.dma("sp", ca[:], cT, [], ["ca"])
    P.dma("sp", bmt[:], bm, [], ["bmt"])
    P.act(ca[:], ca[:], AF.Silu, ["ca"], ["ca"])
    for j in range(24):
        k = j % 2
        P.dma("sp", wb[:, k, :], wm[j], [], [("wb", k)])
        for kc in range(16):
            P.mm(ps[:, 2 * j:2 * j + 2], wb[:, k, kc * 128:(kc + 1) * 128], ca[:, 2 * kc:2 * kc + 2],
                 kc == 0, kc == 15, [("wb", k), "ca"], ["ps"])
    for b in range(2):
        P.tt("dve", res[:, b::2], ps[:, b::2], bmt[:], ALU.add, ["ps", "bmt"], ["res"])
    P.dma("sp", mo, res[:], ["res"], ["mo"])
    P.dma("sp", mo[0:1, 0:1], res[0:1, 0:1], ["mo"], ["mo"])
    P.emit()
    return nc


def build_ffn(T):
    nc = bass.Bass("TRN2", target_bir_lowering=False)
    dt = nc.dram_tensor
    xm = dt("xm", [D, HALO + T], F32, kind="ExternalInput").ap()
    mod = dt("mod", [128, 96], F32, kind="ExternalInput").ap()
    wup = dt("wup", [88 * 128, 2048], F32, kind="ExternalInput").ap()
    wdn = dt("wdn", [16 * 128, 5632], F32, kind="ExternalInput").ap()
    wc = dt("wc", [128, 3 * NFF], F32, kind="ExternalInput").ap()
    bc = dt("bc", [128, NFF], F32, kind="ExternalInput").ap()
    pg = dt("pg", [128, 16], F32, kind="ExternalInput").ap()
    pb = dt("pb", [128, 16], F32, kind="ExternalInput").ap()
    flag = dt("flag", [128, 1], F32, kind="ExternalInput").ap()
    xo = dt("xo", [D, T], F32, kind="ExternalOutput").ap()
    wupb = dt("wupb", [88 * 128, 2048], BF16).ap()
    wdnb = dt("wdnb", [16 * 128, 5632], BF16).ap()

    C = Ctx(nc)
    P = C.P
    modt = C.load("modt", [128, 96], mod)
    wct = C.load("wct", [128, 3 * NFF], wc)
    bct = C.load("bct", [128, NFF], bc)
    pgt = C.load("pgt", [128, 16], pg)
    pbt = C.load("pbt", [128, 16], pb)
    flt = C.load("flt", [128, 1], flag)
    mod1 = nc.alloc_sbuf_tensor("mod1", [128, 96], F32)
    P.ts("dve", mod1[:], modt[:], 1.0, None, ALU.add, None, ["modt"], ["consts"])
    C.cast_weights(wupb, wup, 88 * 128, 2048)
    C.cast_weights(wdnb, wdn, 16 * 128, 5632)

    big = nc.alloc_sbuf_tensor("big", [128, NCH, 512], F32)
    hT = nc.alloc_sbuf_tensor("hT", [128, NCH, 512], BF16)
    ffT = nc.alloc_sbuf_tensor("ffT", [128, NFF, 512], BF16)
    wbuf = nc.alloc_sbuf_tensor("wbuf", [128, 4, 2048], BF16)
    wdbuf = nc.alloc_sbuf_tensor("wdbuf", [128, 2, 5632], BF16)
    gbuf = nc.alloc_sbuf_tensor("gbuf", [128, 2, 516], F32)
    acc = nc.alloc_sbuf_tensor("acc", [128, 2, 512], F32)
    carry = nc.alloc_sbuf_tensor("carry", [128, NFF, 2], F32)
    xres = nc.alloc_sbuf_tensor("xres", [128, 2, 512], F32)
    obuf = nc.alloc_sbuf_tensor("obuf", [128, 2, 512], F32)
    xm_v = xm.rearrange("(c p) t -> p c t", p=128)
    xo_v = xo.rearrange("(c p) t -> p c t", p=128)

    def group(t0, G, halo, o0):
        P.dma("sp", big[:, :, :G], xm_v[:, :, t0:t0 + G], [], ["big"])
        C.ln_fm(lambda c: big[:, c, :G], lambda c: "big", NCH, G, D, C.eps_ln,
                lambda c: mod1[:, 64 + c:65 + c], lambda c: modt[:, 48 + c:49 + c],
                lambda c: hT[:, c, :G], lambda c: "hT")
        for j in range(NFF):
            blocks = [j] if halo else [j, NFF + j]
            pss = []
            for blk in blocks:
                k = P.nxt("wbuf", 4)
                P.dma("sp", wbuf[:, k, :], wupb[blk * 128:(blk + 1) * 128, :], [("wdram", id(wupb))], [("wbuf", k)])
                s = C.psn(4, 8)
                for c in range(NCH):
                    P.mm(C.ps[s][:, :G], wbuf[:, k, c * 128:(c + 1) * 128], hT[:, c, :G], c == 0, c == NCH - 1,
                         [("wbuf", k), "hT"], [("ps", s)])
                pss.append(s)
            sa = pss[0]
            if halo:
                P.ts("dve", carry[:, j, :], C.ps[sa][:, G - 2:G], flt[:, 0:1], None, ALU.mult, None,
                     [("ps", sa), "flt"], [("carry", j)])
                continue
            sb_ = pss[1]
            k = P.nxt("gbuf", 2)
            gb = gbuf[:, k, :]
            P.cp("pool", gb[:, 0:2], carry[:, j, :], [("carry", j)], [("gbuf", k)])
            P.act(gb[:, 2:2 + G], C.ps[sa][:, :G], AF.Copy, [("ps", sa)], [("gbuf", k)])
            a = acc[:, k, :G]
            P.ts("dve", a, gb[:, 2:2 + G], wct[:, 2 * NFF + j:2 * NFF + j + 1], bct[:, j:j + 1], ALU.mult, ALU.add,
                 [("gbuf", k), "wct", "bct"], [("acc", k)])
            P.stt("dve", a, gb[:, 1:1 + G], wct[:, NFF + j:NFF + j + 1], a, ALU.mult, ALU.add, [("gbuf", k), ("acc", k)], [("acc", k)])
            P.stt("dve", a, gb[:, 0:G], wct[:, j:j + 1], a, ALU.mult, ALU.add, [("gbuf", k), ("acc", k)], [("acc", k)])
            P.cp("pool", carry[:, j, :], gb[:, G:G + 2], [("gbuf", k)], [("carry", j)])
            P.act(a, a, AF.Silu, [("acc", k)], [("acc", k)])
            P.tt("dve", ffT[:, j, :G], a, C.ps[sb_][:, :G], ALU.mult, [("acc", k), ("ps", sb_)], [("ffT", j)])
        if halo:
            return
        for ob in range(NCH):
            k = P.nxt("wdbuf", 2)
            P.dma("sp", wdbuf[:, k, :], wdnb[ob * 128:(ob + 1) * 128, :], [("wdram", id(wdnb))], [("wdbuf", k)])
            kx = P.nxt("xres", 2)
            P.dma("pool", xres[:, kx, :G], xm_v[:, ob, t0:t0 + G], [], [("xres", kx)])
            s = C.psn(4, 8)
            for c in range(NFF):
                P.mm(C.ps[s][:, :G], wdbuf[:, k, c * 128:(c + 1) * 128], ffT[:, c, :G], c == 0, c == NFF - 1,
                     [("wdbuf", k), ("ffT", c)], [("ps", s)])
            P.act(big[:, ob, :G], C.ps[s][:, :G], AF.Identity, [("ps", s), "consts"], ["big"], scale=mod1[:, 80 + ob:81 + ob], bias=0.0)
            P.stt("dve", big[:, ob, :G], xres[:, kx, :G], ALPHA, big[:, ob, :G], ALU.mult, ALU.add, [("xres", kx), "big"], ["big"])

        def dst(c):
            return obuf[:, c % 2, :G]

        C.ln_fm(lambda c: big[:, c, :G], lambda c: "big", NCH, G, D, C.eps_ln,
                lambda c: pgt[:, c:c + 1], lambda c: pbt[:, c:c + 1],
                dst, lambda c: ("obuf", c % 2))

    orig_act = P.act

    state = {"store": None}

    def act_hook(out, in_, func, reads, writes, bias=None, scale=None):
        orig_act(out, in_, func, reads, writes, bias=bias, scale=scale)
        st = state["store"]
        if st is not None and writes and isinstance(writes[0], tuple) and writes[0][0] == "obuf":
            st(out, writes[0])

    P.act = act_hook

    def make_store(o0, G):
        cnt = {"c": 0}

        def st(out_ap, key):
            c = cnt["c"]
            cnt["c"] += 1
            P.dma("pool", xo_v[:, c, o0:o0 + G], out_ap, [key], [("xo", c, o0)])
        return st

    group(0, HALO, True, 0)
    for g in range(T // 512):
        state["store"] = make_store(g * 512, 512)
        group(HALO + g * 512, 512, False, g * 512)
        state["store"] = None
    outs = [("xo", c, g * 512) for c in range(NCH) for g in range(T // 512)]
    P.dma("sp", wupb[0:1, 0:8], wupb[0:1, 8:16], outs, ["fin"])
    P.emit()
    return nc


def pl(v, n):
    return np.ascontiguousarray(np.asarray(v, np.float32).reshape(n, 128).T)


def mod_layout(modv):
    return np.ascontiguousarray(np.concatenate([pl(modv[j], 16) for j in range(6)], axis=1))


def tile_w(w, nk, cols_list):
    out = []
    for cols in cols_list:
        blk = w[:, cols].reshape(nk, 128, len(cols)).transpose(1, 0, 2).reshape(128, nk * len(cols))
        out.append(blk)
    return np.ascontiguousarray(np.concatenate(out, axis=0), dtype=np.float32)


def ffn_inputs(xmT_ext, modv, wup, wdn, wconv, bconv, pg, pb, flag):
    return {
        "xm": np.ascontiguousarray(xmT_ext, dtype=np.float32),
        "mod": mod_layout(modv),
        "wup": tile_w(wup, 16, [np.arange(b * 128, (b + 1) * 128) for b in range(88)]),
        "wdn": tile_w(wdn, 44, [np.arange(b * 128, (b + 1) * 128) for b in range(16)]),
        "wc": np.ascontiguousarray(np.concatenate([pl(wconv[k], 44) for k in range(3)], axis=1)),
        "bc": pl(bconv, 44),
        "pg": pl(pg, 16),
        "pb": pl(pb, 16),
        "flag": np.full((128, 1), flag, np.float32),
    }


def build_mix(T, kv_only):
    nc = bass.Bass("TRN2", target_bir_lowering=False)
    NT = T // 128
    NPREV = 3 * NT

    def din(name, shape, dtype=F32):
        return nc.dram_tensor(name, shape, dtype, kind="ExternalInput").ap()

    x = din("x", [D, HALO + T])
    mod = din("mod", [128, 96])
    winF = din("winF", [29 * 128, 2048])
    winkr = din("winkr", [2 * 128, 1024])
    wukvk = din("wukvk", [4 * 128, 256])
    wukvv = din("wukvv", [128, 1024])
    gkv = din("gkv", [128, 2])
    CCd = din("CC", [64, T])
    SSd = din("SS", [64, T])
    okind = "ExternalOutput" if kv_only else "Internal"
    KNo = nc.dram_tensor("KNo", [NT * 128, 512], BF16, kind=okind).ap()
    KRo = nc.dram_tensor("KRo", [NT * 64, 128], BF16, kind=okind).ap()
    Vo = nc.dram_tensor("Vo", [NT * 128, 512], BF16, kind=okind).ap()
    winFb = nc.dram_tensor("winFb", [29 * 128, 2048], BF16).ap()
    winkrb = nc.dram_tensor("winkrb", [2 * 128, 1024], BF16).ap()
    wukvkb = nc.dram_tensor("wukvkb", [4 * 128, 256], BF16).ap()
    wukvvb = nc.dram_tensor("wukvvb", [128, 1024], BF16).ap()
    if not kv_only:
        winv = din("winv", [128, 8192])
        wuqn = din("wuqn", [4 * 128, 384])
        wuqr = din("wuqr", [8 * 128, 192])
        gq = din("gq", [128, 3])
        wsT = din("wsT", [128, 512])
        bsrow = din("bsrow", [1, 512])
        lng = din("lng", [128, 4])
        lnbrow = din("lnbrow", [1, 512])
        scw = din("scw", [128, 12])
        cfw = din("cfw", [128, 124])
        cfb = din("cfb", [128, 4])
        cfg = din("cfg", [128, 4])
        cfbt = din("cfbt", [128, 4])
        wout = din("wout", [16 * 128, 2048])
        pmg = din("pmg", [128, 16])
        pmb = din("pmb", [128, 16])
        flag = din("flag", [128, 1])
        sbias = din("sbias", [128, 3])
        KNp = din("KNp", [NPREV * 128, 512], BF16)
        KRp = din("KRp", [NPREV * 64, 128], BF16)
        Vp = din("Vp", [NPREV * 128, 512], BF16)
        xmid = nc.dram_tensor("xmid", [D, T], F32, kind="ExternalOutput").ap()
        winvb = nc.dram_tensor("winvb", [128, 8192], BF16).ap()
        wuqnb = nc.dram_tensor("wuqnb", [4 * 128, 384], BF16).ap()
        wuqrb = nc.dram_tensor("wuqrb", [8 * 128, 192], BF16).ap()
        woutb = nc.dram_tensor("woutb", [16 * 128, 2048], BF16).ap()

    C = Ctx(nc)
    P = C.P
    al = nc.alloc_sbuf_tensor
    modt = C.load("modt", [128, 96], mod)
    gkvt = C.load("gkvt", [128, 2], gkv)
    mod1 = al("mod1", [128, 96], F32)
    P.ts("dve", mod1[:], modt[:], 1.0, None, ALU.add, None, ["modt"], ["consts"])
    C.cast_weights(winFb, winF, 29 * 128, 2048)
    C.cast_weights(winkrb, winkr, 2 * 128, 1024)
    C.cast_weights(wukvkb, wukvk, 4 * 128, 256)
    C.cast_weights(wukvvb, wukvv, 128, 1024)
    wvv = al("wvv", [128, 2, 512], BF16)
    P.dma("sp", wvv[:], wukvvb.rearrange("p (c n) -> p c n", c=2), [("wdram", id(wukvvb))], ["wvv"])

    big = al("big", [128, NCH, 512], F32)
    hT = al("hT", [128, NCH, 512], BF16)
    wbuf = al("wbuf", [128, 3, 2048], BF16)
    sqkv = al("sqkv", [128, 2, 512], BF16)
    kvcg = al("kvcg", [128, 2, 512], BF16)
    rkv = al("rkv", [128, 512], F32)
    kst = al("kst", [128, 4, 512], BF16)
    krs_full = al("krs", [128, 512], BF16)
    r1_full = al("r1", [128, 512], F32)
    r2_full = al("r2", [128, 512], F32)
    cct_full = al("cct", [128, 512], F32)
    sst_full = al("sst", [128, 512], F32)
    krs, r1, r2, cct, sst = (t_[0:64] for t_ in (krs_full, r1_full, r2_full, cct_full, sst_full))
    rt = al("rt", [128, 2, 1], F32)
    vst = al("vst", [128, 2, 512], BF16)
    x_v = x.rearrange("(c p) t -> p c t", p=128)

    if not kv_only:
        consts = {}
        for nm, ap_, shp in (("gqt", gq, [128, 3]), ("lngt", lng, [128, 4]), ("scwt", scw, [128, 12]),
                             ("cfwt", cfw, [128, 124]), ("cfbt_", cfb, [128, 4]), ("cfgt", cfg, [128, 4]),
                             ("cfbtt", cfbt, [128, 4]), ("pmgt", pmg, [128, 16]), ("pmbt", pmb, [128, 16]),
                             ("flt", flag, [128, 1]), ("sbt", sbias, [128, 3]), ("wsTf", wsT, [128, 512]),
                             ("bsr", bsrow, [1, 512]), ("lnbr", lnbrow, [1, 512])):
            consts[nm] = C.load(nm, shp, ap_)
        gqt, lngt, scwt, cfwt, cfbt_, cfgt, cfbtt = (consts[k] for k in ("gqt", "lngt", "scwt", "cfwt", "cfbt_", "cfgt", "cfbtt"))
        pmgt, pmbt, flt, sbt, wsTf, bsr, lnbr = (consts[k] for k in ("pmgt", "pmbt", "flt", "sbt", "wsTf", "bsr", "lnbr"))
        C.cast_weights(winvb, winv, 128, 8192)
        C.cast_weights(wuqnb, wuqn, 4 * 128, 384)
        C.cast_weights(wuqrb, wuqr, 8 * 128, 192)
        C.cast_weights(woutb, wout, 16 * 128, 2048)
        P.ms("pool", wsTf[64:128, :].rearrange("p (h i) -> p h i", h=4)[:, :, 0:64], 0.0, ["wsTf"])
        wsTb = al("wsTb", [128, 512], BF16)
        P.cp("dve", wsTb[:], wsTf[:], ["wsTf"], ["wsTb"])
        onesf = al("onesf", [128, 128], F32)
        P.ms("pool", onesf[:], 1.0, ["onesf"])
        rsw = al("rsw", [1, 512], F32)
        s = C.psn(4, 8)
        P.mm(C.ps[s][0:1, :], onesf[:, 0:1], wsTf[:], True, True, ["onesf", "wsTf"], [("ps", s)])
        P.cp("dve", rsw[:], C.ps[s][0:1, :], [("ps", s)], ["rsw"])
        Bm = al("Bm", [128, 512], F32)
        s = C.psn(4, 8)
        for h in range(4):
            sl = slice(h * 128, (h + 1) * 128)
            P.mm(C.ps[s][:, sl], lnbr[0:1, sl], rsw[0:1, sl], True, False, ["lnbr", "rsw"], [("ps", s)])
            P.mm(C.ps[s][:, sl], onesf[0:1, :], bsr[0:1, sl], False, True, ["onesf", "bsr"], [("ps", s)])
        P.cp("dve", Bm[:], C.ps[s][:], [("ps", s)], ["Bm"])

        yT = al("yT", [128, NCH, 512], BF16)
        uT = al("uT", [128, 4, 512], BF16)
        wvb = al("wvb", [128, 2, 512], BF16)
        vf = al("vf", [128, 512], F32)
        vn = al("vn", [128, 512], BF16)
        bst = al("bst", [128, 8], F32)
        gt = al("gt", [128, 512], F32)
        sqq = al("sqq", [128, 2, 512], BF16)
        qcg = al("qcg", [128, 3, 512], BF16)
        rq = al("rq", [128, 512], F32)
        QnT = al("QnT", [128, 4, 512], BF16)
        QrT = al("QrT", [128, 4, 512], BF16)[0:64]
        sbT = al("sbT", [128, 4, 512], BF16)
        scT = al("scT", [128, 512], F32)
        zbuf = al("zbuf", [128, 516], F32)
        ybuf = al("ybuf", [128, 544], F32)
        cz = al("cz", [128, 4, 2], F32)
        cy = al("cy", [128, 4, 30], F32)
        cacc = al("cacc", [128, 512], F32)
        cT = al("cT", [128, 4, 512], F32)
        knb = al("knb", [128, 2, 4, 128], BF16)
        vpb = al("vpb", [128, 2, 4, 128], BF16)
        krb = al("krb", [128, 2, 4, 128], BF16)[0:64]
        pT = al("pT", [128, 3, 512], BF16)
        rl = al("rl", [128, 512], F32)
        xres = al("xres", [128, 2, 512], F32)
        obuf = al("obuf", [128, 2, 512], F32)
        xmid_v = xmid.rearrange("(c p) t -> p c t", p=128)

    def wload(dram2d, blk, width):
        k = P.nxt("wbuf", 3)
        P.dma("sp", wbuf[:, k, :width], dram2d[blk * 128:(blk + 1) * 128, :], [("wdram", id(dram2d))], [("wbuf", k)])
        return k

    def proj(dram2d, blk, M, nk, rhs, rkeys, G):
        k = wload(dram2d, blk, nk * M)
        s = C.psn(4, 8)
        for c in range(nk):
            P.mm(C.ps[s][:M, :G], wbuf[:, k, c * M:(c + 1) * M], rhs(c), c == 0, c == nk - 1,
                 [("wbuf", k)] + rkeys, [("ps", s)])
        return s

    def group(t0, G, halo):
        o0 = t0 - HALO
        P.dma("sp", big[:, :, :G], x_v[:, :, t0:t0 + G], [], ["big"])
        C.ln_fm(lambda c: big[:, c, :G], lambda c: "big", NCH, G, D, C.eps_ln,
                lambda c: mod1[:, 16 + c:17 + c], lambda c: modt[:, c:c + 1],
                lambda c: hT[:, c, :G], lambda c: "hT")
        hrhs = lambda c: hT[:, c, :G]
        import os
        if int(os.environ.get("KSTOP", "9")) <= 0:
            return
        if not halo:
            P.dma("sp", cct[:, :G], CCd[:, o0:o0 + G], [], ["cct"])
            P.dma("sp", sst[:, :G], SSd[:, o0:o0 + G], [], ["sst"])
            KS2 = int(os.environ.get("KS2", "9"))
            if KS2 <= 1:
                return
            for j in range(2):
                s = proj(winFb, 7 + j, 128, NCH, hrhs, ["hT"], G)
                if KS2 <= 2:
                    continue
                P.act(sqkv[:, j, :G], C.ps[s][:, :G], AF.Square, [("ps", s)], [("sqkv", j)])
                P.ts("dve", kvcg[:, j, :G], C.ps[s][:, :G], gkvt[:, j:j + 1], None, ALU.mult, None,
                     [("ps", s), "gkvt"], [("kvcg", j)])
            if KS2 <= 3:
                return
            s = C.psn(4, 8)
            for j in range(2):
                P.mm(C.ps[s][:, :G], C.ones[:], sqkv[:, j, :G], j == 0, j == 1, ["ones", ("sqkv", j)], [("ps", s)])
            P.act(rkv[:, :G], C.ps[s][:, :G], AF.Sqrt, [("ps", s), "eps"], ["rkv"], bias=C.eps_rms[:], scale=1.0 / 256)
            P.rcp(rkv[:, :G], rkv[:, :G], ["rkv"], ["rkv"])
            import os
            KSTOP = int(os.environ.get("KSTOP", "9"))
            if KSTOP <= 1:
                return
            for h in range(4):
                s = proj(wukvkb, h, 128, 2, lambda c: kvcg[:, c, :G], [("kvcg", 0), ("kvcg", 1)], G)
                P.tt("dve", kst[:, h, :G], C.ps[s][:, :G], rkv[:, :G], ALU.mult, [("ps", s), "rkv"], ["kst"])
            if KSTOP <= 2:
                return
            sa = proj(winkrb, 0, 64, NCH, hrhs, ["hT"], G)
            sb_ = proj(winkrb, 1, 64, NCH, hrhs, ["hT"], G)
            P.tt("dve", r1[:, :G], C.ps[sa][:64, :G], cct[:, :G], ALU.mult, [("ps", sa), "cct"], ["r1"])
            P.tt("dve", r2[:, :G], C.ps[sb_][:64, :G], sst[:, :G], ALU.mult, [("ps", sb_), "sst"], ["r2"])
            P.tt("pool", krs[:, :G], r1[:, :G], r2[:, :G], ALU.add, ["r1", "r2"], ["krs"])
            if KSTOP <= 3:
                return
            for sub in range(G // 128):
                kt = o0 // 128 + sub
                cs = slice(sub * 128, (sub + 1) * 128)
                P.dma("pool", KNo[kt * 128:(kt + 1) * 128, :].rearrange("p (h k) -> p h k", h=4), kst[:, :, cs],
                      ["kst"], [("KNo", kt)])
                P.dma("pool", KRo[kt * 64:(kt + 1) * 64, :], krs[:, cs], ["krs"], [("KRo", kt)])
                if KSTOP <= 4:
                    continue
                s = C.psn(4, 8)
                for j in range(2):
                    P.mm(C.ps[s][:, 0:1], sqkv[:, j, cs], C.ones[:, 0:1], j == 0, j == 1, [("sqkv", j), "ones"], [("ps", s)])
                kk = P.nxt("rt", 2)
                P.act(rt[:, kk, :], C.ps[s][:, 0:1], AF.Sqrt, [("ps", s), "eps"], [("rt", kk)], bias=C.eps_rms[:], scale=1.0 / 256)
                P.rcp(rt[:, kk, :], rt[:, kk, :], [("rt", kk)], [("rt", kk)])
                s = C.psn(4, 8)
                for c in range(2):
                    P.mm(C.ps[s][:, :], kvcg[:, c, cs], wvv[:, c, :], c == 0, c == 1, [("kvcg", c), "wvv"], [("ps", s)])
                P.act(vst[:, kk, :], C.ps[s][:, :], AF.Copy, [("ps", s), ("rt", kk)], [("vst", kk)], scale=rt[:, kk, :])
                P.dma("pool", Vo[kt * 128:(kt + 1) * 128, :], vst[:, kk, :], [("vst", kk)], [("Vo", kt)])
            if kv_only:
                return
            for j in range(4):
                s = proj(winFb, j, 128, NCH, hrhs, ["hT"], G)
                P.act(uT[:, j, :G], C.ps[s][:, :G], AF.Gelu_apprx_tanh, [("ps", s)], [("uT", j)])
            nsub = G // 128
            for c in range(NCH):
                k = P.nxt("wvb", 2)
                P.dma("sp", wvb[:, k, :], winvb[:, c * 512:(c + 1) * 512], [("wdram", id(winvb))], [("wvb", k)])
                for sub in range(nsub):
                    P.mm(C.ps[sub][:, :], hT[:, c, sub * 128:(sub + 1) * 128], wvb[:, k, :], c == 0, c == NCH - 1,
                         [("wvb", k), "hT"], [("ps", sub)])
            for sub in range(nsub):
                cs = slice(sub * 128, (sub + 1) * 128)
                P.act(vf[:], C.ps[sub][:, :], AF.Gelu_apprx_tanh, [("ps", sub)], ["vf"])
                P.op("dve", lambda e: e.bn_stats(out=bst[:, 0:6], in_=vf[:]), ["vf"], ["bst"])
                P.op("dve", lambda e: e.bn_aggr(out=bst[:, 6:8], in_=bst[:, 0:6]), ["bst"], ["bst"])
                P.act(bst[:, 7:8], bst[:, 7:8], AF.Sqrt, ["bst", "eps"], ["bst"], bias=C.eps_ln[:], scale=1.0)
                P.rcp(bst[:, 7:8], bst[:, 7:8], ["bst"], ["bst"])
                P.ts("dve", vf[:], vf[:], bst[:, 6:7], bst[:, 7:8], ALU.subtract, ALU.mult, ["vf", "bst"], ["vf"])
                P.cp("pool", vn[:], vf[:], ["vf"], ["vn"])
                s = C.psn(4, 8)
                for h in range(4):
                    sl = slice(h * 128, (h + 1) * 128)
                    P.mm(C.ps[s][:, sl], vn[:, sl], wsTb[:, sl], True, True, ["vn", "wsTb"], [("ps", s)])
                for h in range(4):
                    sl = slice(h * 128, (h + 1) * 128)
                    P.stt("dve", gt[:, sl], C.ps[s][:, sl], lngt[:, h:h + 1], Bm[:, sl], ALU.mult, ALU.add,
                          [("ps", s), "lngt", "Bm"], ["gt"])
                    P.tt("dve", yT[:, h, cs], gt[:, sl], uT[:, h, cs], ALU.mult, ["gt", ("uT", h)], [("yT", h)])
            s2 = C.psn(0, 4)
            for j in range(3):
                s = proj(winFb, 4 + j, 128, NCH, hrhs, ["hT"], G)
                k = P.nxt("sqq", 2)
                P.act(sqq[:, k, :G], C.ps[s][:, :G], AF.Square, [("ps", s)], [("sqq", k)])
                P.ts("dve", qcg[:, j, :G], C.ps[s][:, :G], gqt[:, j:j + 1], None, ALU.mult, None, [("ps", s), "gqt"], [("qcg", j)])
                P.mm(C.ps[s2][:, :G], C.ones[:], sqq[:, k, :G], j == 0, j == 2, ["ones", ("sqq", k)], [("ps", s2)])
            P.act(rq[:, :G], C.ps[s2][:, :G], AF.Sqrt, [("ps", s2), "eps"], ["rq"], bias=C.eps_rms[:], scale=1.0 / 384)
            P.rcp(rq[:, :G], rq[:, :G], ["rq"], ["rq"])
            qrhs = lambda c: qcg[:, c, :G]
            qk = [("qcg", 0), ("qcg", 1), ("qcg", 2)]
            for h in range(4):
                s = proj(wuqnb, h, 128, 3, qrhs, qk, G)
                P.tt("dve", QnT[:, h, :G], C.ps[s][:, :G], rq[:, :G], ALU.mult, [("ps", s), "rq"], [("QnT", h)])
                sa = proj(wuqrb, 2 * h, 64, 3, qrhs, qk, G)
                sb_ = proj(wuqrb, 2 * h + 1, 64, 3, qrhs, qk, G)
                P.tt("dve", r1[:, :G], C.ps[sa][:64, :G], cct[:, :G], ALU.mult, [("ps", sa), "cct"], ["r1"])
                P.tt("dve", r2[:, :G], C.ps[sb_][:64, :G], sst[:, :G], ALU.mult, [("ps", sb_), "sst"], ["r2"])
                P.tt("pool", r1[:, :G], r1[:, :G], r2[:, :G], ALU.add, ["r1", "r2"], ["r1"])
                P.tt("dve", QrT[:, h, :G], r1[:, :G], rq[:64, :G], ALU.mult, ["r1", "rq"], [("QrT", h)])
        for j in range(4):
            if not halo:
                s = proj(winFb, 9 + j, 128, NCH, hrhs, ["hT"], G)
                P.act(sbT[:, j, :G], C.ps[s][:, :G], AF.Copy, [("ps", s)], [("sbT", j)])
            s = proj(winFb, 13 + j, 128, NCH, hrhs, ["hT"], G)
            P.act(scT[:, :G], C.ps[s][:, :G], AF.Copy, [("ps", s)], ["scT"])
            s = proj(winFb, 17 + j, 128, NCH, hrhs, ["hT"], G)
            if halo:
                P.tt("dve", scT[:, :G], C.ps[s][:, :G], scT[:, :G], ALU.mult, [("ps", s), "scT"], ["scT"])
                P.ts("dve", cz[:, j, :], scT[:, G - 2:G], flt[:, 0:1], None, ALU.mult, None, ["scT", "flt"], [("cz", j)])
            else:
                P.cp("pool", zbuf[:, 0:2], cz[:, j, :], [("cz", j)], ["zbuf"])
                P.tt("dve", zbuf[:, 2:2 + G], C.ps[s][:, :G], scT[:, :G], ALU.mult, [("ps", s), "scT"], ["zbuf"])
                P.cp("pool", cz[:, j, :], zbuf[:, G:G + 2], ["zbuf"], [("cz", j)])
                P.ts("dve", cacc[:, :G], zbuf[:, 2:2 + G], scwt[:, 8 + j:9 + j], None, ALU.mult, None, ["zbuf", "scwt"], ["cacc"])
                P.stt("dve", cacc[:, :G], zbuf[:, 1:1 + G], scwt[:, 4 + j:5 + j], cacc[:, :G], ALU.mult, ALU.add, ["zbuf", "cacc"], ["cacc"])
                P.stt("dve", cacc[:, :G], zbuf[:, 0:G], scwt[:, j:j + 1], cacc[:, :G], ALU.mult, ALU.add, ["zbuf", "cacc"], ["cacc"])
                P.tt("dve", yT[:, 8 + j, :G], cacc[:, :G], sbT[:, j, :G], ALU.mult, ["cacc", ("sbT", j)], [("yT", 8 + j)])
        for j in range(4):
            s = proj(winFb, 25 + j, 128, NCH, hrhs, ["hT"], G)
            P.act(scT[:, :G], C.ps[s][:, :G], AF.Sigmoid, [("ps", s)], ["scT"])
            s = proj(winFb, 21 + j, 128, NCH, hrhs, ["hT"], G)
            if halo:
                P.tt("dve", scT[:, :G], C.ps[s][:, :G], scT[:, :G], ALU.mult, [("ps", s), "scT"], ["scT"])
                P.ts("dve", cy[:, j, :], scT[:, G - 30:G], flt[:, 0:1], None, ALU.mult, None, ["scT", "flt"], [("cy", j)])
                continue
            P.cp("pool", ybuf[:, 0:30], cy[:, j, :], [("cy", j)], ["ybuf"])
            P.tt("dve", ybuf[:, 30:30 + G], C.ps[s][:, :G], scT[:, :G], ALU.mult, [("ps", s), "scT"], ["ybuf"])
            P.cp("pool", cy[:, j, :], ybuf[:, G:G + 30], ["ybuf"], [("cy", j)])
            P.ts("dve", cT[:, j, :G], ybuf[:, 30:30 + G], cfwt[:, 120 + j:121 + j], cfbt_[:, j:j + 1], ALU.mult, ALU.add,
                 ["ybuf", "cfwt", "cfbt_"], [("cT", j)])
            for k in range(30):
                P.stt("dve", cT[:, j, :G], ybuf[:, k:k + G], cfwt[:, 4 * k + j:4 * k + j + 1], cT[:, j, :G], ALU.mult, ALU.add,
                      ["ybuf", ("cT", j)], [("cT", j)])
        if halo:
            return
        C.ln_fm(lambda c: cT[:, c, :G], lambda c: ("cT", c), 4, G, 512, C.eps_ln,
                lambda c: cfgt[:, c:c + 1], lambda c: cfbtt[:, c:c + 1],
                lambda c: yT[:, 12 + c, :G], lambda c: ("yT", 12 + c), func=AF.Silu)
        g4 = o0 // 128
        tiles = [("p", kt) for kt in range(NPREV)] + [("o", kt) for kt in range(g4 + 4)]
        for h in range(4):
            first = True
            for i0 in range(0, len(tiles), 4):
                kind = tiles[i0][0]
                kt0 = tiles[i0][1]
                k = P.nxt("kvb", 2)
                srcs = (KNp, KRp, Vp) if kind == "p" else (KNo, KRo, Vo)
                if kind == "p":
                    rk = []
                else:
                    rk = None
                rdn = [] if kind == "p" else [("KNo", kt0 + i) for i in range(4)]
                rdr = [] if kind == "p" else [("KRo", kt0 + i) for i in range(4)]
                rdv = [] if kind == "p" else [("Vo", kt0 + i) for i in range(4)]
                hs = slice(h * 128, (h + 1) * 128)
                P.dma("sp", knb[:, k, :, 0:128], srcs[0][kt0 * 128:(kt0 + 4) * 128, hs].rearrange("(t p) k -> p t k", p=128),
                      rdn, [("knb", k)])
                P.dma("sp", krb[:, k, :, :], srcs[1][kt0 * 64:(kt0 + 4) * 64, :].rearrange("(t p) k -> p t k", p=64),
                      rdr, [("krb", k)])
                P.dma("sp", vpb[:, k, :, 0:128], srcs[2][kt0 * 128:(kt0 + 4) * 128, hs].rearrange("(t p) k -> p t k", p=128),
                      rdv, [("vpb", k)])
                for i in range(4):
                    kt = kt0 + i
                    if kind == "p":
                        c0 = 0
                        bias = sbt[:, kt // NT:kt // NT + 1]
                        diag = False
                    else:
                        d = kt - g4
                        c0 = max(0, d) * 128
                        bias = 0.0
                        diag = d >= 0
                    sS = C.psn(0, 2)
                    P.mm(C.ps[sS][:, c0:G], knb[:, k, i, 0:128], QnT[:, h, c0:G], True, False, [("knb", k), ("QnT", h)], [("ps", sS)])
                    P.mm(C.ps[sS][:, c0:G], krb[:, k, i, :], QrT[:, h, c0:G], False, True, [("krb", k), ("QrT", h)], [("ps", sS)])
                    kp = P.nxt("pT", 3)
                    P.act(pT[:, kp, c0:G], C.ps[sS][:, c0:G], AF.Exp, [("ps", sS), "sbt"], [("pT", kp)], bias=bias, scale=ATT_SCALE)
                    if diag:
                        P.ms("pool", pT[64:128, kp, c0:c0 + 64], 0.0, [("pT", kp)])
                    last = (i0 + i == len(tiles) - 1)
                    P.mm(C.ps[2][:, c0:G], vpb[:, k, i, 0:128], pT[:, kp, c0:G], first, last, [("vpb", k), ("pT", kp)], [("ps", 2)])
                    P.mm(C.ps[3][:, c0:G], C.ones[:], pT[:, kp, c0:G], first, last, ["ones", ("pT", kp)], [("ps", 3)])
                    first = False
            P.ts("dve", rl[:, :G], C.ps[3][:, :G], 1e-30, None, ALU.max, None, [("ps", 3)], ["rl"])
            P.rcp(rl[:, :G], rl[:, :G], ["rl"], ["rl"])
            P.tt("dve", yT[:, 4 + h, :G], C.ps[2][:, :G], rl[:, :G], ALU.mult, [("ps", 2), "rl"], [("yT", 4 + h)])
        for ob in range(NCH):
            kx = P.nxt("xres", 2)
            P.dma("pool", xres[:, kx, :G], x_v[:, ob, t0:t0 + G], [], [("xres", kx)])
            s = proj(woutb, ob, 128, NCH, lambda c: yT[:, c, :G], [("yT", c) for c in range(NCH)], G)
            P.act(big[:, ob, :G], C.ps[s][:, :G], AF.Identity, [("ps", s), "consts"], ["big"], scale=mod1[:, 32 + ob:33 + ob], bias=0.0)
            P.stt("dve", big[:, ob, :G], xres[:, kx, :G], ALPHA, big[:, ob, :G], ALU.mult, ALU.add, [("xres", kx), "big"], ["big"])
        cnt = {"c": 0}
        orig_act = P.act

        def act_hook(out, in_, func, reads, writes, bias=None, scale=None):
            orig_act(out, in_, func, reads, writes, bias=bias, scale=scale)
            if writes and isinstance(writes[0], tuple) and writes[0][0] == "obuf":
                c = cnt["c"]
                cnt["c"] += 1
                P.dma("pool", xmid_v[:, c, o0:o0 + G], out, [writes[0]], [("xmid", c, o0)])

        P.act = act_hook
        C.ln_fm(lambda c: big[:, c, :G], lambda c: "big", NCH, G, D, C.eps_ln,
                lambda c: pmgt[:, c:c + 1], lambda c: pmbt[:, c:c + 1],
                lambda c: obuf[:, c % 2, :G], lambda c: ("obuf", c % 2))
        P.act = orig_act

    if not kv_only:
        group(0, HALO, True)
    for g in range(T // 512):
        group(HALO + g * 512, 512, False)
    if kv_only:
        outs = [(n, kt) for n in ("KNo", "KRo", "Vo") for kt in range(NT)]
    else:
        outs = [("xmid", c, g * 512) for c in range(NCH) for g in range(T // 512)]
    P.dma("sp", winFb[0:1, 0:8], winFb[0:1, 8:16], outs, ["fin"])
    P.emit()
    return nc


def _bf(a):
    return np.asarray(a)


def kernel(x, c, w_mod, b_mod, w_in, gmlp_ln_g, gmlp_ln_b, gmlp_w_s, gmlp_b_s,
           mla_q_norm, mla_w_uq, mla_kv_norm, mla_w_ukv, sconv_w, conf_w_dw, conf_b_dw,
           conf_ln_g, conf_ln_b, w_out, post_mix_g, post_mix_b, ffn_w_up, ffn_w_conv,
           ffn_b_conv, ffn_w_down, post_ffn_g, post_ffn_b):
    f = lambda a: np.asarray(a, dtype=np.float32)
    x = f(x)
    B, S, _ = x.shape
    L = w_mod.shape[0]
    T = S // 4
    NT = T // 128
    cores = list(range(8))
    ar = np.arange

    c = f(c)
    cT = np.ascontiguousarray(c.T.reshape(16, 128, B).transpose(1, 0, 2).reshape(128, 16 * B))
    wcat = np.concatenate([f(w_mod[l]) for l in range(L)], axis=1)
    bcat = np.concatenate([f(b_mod[l]) for l in range(L)], axis=0)
    ncol = wcat.shape[1] // 8
    nblk = ncol // 128
    nc_mod = build_mod()
    maps = []
    for i in cores:
        cols = [ar(i * ncol + j * 128, i * ncol + (j + 1) * 128) for j in range(nblk)]
        maps.append({"cT": cT, "wm": tile_w(wcat, 16, cols).reshape(nblk, 128, 2048), "bm": pl(bcat[i * ncol:(i + 1) * ncol], nblk)})
    res = run_bass_kernel_spmd(nc_mod, maps, core_ids=cores)
    modall = np.zeros((B, wcat.shape[1]), np.float32)
    for i in cores:
        mo = np.asarray(res.results[i]["mo"]).reshape(128, nblk, B)
        for b in range(B):
            modall[b, i * ncol:(i + 1) * ncol] = mo[:, :, b].T.reshape(-1)
    del wcat, maps

    inv_freq = (10000.0 ** (-np.arange(0, 64, 2, dtype=np.float32) / 64)).astype(np.float32)
    nc_kv = build_mix(T, True)
    nc_mix = build_mix(T, False)
    nc_ffn = build_ffn(T)
    xcur = x
    for l in range(L):
        modv = [modall[b, l * 6 * D:(l + 1) * 6 * D].reshape(6, D) for b in range(B)]
        wi = f(w_in[l])
        wq = f(mla_w_uq[l])
        wkv = f(mla_w_ukv[l])
        blocks = ([ar(j * 128, (j + 1) * 128) for j in range(4)] + [ar(1024 + j * 128, 1024 + (j + 1) * 128) for j in range(3)]
                  + [ar(1408 + j * 128, 1408 + (j + 1) * 128) for j in range(2)]
                  + [ar(1728 + j * 128, 1728 + (j + 1) * 128) for j in range(20)])
        shared = {
            "winF": tile_w(wi, 16, blocks),
            "winkr": tile_w(wi, 16, [ar(1664, 1728), np.concatenate([ar(1696, 1728), ar(1664, 1696)])]),
            "wukvk": tile_w(wkv, 2, [ar(h * 256, h * 256 + 128) for h in range(4)]),
            "wukvv": tile_w(wkv, 2, [np.concatenate([ar(h * 256 + 128, h * 256 + 256) for h in range(4)])]),
            "gkv": pl(mla_kv_norm[l], 2),
        }
        rope_cols = []
        for h in range(4):
            rope_cols.append(ar(h * 192 + 128, h * 192 + 192))
            rope_cols.append(np.concatenate([ar(h * 192 + 160, h * 192 + 192), ar(h * 192 + 128, h * 192 + 160)]))
        shared_mix = {
            "winv": tile_w(wi, 16, [ar(512, 1024)]),
            "wuqn": tile_w(wq, 3, [ar(h * 192, h * 192 + 128) for h in range(4)]),
            "wuqr": tile_w(wq, 3, rope_cols),
            "gq": pl(mla_q_norm[l], 3),
            "wsT": np.ascontiguousarray(np.transpose(f(gmlp_w_s[l]), (2, 0, 1)).reshape(128, 512)),
            "bsrow": f(gmlp_b_s[l]).reshape(1, 512),
            "lng": pl(gmlp_ln_g[l], 4),
            "lnbrow": f(gmlp_ln_b[l]).reshape(1, 512),
            "scw": np.ascontiguousarray(np.concatenate([pl(f(sconv_w[l])[k], 4) for k in range(3)], axis=1)),
            "cfw": np.ascontiguousarray(np.concatenate([pl(f(conf_w_dw[l])[k], 4) for k in range(31)], axis=1)),
            "cfb": pl(conf_b_dw[l], 4),
            "cfg": pl(conf_ln_g[l], 4),
            "cfbt": pl(conf_ln_b[l], 4),
            "wout": tile_w(f(w_out[l]), 16, [ar(j * 128, (j + 1) * 128) for j in range(16)]),
            "pmg": pl(post_mix_g[l], 16),
            "pmb": pl(post_mix_b[l], 16),
        }
        percore = []
        for i in cores:
            b, q = divmod(i, 4)
            xe = np.zeros((D, HALO + T), np.float32)
            if q > 0:
                xe[:, :HALO] = xcur[b, q * T - HALO:q * T].T
            xe[:, HALO:] = xcur[b, q * T:(q + 1) * T].T
            pos = np.arange(q * T, (q + 1) * T, dtype=np.float32)
            ang = pos[:, None] * inv_freq[None, :]
            cs, sn = np.cos(ang).astype(np.float32), np.sin(ang).astype(np.float32)
            percore.append({
                "x": xe, "mod": mod_layout(modv[b]),
                "CC": np.ascontiguousarray(np.concatenate([cs, cs], axis=1).T),
                "SS": np.ascontiguousarray(np.concatenate([-sn, sn], axis=1).T),
            })
        res = run_bass_kernel_spmd(nc_kv, [dict(shared, **pc) for pc in percore], core_ids=cores)
        kvs = [{k: np.asarray(res.results[i][k]) for k in ("KNo", "KRo", "Vo")} for i in cores]
        maps = []
        for i in cores:
            b, q = divmod(i, 4)
            others = [j for j in range(4) if j != q]
            sb = np.zeros((128, 3), np.float32)
            for s_, j in enumerate(others):
                sb[:, s_] = 0.0 if j < q else NEG
            m = dict(shared, **shared_mix, **percore[i])
            m["KNp"] = np.ascontiguousarray(np.concatenate([kvs[b * 4 + j]["KNo"] for j in others], axis=0))
            m["KRp"] = np.ascontiguousarray(np.concatenate([kvs[b * 4 + j]["KRo"] for j in others], axis=0))
            m["Vp"] = np.ascontiguousarray(np.concatenate([kvs[b * 4 + j]["Vo"] for j in others], axis=0))
            m["sbias"] = sb
            m["flag"] = np.full((128, 1), 0.0 if q == 0 else 1.0, np.float32)
            maps.append(m)
        res = run_bass_kernel_spmd(nc_mix, maps, core_ids=cores)
        xmid = [np.asarray(res.results[i]["xmid"]) for i in cores]
        del maps, kvs, shared, shared_mix
        wupt = tile_w(f(ffn_w_up[l]), 16, [ar(j * 128, (j + 1) * 128) for j in range(88)])
        wdnt = tile_w(f(ffn_w_down[l]), 44, [ar(j * 128, (j + 1) * 128) for j in range(16)])
        wct = np.ascontiguousarray(np.concatenate([pl(f(ffn_w_conv[l])[k], 44) for k in range(3)], axis=1))
        maps = []
        for i in cores:
            b, q = divmod(i, 4)
            xe = np.zeros((D, HALO + T), np.float32)
            if q > 0:
                xe[:, :HALO] = xmid[i - 1][:, T - HALO:]
            xe[:, HALO:] = xmid[i]
            maps.append({"xm": xe, "mod": mod_layout(modv[b]), "wup": wupt, "wdn": wdnt, "wc": wct,
                         "bc": pl(ffn_b_conv[l], 44), "pg": pl(post_ffn_g[l], 16), "pb": pl(post_ffn_b[l], 16),
                         "flag": np.full((128, 1), 0.0 if q == 0 else 1.0, np.float32)})
        res = run_bass_kernel_spmd(nc_ffn, maps, core_ids=cores)
        xnew = np.empty((B, S, D), np.float32)
        for i in cores:
            b, q = divmod(i, 4)
            xnew[b, q * T:(q + 1) * T] = np.asarray(res.results[i]["xo"]).T
        xcur = xnew
        del maps, wupt, wdnt
    return xcur


from contextlib import ExitStack


class Phase:
    def __init__(self, nc, tag):
        self.nc, self.tag, self.st = nc, tag, ExitStack()

    def al(self, name, shape, dtype):
        return self.st.enter_context(self.nc.sbuf_tensor(f"{self.tag}_{name}", shape, dtype))

    def close(self):
        self.st.close()


def _barrier(C, scratch_dram):
    P = C.P
    start = getattr(P, "bar_start", 0)
    deps = set()
    last = {}
    for o in P.ops[start:]:
        if o.dma:
            deps.add(o.idx)
        else:
            last[o.eng] = o.idx
    for e in ENGS:
        for o in reversed(P.ops[:start]):
            if o.eng == e and e not in last:
                last[e] = o.idx
                break
    deps |= set(last.values())
    fns = {
        "pe": lambda e: e.matmul(C.ps[7][:, 0:1], lhsT=C.ones[:], rhs=C.ones[:, 0:1], start=True, stop=True),
        "act": lambda e: e.activation(out=C.bsc[:, 0:1], in_=C.eps_ln[:], func=AF.Copy),
        "dve": lambda e: e.tensor_copy(out=C.bsc[:, 1:2], in_=C.eps_ln[:]),
        "pool": lambda e: e.memset(C.bsc[:, 2:3], 0.0),
        "sp": lambda e: e.dma_start(out=scratch_dram[0:1, 0:8], in_=scratch_dram[0:1, 8:16]),
    }
    for e in ENGS:
        idx = len(P.ops)
        P.ops.append(Op(e, fns[e], set(deps), e == "sp", idx))
    P.lastw[("ps", 7)] = len(P.ops) - 5
    P.readers[("ps", 7)] = []
    P.bar_start = len(P.ops)


def mix_phase(C, tag, cfg):
    nc, P = C.nc, C.P
    ph = Phase(nc, tag)
    al = ph.al
    K = lambda *a: (tag,) + a
    kv_only = cfg["kv_only"]
    xv = cfg["xv"]
    modt, mod1 = cfg["modt"], cfg["mod1"]
    W = cfg["W"]
    S = cfg["S"]
    KNo, KRo, Vo = cfg["KN"], cfg["KR"], cfg["V"]
    prev = cfg.get("prev")
    halo_first = cfg.get("halo_first", False)

    def ld(name, shape, src):
        t = al(name, shape, F32)
        P.dma("sp", t[:], src, [], [K(name)])
        return t

    gkvt = ld("gkvt", [128, 2], S["gkv"])
    wvv = al("wvv", [128, 2, 512], BF16)
    P.dma("sp", wvv[:], W["wukvvb"].rearrange("p (c n) -> p c n", c=2), [("wdram", id(W["wukvvb"]))], [K("wvv")])
    big = al("big", [128, NCH, 512], F32)
    hT = al("hT", [128, NCH, 512], BF16)
    wbuf = al("wbuf", [128, 3, 2048], BF16)
    sqkv = al("sqkv", [128, 2, 512], BF16)
    kvcg = al("kvcg", [128, 2, 512], BF16)
    rkv = al("rkv", [128, 512], F32)
    rq = rkv
    kst = al("kst", [128, 4, 512], BF16)
    krs = al("krs", [128, 512], BF16)[0:64]
    r1 = al("r1", [128, 512], F32)[0:64]
    r2 = al("r2", [128, 512], F32)[0:64]
    cct = al("cct", [128, 512], F32)[0:64]
    sst = al("sst", [128, 512], F32)[0:64]
    rt = al("rt", [128, 2, 1], F32)
    vst = al("vst", [128, 2, 512], BF16)
    if not kv_only:
        gqt = ld("gqt", [128, 3], S["gq"])
        lngt = ld("lngt", [128, 4], S["lng"])
        scwt = ld("scwt", [128, 12], S["scw"])
        cfwt = ld("cfwt", [128, 124], S["cfw"])
        cfbt_ = ld("cfbt_", [128, 4], S["cfb"])
        cfgt = ld("cfgt", [128, 4], S["cfg"])
        cfbtt = ld("cfbtt", [128, 4], S["cfbt"])
        pmgt = ld("pmgt", [128, 16], S["pmg"])
        pmbt = ld("pmbt", [128, 16], S["pmb"])
        wsTf = ld("wsTf", [128, 512], S["wsT"])
        bsr = ld("bsr", [1, 512], S["bsrow"])
        lnbr = ld("lnbr", [1, 512], S["lnbrow"])
        if prev is not None:
            tbt = ld("tbt", [128, prev["ntiles"]], prev["tb"])
        P.ms("pool", wsTf[64:128, :].rearrange("p (h i) -> p h i", h=4)[:, :, 0:64], 0.0, [K("wsTf")])
        wsTb = al("wsTb", [128, 512], BF16)
        P.cp("dve", wsTb[:], wsTf[:], [K("wsTf")], [K("wsTb")])
        onesf = al("onesf", [128, 128], F32)
        P.ms("pool", onesf[:], 1.0, [K("onesf")])
        rsw = al("rsw", [1, 512], F32)
        s = C.psn(4, 8)
        P.mm(C.ps[s][0:1, :], onesf[:, 0:1], wsTf[:], True, True, [K("onesf"), K("wsTf")], [("ps", s)])
        P.cp("dve", rsw[:], C.ps[s][0:1, :], [("ps", s)], [K("rsw")])
        Bm = al("Bm", [128, 512], F32)
        s = C.psn(4, 8)
        for h in range(4):
            sl = slice(h * 128, (h + 1) * 128)
            P.mm(C.ps[s][:, sl], lnbr[0:1, sl], rsw[0:1, sl], True, False, [K("lnbr"), K("rsw")], [("ps", s)])
            P.mm(C.ps[s][:, sl], onesf[0:1, :], bsr[0:1, sl], False, True, [K("onesf"), K("bsr")], [("ps", s)])
        P.cp("dve", Bm[:], C.ps[s][:], [("ps", s)], [K("Bm")])
        yT = al("yT", [128, NCH, 512], BF16)
        uT = al("uT", [128, 4, 512], BF16)
        wvb = al("wvb", [128, 4, 512], BF16)
        vn = al("vn", [128, 512], BF16)
        bst = al("bst", [128, 8], F32)
        sqq = al("sqq", [128, 3, 512], BF16)
        qcg = al("qcg", [128, 3, 512], BF16)
        QnT = al("QnT", [128, 4, 512], BF16)
        QrT = al("QrT", [128, 4, 512], BF16)[0:64]
        sbT = al("sbT", [128, 4, 512], BF16)
        scT = al("scT", [128, 512], F32)
        vf = scT
        zbuf = al("zbuf", [128, 516], F32)
        ybuf4 = al("ybuf", [128, 4, 544], F32)
        cz = al("cz", [128, 4, 2], F32)
        cy = al("cy", [128, 4, 30], F32)
        cacc = al("cacc", [128, 512], F32)
        gt = cacc
        rl = cacc
        cT = al("cT", [128, 4, 512], F32)
        knb = al("knb", [128, 3, 4, 128], BF16)
        vpb = al("vpb", [128, 3, 4, 128], BF16)
        krb = al("krb", [128, 3, 4, 128], BF16)[0:64]
        pT = al("pT", [128, 4, 512], BF16)
        xres = al("xres", [128, 2, 512], F32)
        obuf = al("obuf", [128, 2, 512], F32)
        xmid_v = cfg["xmid_v"]
        P.ms("pool", cz[:], 0.0, [K("cz", j) for j in range(4)])
        P.ms("pool", cy[:], 0.0, [K("cy", j) for j in range(4)])

    bgc = [0.0]

    def proj(dram2d, blk, M, nk, rhs, rkeys, G):
        k = P.nxt(K("wbuf"), 3)
        P.dma("sp", wbuf[:, k, :nk * M], dram2d[blk * 128:(blk + 1) * 128, :], [("wdram", id(dram2d))], [K("wbuf", k)])
        s = C.psn(0, 8)
        for c in range(nk):
            P.mm(C.ps[s][:M, :G], wbuf[:, k, c * M:(c + 1) * M], rhs(c), c == 0, c == nk - 1,
                 [K("wbuf", k)] + rkeys, [("ps", s)])
        P.flush(3 if nk >= 16 else 0)
        if nk >= 16:
            bgc[0] += cfg.get("bgn", 0.6)
            while bgc[0] >= 1.0:
                P.bg_flush(1)
                bgc[0] -= 1.0
        return s

    def stageL(t0, G, banks=None):
        C.ln_stream(xv, cfg.get("xkeys", []), t0, NCH, G, D, C.eps_ln,
                    lambda c: mod1[:, 16 + c:17 + c], lambda c: modt[:, c:c + 1],
                    lambda c: hT[:, c, :G], lambda c: K("hT"), banks=banks)

    def group(t0, G, gi, nxt):
        g4 = t0 // 128
        if kv_only:
            P.dma("sp", big[:, :, :G], xv[:, :, t0:t0 + G], cfg.get("xkeys", []), [K("big")])
            C.ln_fm(lambda c: big[:, c, :G], lambda c: K("big"), NCH, G, D, C.eps_ln,
                    lambda c: mod1[:, 16 + c:17 + c], lambda c: modt[:, c:c + 1],
                    lambda c: hT[:, c, :G], lambda c: K("hT"))
        hrhs = lambda c: hT[:, c, :G]
        hk = [K("hT")]
        if not kv_only:
            for j in range(4):
                yb = ybuf4[:, j, :]
                s = proj(W["winFb"], 25 + j, 128, NCH, hrhs, hk, G)
                P.act(scT[:, :G], C.ps[s][:, :G], AF.Sigmoid, [("ps", s)], [K("scT")])
                s = proj(W["winFb"], 21 + j, 128, NCH, hrhs, hk, G)
                P.cp("pool", yb[:, 0:30], cy[:, j, :], [K("cy", j)], [K("ybuf", j)])
                P.tt("dve", yb[:, 30:30 + G], C.ps[s][:, :G], scT[:, :G], ALU.mult, [("ps", s), K("scT")], [K("ybuf", j)])
                P.cp("pool", cy[:, j, :], yb[:, G:G + 30], [K("ybuf", j)], [K("cy", j)])
                P.defer(lambda j=j, yb=yb: P.ts("dve", cT[:, j, :G], yb[:, 30:30 + G], cfwt[:, 120 + j:121 + j], cfbt_[:, j:j + 1],
                                                ALU.mult, ALU.add, [K("ybuf", j), K("cfwt"), K("cfbt_")], [K("cT", j)]))
                for k in range(30):
                    P.defer(lambda j=j, yb=yb, k=k: P.stt("dve", cT[:, j, :G], yb[:, k:k + G], cfwt[:, 4 * k + j:4 * k + j + 1],
                                                          cT[:, j, :G], ALU.mult, ALU.add, [K("ybuf", j), K("cT", j)], [K("cT", j)]))
            if halo_first and gi == 0:
                for j in range(4):
                    P.ts("dve", cy[:, j, :], cy[:, j, :], cfg["flt"][:, 0:1], None, ALU.mult, None, [K("cy", j)], [K("cy", j)])
        P.dma("sp", cct[:, :G], cfg["CC"][:, t0:t0 + G], [], [K("cct")])
        P.dma("sp", sst[:, :G], cfg["SS"][:, t0:t0 + G], [], [K("sst")])
        for j in range(2):
            s = proj(W["winFb"], 7 + j, 128, NCH, hrhs, hk, G)
            P.act(sqkv[:, j, :G], C.ps[s][:, :G], AF.Square, [("ps", s)], [K("sqkv", j)])
            P.ts("dve", kvcg[:, j, :G], C.ps[s][:, :G], gkvt[:, j:j + 1], None, ALU.mult, None,
                 [("ps", s), K("gkvt")], [K("kvcg", j)])
        s = C.psn(4, 8)
        for j in range(2):
            P.mm(C.ps[s][:, :G], C.ones[:], sqkv[:, j, :G], j == 0, j == 1, ["ones", K("sqkv", j)], [("ps", s)])
        P.act(rkv[:, :G], C.ps[s][:, :G], AF.Sqrt, [("ps", s), "eps"], [K("rkv")], bias=C.eps_rms[:], scale=1.0 / 256)
        P.rcp(rkv[:, :G], rkv[:, :G], [K("rkv")], [K("rkv")])
        for h in range(4):
            s = proj(W["wukvkb"], h, 128, 2, lambda c: kvcg[:, c, :G], [K("kvcg", 0), K("kvcg", 1)], G)
            P.tt("dve", kst[:, h, :G], C.ps[s][:, :G], rkv[:, :G], ALU.mult, [("ps", s), K("rkv")], [K("kst")])
        sa = proj(W["winkrb"], 0, 64, NCH, hrhs, hk, G)
        sb_ = proj(W["winkrb"], 1, 64, NCH, hrhs, hk, G)
        P.tt("dve", r1[:, :G], C.ps[sa][:64, :G], cct[:, :G], ALU.mult, [("ps", sa), K("cct")], [K("r1")])
        P.tt("dve", r2[:, :G], C.ps[sb_][:64, :G], sst[:, :G], ALU.mult, [("ps", sb_), K("sst")], [K("r2")])
        P.tt("pool", krs[:, :G], r1[:, :G], r2[:, :G], ALU.add, [K("r1"), K("r2")], [K("krs")])
        for sub in range(G // 128):
            kt = g4 + sub
            cs = slice(sub * 128, (sub + 1) * 128)
            P.dma("pool", KNo[kt * 128:(kt + 1) * 128, :].rearrange("p (h k) -> p h k", h=4), kst[:, :, cs],
                  [K("kst")], [K("KNo", kt)])
            P.dma("pool", KRo[kt * 64:(kt + 1) * 64, :], krs[:, cs], [K("krs")], [K("KRo", kt)])
            s = C.psn(4, 8)
            for j in range(2):
                P.mm(C.ps[s][:, 0:1], sqkv[:, j, cs], C.ones[:, 0:1], j == 0, j == 1, [K("sqkv", j), "ones"], [("ps", s)])
            kk = P.nxt(K("rt"), 2)
            P.act(rt[:, kk, :], C.ps[s][:, 0:1], AF.Sqrt, [("ps", s), "eps"], [K("rt", kk)], bias=C.eps_rms[:], scale=1.0 / 256)
            P.rcp(rt[:, kk, :], rt[:, kk, :], [K("rt", kk)], [K("rt", kk)])
            s = C.psn(4, 8)
            for c in range(2):
                P.mm(C.ps[s][:, :], kvcg[:, c, cs], wvv[:, c, :], c == 0, c == 1, [K("kvcg", c), K("wvv")], [("ps", s)])
            P.act(vst[:, kk, :], C.ps[s][:, :], AF.Copy, [("ps", s), K("rt", kk)], [K("vst", kk)], scale=rt[:, kk, :])
            P.dma("pool", Vo[kt * 128:(kt + 1) * 128, :], vst[:, kk, :], [K("vst", kk)], [K("Vo", kt)])
        if kv_only:
            return
        for j in range(4):
            s = proj(W["winFb"], j, 128, NCH, hrhs, hk, G)
            P.act(uT[:, j, :G], C.ps[s][:, :G], AF.Gelu_apprx_tanh, [("ps", s)], [K("uT", j)])
        nsub = G // 128
        for c in range(NCH):
            k = P.nxt(K("wvb"), 4)
            P.dma("sp", wvb[:, k, :], W["winvb"][:, c * 512:(c + 1) * 512], [("wdram", id(W["winvb"]))], [K("wvb", k)])
            for sub in range(nsub):
                P.mm(C.ps[sub][:, :], hT[:, c, sub * 128:(sub + 1) * 128], wvb[:, k, :], c == 0, c == NCH - 1,
                     [K("wvb", k), K("hT")], [("ps", sub)])
        for sub in range(nsub):
            cs = slice(sub * 128, (sub + 1) * 128)
            P.act(vf[:], C.ps[sub][:, :], AF.Gelu_apprx_tanh, [("ps", sub)], [K("scT")])
            P.op("dve", lambda e: e.bn_stats(out=bst[:, 0:6], in_=vf[:]), [K("scT")], [K("bst")])
            P.op("dve", lambda e: e.bn_aggr(out=bst[:, 6:8], in_=bst[:, 0:6]), [K("bst")], [K("bst")])
            P.act(bst[:, 7:8], bst[:, 7:8], AF.Sqrt, [K("bst"), "eps"], [K("bst")], bias=C.eps_ln[:], scale=1.0)
            P.rcp(bst[:, 7:8], bst[:, 7:8], [K("bst")], [K("bst")])
            P.ts("dve", vn[:], vf[:], bst[:, 6:7], bst[:, 7:8], ALU.subtract, ALU.mult, [K("scT"), K("bst")], [K("vn")])
            s = C.psn(4, 8)
            for h in range(4):
                sl = slice(h * 128, (h + 1) * 128)
                P.mm(C.ps[s][:, sl], vn[:, sl], wsTb[:, sl], True, True, [K("vn"), K("wsTb")], [("ps", s)])
            for h in range(4):
                sl = slice(h * 128, (h + 1) * 128)
                P.stt("dve", gt[:, sl], C.ps[s][:, sl], lngt[:, h:h + 1], Bm[:, sl], ALU.mult, ALU.add,
                      [("ps", s), K("lngt"), K("Bm")], [K("cacc")])
                P.tt("dve", yT[:, h, cs], gt[:, sl], uT[:, h, cs], ALU.mult, [K("cacc"), K("uT", h)], [K("yT", h)])
        for j in range(3):
            s = proj(W["winFb"], 4 + j, 128, NCH, hrhs, hk, G)
            P.act(sqq[:, j, :G], C.ps[s][:, :G], AF.Square, [("ps", s)], [K("sqq", j)])
            P.ts("dve", qcg[:, j, :G], C.ps[s][:, :G], gqt[:, j:j + 1], None, ALU.mult, None, [("ps", s), K("gqt")], [K("qcg", j)])
        s2 = C.psn(0, 4)
        for j in range(3):
            P.mm(C.ps[s2][:, :G], C.ones[:], sqq[:, j, :G], j == 0, j == 2, ["ones", K("sqq", j)], [("ps", s2)])
        P.act(rq[:, :G], C.ps[s2][:, :G], AF.Sqrt, [("ps", s2), "eps"], [K("rkv")], bias=C.eps_rms[:], scale=1.0 / 384)
        P.rcp(rq[:, :G], rq[:, :G], [K("rkv")], [K("rkv")])
        qrhs = lambda c: qcg[:, c, :G]
        qk = [K("qcg", 0), K("qcg", 1), K("qcg", 2)]
        for h in range(4):
            s = proj(W["wuqnb"], h, 128, 3, qrhs, qk, G)
            P.tt("dve", QnT[:, h, :G], C.ps[s][:, :G], rq[:, :G], ALU.mult, [("ps", s), K("rkv")], [K("QnT", h)])
            sa = proj(W["wuqrb"], 2 * h, 64, 3, qrhs, qk, G)
            sb_ = proj(W["wuqrb"], 2 * h + 1, 64, 3, qrhs, qk, G)
            P.tt("dve", r1[:, :G], C.ps[sa][:64, :G], cct[:, :G], ALU.mult, [("ps", sa), K("cct")], [K("r1")])
            P.tt("dve", r2[:, :G], C.ps[sb_][:64, :G], sst[:, :G], ALU.mult, [("ps", sb_), K("sst")], [K("r2")])
            P.tt("pool", r1[:, :G], r1[:, :G], r2[:, :G], ALU.add, [K("r1"), K("r2")], [K("r1")])
            P.tt("dve", QrT[:, h, :G], r1[:, :G], rq[:64, :G], ALU.mult, [K("r1"), K("rkv")], [K("QrT", h)])
        for j in range(4):
            s = proj(W["winFb"], 9 + j, 128, NCH, hrhs, hk, G)
            P.act(sbT[:, j, :G], C.ps[s][:, :G], AF.Copy, [("ps", s)], [K("sbT", j)])
            s = proj(W["winFb"], 13 + j, 128, NCH, hrhs, hk, G)
            P.act(scT[:, :G], C.ps[s][:, :G], AF.Copy, [("ps", s)], [K("scT")])
            s = proj(W["winFb"], 17 + j, 128, NCH, hrhs, hk, G)
            P.cp("pool", zbuf[:, 0:2], cz[:, j, :], [K("cz", j)], [K("zbuf")])
            P.tt("dve", zbuf[:, 2:2 + G], C.ps[s][:, :G], scT[:, :G], ALU.mult, [("ps", s), K("scT")], [K("zbuf")])
            P.cp("pool", cz[:, j, :], zbuf[:, G:G + 2], [K("zbuf")], [K("cz", j)])
            P.ts("dve", cacc[:, :G], zbuf[:, 2:2 + G], scwt[:, 8 + j:9 + j], None, ALU.mult, None, [K("zbuf"), K("scwt")], [K("cacc")])
            P.stt("dve", cacc[:, :G], zbuf[:, 1:1 + G], scwt[:, 4 + j:5 + j], cacc[:, :G], ALU.mult, ALU.add, [K("zbuf"), K("cacc")], [K("cacc")])
            P.stt("dve", cacc[:, :G], zbuf[:, 0:G], scwt[:, j:j + 1], cacc[:, :G], ALU.mult, ALU.add, [K("zbuf"), K("cacc")], [K("cacc")])
            P.tt("dve", yT[:, 8 + j, :G], cacc[:, :G], sbT[:, j, :G], ALU.mult, [K("cacc"), K("sbT", j)], [K("yT", 8 + j)])
        if halo_first and gi == 0:
            for j in range(4):
                P.ts("dve", cz[:, j, :], cz[:, j, :], cfg["flt"][:, 0:1], None, ALU.mult, None, [K("cz", j)], [K("cz", j)])
        if nxt is not None:
            P.deferring = True
            stageL(*nxt, banks=(6, 7))
            P.deferring = False
        tiles = []
        if prev is not None:
            tiles += [("p", kt) for kt in range(prev["ntiles"])]
        tiles += [("o", kt) for kt in range(g4 + G // 128)]
        SB = (0, 1, 4, 5)
        PD = 2
        for h in range(4):
            hs = slice(h * 128, (h + 1) * 128)
            tl = []
            i = 0
            ch = 0
            while i < len(tiles):
                kind, kt0 = tiles[i]
                n = 1
                while n < 4 and i + n < len(tiles) and tiles[i + n][0] == kind:
                    n += 1
                for a in range(n):
                    tl.append(dict(kind=kind, kt=kt0 + a, a=a, ch=ch, load=(kind, kt0, n) if a == 0 else None))
                ch += 1
                i += n
            ntl = len(tl)
            cbuf = {}
            st = {}

            def stageS(idx):
                r = tl[idx]
                if r["load"] is not None:
                    kind, kt0, n = r["load"]
                    k = P.nxt(K("kvb"), 3)
                    cbuf[r["ch"]] = k
                    if kind == "p":
                        srcs = (prev["KN"], prev["KR"], prev["V"])
                        rdn = rdr = rdv = []
                    else:
                        srcs = (KNo, KRo, Vo)
                        rdn = [K("KNo", kt0 + a) for a in range(n)]
                        rdr = [K("KRo", kt0 + a) for a in range(n)]
                        rdv = [K("Vo", kt0 + a) for a in range(n)]
                    P.dma("sp", knb[:, k, 0:n, :], srcs[0][kt0 * 128:(kt0 + n) * 128, hs].rearrange("(t p) k -> p t k", p=128),
                          rdn, [K("knb", k)])
                    P.dma("sp", krb[:, k, 0:n, :], srcs[1][kt0 * 64:(kt0 + n) * 64, :].rearrange("(t p) k -> p t k", p=64),
                          rdr, [K("krb", k)])
                    P.dma("sp", vpb[:, k, 0:n, :], srcs[2][kt0 * 128:(kt0 + n) * 128, hs].rearrange("(t p) k -> p t k", p=128),
                          rdv, [K("vpb", k)])
                k = cbuf[r["ch"]]
                kt, a = r["kt"], r["a"]
                if r["kind"] == "p":
                    c0, bias, diag = 0, tbt[:, kt:kt + 1], False
                else:
                    d = kt - g4
                    c0 = max(0, d) * 128
                    diag = d >= 0
                    bias = cfg["hbt"][:, 0:1] if (halo_first and kt == 0 and g4 > 0) else 0.0
                sS = SB[P.nxt(K("sb"), 4)]
                P.mm(C.ps[sS][:, c0:G], knb[:, k, a, :], QnT[:, h, c0:G], True, False, [K("knb", k), K("QnT", h)], [("ps", sS)])
                P.mm(C.ps[sS][:, c0:G], krb[:, k, a, :], QrT[:, h, c0:G], False, True, [K("krb", k), K("QrT", h)], [("ps", sS)])
                kp = P.nxt(K("pT"), 4)
                P.act(pT[:, kp, c0:G], C.ps[sS][:, c0:G], AF.Exp, [("ps", sS)], [K("pT", kp)], bias=bias, scale=ATT_SCALE)
                if diag:
                    P.ms("dve", pT[64:128, kp, c0:c0 + 64], 0.0, [K("pT", kp)])
                st[idx] = (k, kp, c0, a)

            def stagePV(idx):
                k, kp, c0, a = st[idx]
                first, last = idx == 0, idx == ntl - 1
                P.mm(C.ps[2][:, c0:G], vpb[:, k, a, :], pT[:, kp, c0:G], first, last, [K("vpb", k), K("pT", kp)], [("ps", 2)])
                P.mm(C.ps[3][:, c0:G], C.ones[:], pT[:, kp, c0:G], first, last, ["ones", K("pT", kp)], [("ps", 3)])
                P.flush(2)

            for idx in range(ntl + PD):
                if idx < ntl:
                    stageS(idx)
                if idx >= PD:
                    stagePV(idx - PD)
            P.ts("dve", rl[:, :G], C.ps[3][:, :G], 1e-30, None, ALU.max, None, [("ps", 3)], [K("cacc")])
            P.rcp(rl[:, :G], rl[:, :G], [K("cacc")], [K("cacc")])
            P.tt("dve", yT[:, 4 + h, :G], C.ps[2][:, :G], rl[:, :G], ALU.mult, [("ps", 2), K("cacc")], [K("yT", 4 + h)])
        P.flush()
        C.ln_fm(lambda c: cT[:, c, :G], lambda c: K("cT", c), 4, G, 512, C.eps_ln,
                lambda c: cfgt[:, c:c + 1], lambda c: cfbtt[:, c:c + 1],
                lambda c: yT[:, 12 + c, :G], lambda c: K("yT", 12 + c), func=AF.Silu)
        for ob in range(NCH):
            kx = P.nxt(K("xres"), 2)
            P.dma("pool", xres[:, kx, :G], xv[:, ob, t0:t0 + G], cfg.get("xkeys", []), [K("xres", kx)])
            s = proj(W["woutb"], ob, 128, NCH, lambda c: yT[:, c, :G], [K("yT", c) for c in range(NCH)], G)
            P.act(big[:, ob, :G], C.ps[s][:, :G], AF.Identity, [("ps", s)], [K("big")], scale=mod1[:, 32 + ob:33 + ob], bias=0.0)
            P.stt("dve", big[:, ob, :G], xres[:, kx, :G], ALPHA, big[:, ob, :G], ALU.mult, ALU.add, [K("xres", kx), K("big")], [K("big")])
        cnt = {"c": 0}
        orig_act = P.act

        def act_hook(out, in_, func, reads, writes, bias=None, scale=None):
            orig_act(out, in_, func, reads, writes, bias=bias, scale=scale)
            if writes and isinstance(writes[0], tuple) and len(writes[0]) > 1 and writes[0][1] == "obuf":
                c = cnt["c"]
                cnt["c"] += 1
                P.dma("pool", xmid_v[:, c, t0:t0 + G], out, [writes[0]], [K("xmid", c, t0)])

        P.act = act_hook
        C.ln_fm(lambda c: big[:, c, :G], lambda c: K("big"), NCH, G, D, C.eps_ln,
                lambda c: pmgt[:, c:c + 1], lambda c: pmbt[:, c:c + 1],
                lambda c: obuf[:, c % 2, :G], lambda c: K("obuf", c % 2))
        P.act = orig_act

    glist = cfg["groups"]
    if not kv_only:
        stageL(*glist[0])
    for gi, (t0, G) in enumerate(glist):
        group(t0, G, gi, glist[gi + 1] if (gi + 1 < len(glist) and not kv_only) else None)
    if cfg.get("bg_end", True):
        P.bg_flush()
    ph.close()


def ffn_phase(C, tag, cfg):
    nc, P = C.nc, C.P
    ph = Phase(nc, tag)
    al = ph.al
    K = lambda *a: (tag,) + a
    xm_v, xo_v, ooff = cfg["xm_v"], cfg["xo_v"], cfg["ooff"]
    modt, mod1 = cfg["modt"], cfg["mod1"]
    wupb, wdnb = cfg["wupb"], cfg["wdnb"]
    S = cfg["S"]

    def ld(name, shape, src):
        t = al(name, shape, F32)
        P.dma("sp", t[:], src, [], [K(name)])
        return t

    wct = ld("wct", [128, 3 * NFF], S["wc"])
    bct = ld("bct", [128, NFF], S["bc"])
    pgt = ld("pgt", [128, 16], S["pg"])
    pbt = ld("pbt", [128, 16], S["pb"])
    big = al("big", [128, NCH, 512], F32)
    xin = al("xin", [128, NCH, 512], F32)
    hT = al("hT", [128, NCH, 512], BF16)
    ffT = al("ffT", [128, NFF, 512], BF16)
    wbuf = al("wbuf", [128, 4, 2048], BF16)
    wdbuf = al("wdbuf", [128, 2, 5632], BF16)
    gbuf = al("gbuf", [128, 2, 516], F32)
    acc = al("acc", [128, 2, 512], F32)
    carry = al("carry", [128, NFF, 2], F32)
    xres = al("xres", [128, 2, 512], F32)
    obuf = al("obuf", [128, 2, 512], F32)
    P.ms("pool", carry[:], 0.0, [K("carry", j) for j in range(NFF)])

    def stageL(t0, G, halo):
        P.dma("sp", xin[:, :, :G], xm_v[:, :, t0:t0 + G], [], [K("xin")])
        C.ln_fm(lambda c: xin[:, c, :G], lambda c: K("xin"), NCH, G, D, C.eps_ln,
                lambda c: mod1[:, 64 + c:65 + c], lambda c: modt[:, 48 + c:49 + c],
                lambda c: hT[:, c, :G], lambda c: K("hT"))

    def group(t0, G, halo, nxt):
        for j in range(NFF):
            blocks = [j] if halo else [j, NFF + j]
            pss = []
            for blk in blocks:
                k = P.nxt(K("wbuf"), 4)
                P.dma("sp", wbuf[:, k, :], wupb[blk * 128:(blk + 1) * 128, :], [("wdram", id(wupb))], [K("wbuf", k)])
                if blk % 12 == 0:
                    P.bg_flush(1)
                s = C.psn(0, 8)
                for c in range(NCH):
                    P.mm(C.ps[s][:, :G], wbuf[:, k, c * 128:(c + 1) * 128], hT[:, c, :G], c == 0, c == NCH - 1,
                         [K("wbuf", k), K("hT")], [("ps", s)])
                pss.append(s)
            sa = pss[0]
            if halo:
                P.ts("dve", carry[:, j, :], C.ps[sa][:, G - 2:G], cfg["flt"][:, 0:1], None, ALU.mult, None,
                     [("ps", sa)], [K("carry", j)])
                continue
            sb_ = pss[1]
            k = P.nxt(K("gbuf"), 2)
            gb = gbuf[:, k, :]
            P.cp("pool", gb[:, 0:2], carry[:, j, :], [K("carry", j)], [K("gbuf", k)])
            P.act(gb[:, 2:2 + G], C.ps[sa][:, :G], AF.Copy, [("ps", sa)], [K("gbuf", k)])
            a = acc[:, k, :G]
            P.ts("dve", a, gb[:, 2:2 + G], wct[:, 2 * NFF + j:2 * NFF + j + 1], bct[:, j:j + 1], ALU.mult, ALU.add,
                 [K("gbuf", k), K("wct"), K("bct")], [K("acc", k)])
            P.stt("dve", a, gb[:, 1:1 + G], wct[:, NFF + j:NFF + j + 1], a, ALU.mult, ALU.add, [K("gbuf", k), K("acc", k)], [K("acc", k)])
            P.stt("dve", a, gb[:, 0:G], wct[:, j:j + 1], a, ALU.mult, ALU.add, [K("gbuf", k), K("acc", k)], [K("acc", k)])
            P.cp("pool", carry[:, j, :], gb[:, G:G + 2], [K("gbuf", k)], [K("carry", j)])
            P.act(a, a, AF.Silu, [K("acc", k)], [K("acc", k)])
            P.tt("dve", ffT[:, j, :G], a, C.ps[sb_][:, :G], ALU.mult, [K("acc", k), ("ps", sb_)], [K("ffT", j)])
        if nxt is not None:
            stageL(*nxt)
        if halo:
            return
        for ob in range(NCH):
            k = P.nxt(K("wdbuf"), 2)
            P.dma("sp", wdbuf[:, k, :], wdnb[ob * 128:(ob + 1) * 128, :], [("wdram", id(wdnb))], [K("wdbuf", k)])
            kx = P.nxt(K("xres"), 2)
            P.dma("pool", xres[:, kx, :G], xm_v[:, ob, t0:t0 + G], [], [K("xres", kx)])
            s = C.psn(0, 8)
            for c in range(NFF):
                P.mm(C.ps[s][:, :G], wdbuf[:, k, c * 128:(c + 1) * 128], ffT[:, c, :G], c == 0, c == NFF - 1,
                     [K("wdbuf", k), K("ffT", c)], [("ps", s)])
            P.act(big[:, ob, :G], C.ps[s][:, :G], AF.Identity, [("ps", s)], [K("big")], scale=mod1[:, 80 + ob:81 + ob], bias=0.0)
            P.stt("dve", big[:, ob, :G], xres[:, kx, :G], ALPHA, big[:, ob, :G], ALU.mult, ALU.add, [K("xres", kx), K("big")], [K("big")])
        cnt = {"c": 0}
        orig_act = P.act

        def act_hook(out, in_, func, reads, writes, bias=None, scale=None):
            orig_act(out, in_, func, reads, writes, bias=bias, scale=scale)
            if writes and isinstance(writes[0], tuple) and len(writes[0]) > 1 and writes[0][1] == "obuf":
                c = cnt["c"]
                cnt["c"] += 1
                P.dma("pool", xo_v[:, c, t0 - ooff:t0 - ooff + G], out, [writes[0]], [K("xo", c, t0)])

        P.act = act_hook
        C.ln_fm(lambda c: big[:, c, :G], lambda c: K("big"), NCH, G, D, C.eps_ln,
                lambda c: pgt[:, c:c + 1], lambda c: pbt[:, c:c + 1],
                lambda c: obuf[:, c % 2, :G], lambda c: K("obuf", c % 2))
        P.act = orig_act

    glist = cfg["groups"]
    stageL(*glist[0])
    for gi, (t0, G, halo) in enumerate(glist):
        group(t0, G, halo, glist[gi + 1] if gi + 1 < len(glist) else None)
    P.bg_flush()
    ph.close()


MIXW = (("winF", 29 * 128, 2048), ("winkr", 2 * 128, 1024), ("wukvk", 4 * 128, 256), ("wukvv", 128, 1024),
        ("winv", 128, 8192), ("wuqn", 4 * 128, 384), ("wuqr", 8 * 128, 192), ("wout", 16 * 128, 2048))
FFNW = (("wup", 88 * 128, 2048), ("wdn", 16 * 128, 5632))
MIXS = (("gkv", [128, 2]), ("gq", [128, 3]), ("wsT", [128, 512]), ("bsrow", [1, 512]), ("lng", [128, 4]),
        ("lnbrow", [1, 512]), ("scw", [128, 12]), ("cfw", [128, 124]), ("cfb", [128, 4]), ("cfg", [128, 4]),
        ("cfbt", [128, 4]), ("pmg", [128, 16]), ("pmb", [128, 16]))
FFNS = (("wc", [128, 3 * NFF]), ("bc", [128, NFF]), ("pg", [128, 16]), ("pb", [128, 16]))


def build_fused(TA, T, L=2):
    nc = bass.Bass("TRN2", target_bir_lowering=False)
    NTA = TA // 128
    TE = HALO + T

    def din(name, shape, dtype=F32):
        return nc.dram_tensor(name, shape, dtype, kind="ExternalInput").ap()

    def dsc(name, shape, dtype):
        return nc.dram_tensor(name, shape, dtype).ap()

    xT = din("xT", [D, TA])
    cT = din("cT", [128, 16])
    wm = din("wm", [L * 96 * 128, 2048])
    bm = din("bm", [128, L * 96])
    CCa, SSa = din("CCa", [64, TA]), din("SSa", [64, TA])
    CCo, SSo = din("CCo", [64, TE]), din("SSo", [64, TE])
    sel, flag, hb, tb = din("sel", [128, 4]), din("flag", [128, 1]), din("hb", [128, 1]), din("tb", [128, 3 * (T // 128)])
    Wf, Wb, Sm = [], [], []
    for l in range(L):
        wf, wb, sm = {}, {}, {}
        for nm, r, c_ in MIXW + FFNW:
            wf[nm] = din(f"{nm}{l}", [r, c_])
            wb[nm + "b"] = dsc(f"{nm}b{l}", [r, c_], BF16)
        for nm, shp in MIXS + FFNS:
            sm[nm] = din(f"{nm}{l}", shp)
        Wf.append(wf)
        Wb.append(wb)
        Sm.append(sm)
    xo = nc.dram_tensor("xo", [D, T], F32, kind="ExternalOutput").ap()
    KN0, KR0, V0 = dsc("KN0", [NTA * 128, 512], BF16), dsc("KR0", [NTA * 64, 128], BF16), dsc("V0", [NTA * 128, 512], BF16)
    KNp, KRp, Vp = dsc("KNp", [NTA * 128, 512], BF16), dsc("KRp", [NTA * 64, 128], BF16), dsc("Vp", [NTA * 128, 512], BF16)
    NTE = TE // 128
    KN1, KR1, V1 = dsc("KN1", [NTE * 128, 512], BF16), dsc("KR1", [NTE * 64, 128], BF16), dsc("V1", [NTE * 128, 512], BF16)
    xmid0, x1 = dsc("xmid0", [D, TA], F32), dsc("x1", [D, TA], F32)
    x1e, xmid1e = dsc("x1e", [D, TE], F32), dsc("xmid1e", [D, TE], F32)
    scr = dsc("scr", [1, 64], F32)
    fm = lambda ap: ap.rearrange("(c p) t -> p c t", p=128)

    C = Ctx(nc)
    P = C.P
    C.bsc = nc.alloc_sbuf_tensor("bsc", [128, 4], F32)
    modall = C.load("modall_", [128, L * 96], bm)
    mod1all = nc.alloc_sbuf_tensor("mod1all", [128, L * 96], F32)
    selt = C.load("selt", [128, 4], sel)
    flt = C.load("flt", [128, 1], flag)
    hbt = C.load("hbt", [128, 1], hb)

    def cast(l, table):
        for nm, r, c_ in table:
            C.cast_weights_bg(Wb[l][nm + "b"], Wf[l][nm], r, c_)

    cast(0, MIXW)
    ph = Phase(nc, "mod")
    ca = ph.al("ca", [128, 16], F32)
    wbm = ph.al("wbm", [128, 2, 2048], F32)
    P.dma("sp", ca[:], cT, [], ["ca"])
    P.act(ca[:], ca[:], AF.Silu, ["ca"], ["ca"])
    for j in range(L * 96):
        k = j % 2
        P.dma("sp", wbm[:, k, :], wm[j * 128:(j + 1) * 128, :], [], [("wbm", k)])
        P.bg_flush(2)
        for kc in range(16):
            P.mm(C.ps[4][:, j:j + 1], wbm[:, k, kc * 128:(kc + 1) * 128], ca[:, kc:kc + 1], kc == 0, kc == 15,
                 [("wbm", k), "ca"], [("ps", 4)])
    P.tt("dve", modall[:], C.ps[4][:, 0:L * 96], modall[:], ALU.add, [("ps", 4), "modall_"], ["modall_"])
    P.ts("dve", mod1all[:], modall[:], 1.0, None, ALU.add, None, ["modall_"], ["consts"])
    P.bg_flush()
    ph.close()
    _barrier(C, scr)

    def mods(l):
        return modall[:, l * 96:(l + 1) * 96], mod1all[:, l * 96:(l + 1) * 96]

    cast(0, FFNW)
    m0, m01 = mods(0)
    mix_phase(C, "m0", dict(kv_only=False, xv=fm(xT), modt=m0, mod1=m01, W=Wb[0], S=Sm[0], KN=KN0, KR=KR0, V=V0,
                            CC=CCa, SS=SSa, xmid_v=fm(xmid0), groups=[(g * 512, 512) for g in range(TA // 512)]))
    _barrier(C, scr)
    cast(1, MIXW)
    ffn_phase(C, "f0", dict(xm_v=fm(xmid0), xo_v=fm(x1), ooff=0, modt=m0, mod1=m01, wupb=Wb[0]["wupb"], wdnb=Wb[0]["wdnb"],
                            S=Sm[0], groups=[(g * 512, 512, False) for g in range(TA // 512)]))
    _barrier(C, scr)
    cast(1, FFNW)
    m1, m11 = mods(1)
    mix_phase(C, "k1", dict(kv_only=True, xv=fm(x1), modt=m1, mod1=m11, W=Wb[1], S=Sm[1], KN=KNp, KR=KRp, V=Vp,
                            CC=CCa, SS=SSa, bgn=2, bg_end=False, groups=[(g * 512, 512) for g in range(3 * T // 512)]))
    _barrier(C, scr)
    ph = Phase(nc, "sel")
    pc = [ph.al(f"pc{j}", [128, NCH, 512], F32) for j in range(4)]
    x1v, x1ev = fm(x1), fm(x1e)
    pieces = [(0, HALO, True)] + [(HALO + i * 512, 512, False) for i in range(T // 512)]
    for (e0, w, is_halo) in pieces:
        js = [1, 2, 3] if is_halo else [0, 1, 2, 3]
        for j in js:
            s0 = j * T - HALO + e0
            P.dma("sp", pc[j][:, :, :w], x1v[:, :, s0:s0 + w], [], [("pc", j)])
        j0 = js[0]
        P.ts("dve", pc[j0][:, :, :w], pc[j0][:, :, :w], selt[:, j0:j0 + 1], None, ALU.mult, None, [("pc", j0), "selt"], [("pc", j0)])
        for j in js[1:]:
            P.stt("dve", pc[j0][:, :, :w], pc[j][:, :, :w], selt[:, j:j + 1], pc[j0][:, :, :w], ALU.mult, ALU.add,
                  [("pc", j), ("pc", j0)], [("pc", j0)])
        P.dma("pool", x1ev[:, :, e0:e0 + w], pc[j0][:, :, :w], [("pc", j0)], [("x1e", e0)])
    ph.close()
    _barrier(C, scr)
    own_groups = [(0, HALO)] + [(HALO + g * 512, 512) for g in range(T // 512)]
    mix_phase(C, "m1", dict(kv_only=False, xv=fm(x1e), modt=m1, mod1=m11, W=Wb[1], S=Sm[1], KN=KN1, KR=KR1, V=V1,
                            CC=CCo, SS=SSo, xmid_v=fm(xmid1e), groups=own_groups, halo_first=True, flt=flt, hbt=hbt,
                            bgn=2, prev=dict(KN=KNp, KR=KRp, V=Vp, ntiles=3 * (T // 128), tb=tb)))
    _barrier(C, scr)
    ffn_phase(C, "f1", dict(xm_v=fm(xmid1e), xo_v=fm(xo), ooff=HALO, modt=m1, mod1=m11, wupb=Wb[1]["wupb"], wdnb=Wb[1]["wdnb"],
                            S=Sm[1], flt=flt, groups=[(0, HALO, True)] + [(HALO + g * 512, 512, False) for g in range(T // 512)]))
    _barrier(C, scr)
    P.dma("sp", scr[0:1, 32:40], scr[0:1, 40:48], [], ["fin"])
    P.emit()
    return nc


def layer_inputs(l, w_in, gmlp_ln_g, gmlp_ln_b, gmlp_w_s, gmlp_b_s, mla_q_norm, mla_w_uq, mla_kv_norm, mla_w_ukv,
                 sconv_w, conf_w_dw, conf_b_dw, conf_ln_g, conf_ln_b, w_out, post_mix_g, post_mix_b, ffn_w_up,
                 ffn_w_conv, ffn_b_conv, ffn_w_down, post_ffn_g, post_ffn_b):
    f = lambda a: np.asarray(a, dtype=np.float32)
    ar = np.arange
    wi, wq, wkv = f(w_in[l]), f(mla_w_uq[l]), f(mla_w_ukv[l])
    blocks = ([ar(j * 128, (j + 1) * 128) for j in range(4)] + [ar(1024 + j * 128, 1024 + (j + 1) * 128) for j in range(3)]
              + [ar(1408 + j * 128, 1408 + (j + 1) * 128) for j in range(2)]
              + [ar(1728 + j * 128, 1728 + (j + 1) * 128) for j in range(20)])
    rope_cols = []
    for h in range(4):
        rope_cols.append(ar(h * 192 + 128, h * 192 + 192))
        rope_cols.append(np.concatenate([ar(h * 192 + 160, h * 192 + 192), ar(h * 192 + 128, h * 192 + 160)]))
    d = {
        "winF": tile_w(wi, 16, blocks),
        "winkr": tile_w(wi, 16, [ar(1664, 1728), np.concatenate([ar(1696, 1728), ar(1664, 1696)])]),
        "wukvk": tile_w(wkv, 2, [ar(h * 256, h * 256 + 128) for h in range(4)]),
        "wukvv": tile_w(wkv, 2, [np.concatenate([ar(h * 256 + 128, h * 256 + 256) for h in range(4)])]),
        "gkv": pl(mla_kv_norm[l], 2),
        "winv": tile_w(wi, 16, [ar(512, 1024)]),
        "wuqn": tile_w(wq, 3, [ar(h * 192, h * 192 + 128) for h in range(4)]),
        "wuqr": tile_w(wq, 3, rope_cols),
        "gq": pl(mla_q_norm[l], 3),
        "wsT": np.ascontiguousarray(np.transpose(f(gmlp_w_s[l]), (2, 0, 1)).reshape(128, 512)),
        "bsrow": f(gmlp_b_s[l]).reshape(1, 512),
        "lng": pl(gmlp_ln_g[l], 4),
        "lnbrow": f(gmlp_ln_b[l]).reshape(1, 512),
        "scw": np.ascontiguousarray(np.concatenate([pl(f(sconv_w[l])[k], 4) for k in range(3)], axis=1)),
        "cfw": np.ascontiguousarray(np.concatenate([pl(f(conf_w_dw[l])[k], 4) for k in range(31)], axis=1)),
        "cfb": pl(conf_b_dw[l], 4),
        "cfg": pl(conf_ln_g[l], 4),
        "cfbt": pl(conf_ln_b[l], 4),
        "wout": tile_w(f(w_out[l]), 16, [ar(j * 128, (j + 1) * 128) for j in range(16)]),
        "pmg": pl(post_mix_g[l], 16),
        "pmb": pl(post_mix_b[l], 16),
        "wup": tile_w(f(ffn_w_up[l]), 16, [ar(j * 128, (j + 1) * 128) for j in range(88)]),
        "wdn": tile_w(f(ffn_w_down[l]), 44, [ar(j * 128, (j + 1) * 128) for j in range(16)]),
        "wc": np.ascontiguousarray(np.concatenate([pl(f(ffn_w_conv[l])[k], 44) for k in range(3)], axis=1)),
        "bc": pl(ffn_b_conv[l], 44),
        "pg": pl(post_ffn_g[l], 16),
        "pb": pl(post_ffn_b[l], 16),
    }
    return {f"{k}{l}": v for k, v in d.items()}


def rope_tables(pos):
    inv_freq = (10000.0 ** (-np.arange(0, 64, 2, dtype=np.float32) / 64)).astype(np.float32)
    ang = pos.astype(np.float32)[:, None] * inv_freq[None, :]
    cs, sn = np.cos(ang).astype(np.float32), np.sin(ang).astype(np.float32)
    return (np.ascontiguousarray(np.concatenate([cs, cs], axis=1).T),
            np.ascontiguousarray(np.concatenate([-sn, sn], axis=1).T))


kernel_unfused = kernel


def kernel(x, c, w_mod, b_mod, **kw):
    f = lambda a: np.asarray(a, dtype=np.float32)
    x = f(x)
    B, S, _ = x.shape
    L = w_mod.shape[0]
    T = S // 4
    cores = list(range(8))
    shared = {}
    for l in range(L):
        shared.update(layer_inputs(l, **kw))
    wcat = np.concatenate([f(w_mod[l]) for l in range(L)], axis=1)
    bcat = np.concatenate([f(b_mod[l]) for l in range(L)], axis=0)
    shared["wm"] = tile_w(wcat, 16, [np.arange(j * 128, (j + 1) * 128) for j in range(L * 96)])
    shared["bm"] = pl(bcat, L * 96)
    shared["CCa"], shared["SSa"] = rope_tables(np.arange(S))
    del wcat
    xTs = [np.ascontiguousarray(x[b].T) for b in range(B)]
    cTs = [pl(f(c)[b], 16) for b in range(B)]
    maps = []
    for i in cores:
        b, q = divmod(i, 4)
        pos = np.arange(q * T - HALO, (q + 1) * T)
        cco, sso = rope_tables(np.maximum(pos, 0))
        selv = np.zeros((128, 4), np.float32)
        selv[:, q] = 1.0
        tbv = np.full((128, 3 * (T // 128)), NEG, np.float32)
        tbv[:, :max(0, q * (T // 128) - 1)] = 0.0
        m = dict(shared)
        m.update({"xT": xTs[b], "cT": cTs[b], "CCo": cco, "SSo": sso, "sel": selv,
                  "flag": np.full((128, 1), 0.0 if q == 0 else 1.0, np.float32),
                  "hb": np.full((128, 1), NEG if q == 0 else 0.0, np.float32), "tb": tbv})
        maps.append(m)
    nc = build_fused(S, T, L)
    res = run_bass_kernel_spmd(nc, maps, core_ids=cores)
    out = np.empty((B, S, D), np.float32)
    for i in cores:
        b, q = divmod(i, 4)
        out[b, q * T:(q + 1) * T] = np.asarray(res.results[i]["xo"]).T
    return out
```

```python
import numpy as np
import ml_dtypes
import concourse.bass as bass
import concourse.mybir as mybir
from concourse.bass_utils import run_bass_kernel_spmd

F32 = mybir.dt.float32
BF16 = mybir.dt.bfloat16
AF = mybir.ActivationFunctionType
ALU = mybir.AluOpType

D = 2048
NCH = 16
DFF = 5632
NFF = 44
HALO = 128
ALPHA = (2.0 * 2) ** 0.25
LN_EPS = 1e-5
RMS_EPS = 1e-6
ATT_SCALE = 192.0 ** -0.5
NEG = -30000.0

ENGS = ("pe", "act", "dve", "pool", "sp")
EPOCH = 30000
NDMA = {"sp": 40, "pool": 24, "act": 1}


class Op:
    __slots__ = ("eng", "fn", "deps", "dma", "idx", "signal", "sem", "val")

    def __init__(self, eng, fn, deps, dma, idx):
        self.eng = eng
        self.fn = fn
        self.deps = deps
        self.dma = dma
        self.idx = idx
        self.signal = False
        self.sem = None
        self.val = 0


class Prog:
    def __init__(self, nc):
        self.nc = nc
        self.ops = []
        self.lastw = {}
        self.readers = {}
        self.rot = {}

    def op(self, eng, fn, reads=(), writes=(), dma=False):
        if getattr(self, "deferring", False):
            self.defer(lambda: self._op(eng, fn, reads, writes, dma))
            return None
        return self._op(eng, fn, reads, writes, dma)

    def _op(self, eng, fn, reads=(), writes=(), dma=False):
        extra = [b for b in reads if isinstance(b, tuple) and b[0] == "ps" and b not in writes]
        if extra and eng != "pe":
            writes = list(writes) + extra
        deps = set()
        for b in reads:
            w = self.lastw.get(b)
            if w is not None:
                deps.add(w)
        for b in writes:
            w = self.lastw.get(b)
            if w is not None:
                deps.add(w)
            for r in self.readers.get(b, ()):
                deps.add(r)
        idx = len(self.ops)
        self.ops.append(Op(eng, fn, deps, dma, idx))
        for b in reads:
            self.readers.setdefault(b, []).append(idx)
        for b in writes:
            self.lastw[b] = idx
            self.readers[b] = []
        return idx

    def dma(self, eng, out, in_, reads, writes):
        self.op(eng, lambda e, o=out, i=in_: e.dma_start(out=o, in_=i), reads, writes, dma=True)

    def mm(self, out, lhsT, rhs, start, stop, reads, writes):
        self.op("pe", lambda e, o=out, l=lhsT, r=rhs, s=start, t=stop: e.matmul(o, lhsT=l, rhs=r, start=s, stop=t),
                reads, writes)

    def act(self, out, in_, func, reads, writes, bias=None, scale=None):
        kw = {}
        if bias is not None:
            kw["bias"] = bias
        if scale is not None:
            kw["scale"] = scale
        self.op("act", lambda e, o=out, i=in_, f=func, k=kw: e.activation(out=o, in_=i, func=f, **k), reads, writes)

    def tt(self, eng, out, in0, in1, op, reads, writes):
        self.op(eng, lambda e, o=out, a=in0, b=in1, p=op: e.tensor_tensor(out=o, in0=a, in1=b, op=p), reads, writes)

    def ts(self, eng, out, in0, s1, s2, op0, op1, reads, writes):
        if s2 is None:
            self.op(eng, lambda e, o=out, a=in0, x=s1, p=op0: e.tensor_scalar(out=o, in0=a, scalar1=x, scalar2=None, op0=p),
                    reads, writes)
        else:
            self.op(eng, lambda e, o=out, a=in0, x=s1, y=s2, p=op0, q=op1: e.tensor_scalar(
                out=o, in0=a, scalar1=x, scalar2=y, op0=p, op1=q), reads, writes)

    def stt(self, eng, out, in0, scalar, in1, op0, op1, reads, writes):
        self.op(eng, lambda e, o=out, a=in0, s=scalar, b=in1, p=op0, q=op1: e.scalar_tensor_tensor(
            out=o, in0=a, scalar=s, in1=b, op0=p, op1=q), reads, writes)

    def cp(self, eng, out, in_, reads, writes):
        self.op(eng, lambda e, o=out, i=in_: e.tensor_copy(out=o, in_=i), reads, writes)

    def ms(self, eng, ap, val, writes):
        self.op(eng, lambda e, a=ap, v=val: e.memset(a, v), (), writes)

    def rcp(self, out, in_, reads, writes):
        self.op("dve", lambda e, o=out, i=in_: e.reciprocal(out=o, in_=i), reads, writes)

    def defer(self, thunk):
        if not hasattr(self, "deferred"):
            self.deferred = []
        self.deferred.append(thunk)

    def flush(self, n=None):
        q = getattr(self, "deferred", [])
        k = len(q) if n is None else min(n, len(q))
        for _ in range(k):
            q.pop(0)()

    def bg_flush(self, n=None):
        q = getattr(self, "bgq", [])
        k = len(q) if n is None else min(n, len(q))
        for _ in range(k):
            q.pop(0)()

    def nxt(self, name, n):
        k = self.rot.get(name, 0)
        self.rot[name] = k + 1
        return k % n

    def emit(self):
        nc = self.nc
        ops = self.ops
        for o in ops:
            keep = set()
            for d in o.deps:
                po = ops[d]
                if po.eng == "pe" and o.eng == "pe" and not po.dma and not o.dma:
                    continue
                keep.add(d)
            o.deps = keep
            for d in keep:
                ops[d].signal = True
        ops[-1].signal = True
        for o in ops:
            if o.dma:
                o.signal = True
        sems = {}
        for e in ENGS:
            n = sum(1 for o in ops if o.eng == e and not o.dma and o.signal)
            sems[e] = [nc.alloc_semaphore(name=f"s_{e}_{i}") for i in range(max(1, -(-n // EPOCH)))]
        dsems = {e: [nc.alloc_semaphore(name=f"d_{e}_{i}") for i in range(NDMA[e])] for e in NDMA}
        cnt = {e: 0 for e in ENGS}
        dcnt = {e: 0 for e in NDMA}
        for o in ops:
            if not o.signal:
                continue
            if o.dma:
                k = dcnt[o.eng]
                dcnt[o.eng] += 1
                o.sem = dsems[o.eng][k % NDMA[o.eng]]
                o.val = 16 * (k // NDMA[o.eng] + 1)
            else:
                k = cnt[o.eng]
                cnt[o.eng] += 1
                o.sem = sems[o.eng][k // EPOCH]
                o.val = (k % EPOCH) + 1
        per_eng = {e: [o for o in ops if o.eng == e] for e in ENGS}
        final = ops[-1]

        def run_engine(e, eng):
            waited = {}
            for o in per_eng[e]:
                need = {}
                for d in o.deps:
                    po = ops[d]
                    key = id(po.sem)
                    if need.get(key, (None, 0))[1] < po.val:
                        need[key] = (po.sem, po.val)
                for key, (sem, val) in need.items():
                    if waited.get(key, 0) >= val:
                        continue
                    eng.wait_ge(sem, val)
                    waited[key] = val
                ins = o.fn(eng)
                if o.signal:
                    ins.then_inc(o.sem, 16 if o.dma else 1)
            if e == final.eng:
                eng.wait_ge(final.sem, final.val)

        with nc.Block() as block:
            @block.tensor
            def _(eng):
                run_engine("pe", eng)

            @block.scalar
            def _(eng):
                run_engine("act", eng)

            @block.vector
            def _(eng):
                run_engine("dve", eng)

            @block.gpsimd
            def _(eng):
                run_engine("pool", eng)

            @block.sync
            def _(eng):
                run_engine("sp", eng)


class Ctx:
    def __init__(self, nc):
        self.nc = nc
        self.P = Prog(nc)
        self.ps = [nc.alloc_psum_tensor(f"ps{i}", [128, 512], F32) for i in range(8)]
        self.ones = nc.alloc_sbuf_tensor("ones", [128, 128], BF16)
        self.eps_ln = nc.alloc_sbuf_tensor("eps_ln", [128, 1], F32)
        self.eps_rms = nc.alloc_sbuf_tensor("eps_rms", [128, 1], F32)
        self.P.ms("pool", self.ones[:], 1.0, ["ones"])
        self.P.ms("pool", self.eps_ln[:], LN_EPS, ["eps"])
        self.P.ms("pool", self.eps_rms[:], RMS_EPS, ["eps"])
        self.lnb = nc.alloc_sbuf_tensor("lnb", [128, 4, 512], BF16)
        self.ln_m = nc.alloc_sbuf_tensor("ln_m", [128, 512], F32)
        self.ln_t = nc.alloc_sbuf_tensor("ln_t", [128, 512], F32)
        self.ln_r = nc.alloc_sbuf_tensor("ln_r", [128, 512], F32)
        self.ln_x = nc.alloc_sbuf_tensor("ln_x", [128, 2, 512], F32)
        self.cst = nc.alloc_sbuf_tensor("cst", [128, 2, 512], F32)
        self.cstb = nc.alloc_sbuf_tensor("cstb", [128, 2, 512], BF16)

    def psn(self, lo=0, hi=8):
        return lo + self.P.nxt(("ps", lo, hi), hi - lo)

    def load(self, name, shape, src, dtype=F32, eng="sp"):
        t = self.nc.alloc_sbuf_tensor(name, shape, dtype)
        self.P.dma(eng, t[:], src, [], [name])
        return t

    def cast_weights(self, dst2d, src2d, rows, cols):
        P = self.P
        for r0 in range(0, rows, 128):
            for c0 in range(0, cols, 512):
                w = min(512, cols - c0)
                k = P.nxt("cst", 2)
                P.dma("sp", self.cst[:, k, :w], src2d[r0:r0 + 128, c0:c0 + w], [], [("cst", k)])
                P.cp("pool", self.cstb[:, k, :w], self.cst[:, k, :w], [("cst", k)], [("cstb", k)])
                P.dma("pool", dst2d[r0:r0 + 128, c0:c0 + w], self.cstb[:, k, :w], [("cstb", k)], [("wdram", id(dst2d))])

    def ln_stream(self, xview, xkey, t0, nch, G, nfeat, eps_tile, scale_fn, bias_fn, dst, dkeys, func=AF.Identity, banks=None):
        P = self.P
        if banks is None:
            s1 = self.psn(0, 4)
            s2 = self.psn(0, 4)
        else:
            s1, s2 = banks
        ps1 = self.ps[s1][:, :G]
        ps2 = self.ps[s2][:, :G]
        for c in range(nch):
            kx = P.nxt("ln_x", 2)
            x = self.ln_x[:, kx, :G]
            P.dma("pool", x, xview[:, c, t0:t0 + G], xkey, [("ln_x", kx)])
            k = P.nxt("lnb", 2)
            P.cp("dve", self.lnb[:, 2 * k, :G], x, [("ln_x", kx)], [("lnb", 2 * k)])
            P.act(self.lnb[:, 2 * k + 1, :G], x, AF.Square, [("ln_x", kx)], [("lnb", 2 * k + 1)])
            P.mm(ps1, self.ones[:], self.lnb[:, 2 * k, :G], c == 0, c == nch - 1, ["ones", ("lnb", 2 * k)], [("ps", s1)])
            P.mm(ps2, self.ones[:], self.lnb[:, 2 * k + 1, :G], c == 0, c == nch - 1, ["ones", ("lnb", 2 * k + 1)], [("ps", s2)])
        m = self.ln_m[:, :G]
        t = self.ln_t[:, :G]
        r = self.ln_r[:, :G]
        P.act(m, ps1, AF.Copy, [("ps", s1)], ["ln_m"], scale=1.0 / nfeat)
        P.tt("dve", t, m, m, ALU.mult, ["ln_m"], ["ln_t"])
        P.stt("dve", t, ps2, 1.0 / nfeat, t, ALU.mult, ALU.subtract, [("ps", s2), "ln_t"], ["ln_t"])
        P.act(t, t, AF.Sqrt, ["ln_t", "eps"], ["ln_t"], bias=eps_tile[:], scale=1.0)
        P.rcp(r, t, ["ln_t"], ["ln_r"])
        for c in range(nch):
            kx = P.nxt("ln_x", 2)
            x = self.ln_x[:, kx, :G]
            P.dma("pool", x, xview[:, c, t0:t0 + G], xkey, [("ln_x", kx)])
            P.tt("dve", x, x, m, ALU.subtract, [("ln_x", kx), "ln_m"], [("ln_x", kx)])
            P.tt("dve", x, x, r, ALU.mult, [("ln_x", kx), "ln_r"], [("ln_x", kx)])
            P.act(dst(c), x, func, [("ln_x", kx), "consts"], [dkeys(c)], bias=bias_fn(c), scale=scale_fn(c))

    def cast_weights_bg(self, dst2d, src2d, rows, cols):
        P = self.P
        if not hasattr(P, "bgq"):
            P.bgq = []
        steps = [(r0, c0, min(512, cols - c0)) for r0 in range(0, rows, 128) for c0 in range(0, cols, 512)]
        key = ("wdram", id(dst2d))

        def t_in(i):
            r0, c0, w = steps[i]
            k = i % 2
            P.dma("pool", self.cst[:, k, :w], src2d[r0:r0 + 128, c0:c0 + w], [], [("cst", k)])

        def t_unit(i):
            r0, c0, w = steps[i]
            k = i % 2
            P.cp("pool", self.cstb[:, k, :w], self.cst[:, k, :w], [("cst", k)], [("cstb", k)])
            P.dma("pool", dst2d[r0:r0 + 128, c0:c0 + w], self.cstb[:, k, :w], [("cstb", k)], [key])
            if i + 2 < len(steps):
                t_in(i + 2)

        P.bgq.append(lambda: t_in(0))
        if len(steps) > 1:
            P.bgq.append(lambda: t_in(1))
        for i in range(len(steps)):
            P.bgq.append(lambda i=i: t_unit(i))

    def ln_fm(self, src, skeys, nch, G, nfeat, eps_tile, scale_fn, bias_fn, dst, dkeys, func=AF.Identity):
        P = self.P
        s1 = self.psn(0, 4)
        s2 = self.psn(0, 4)
        ps1 = self.ps[s1][:, :G]
        ps2 = self.ps[s2][:, :G]
        for c in range(nch):
            k = P.nxt("lnb", 2)
            P.cp("dve", self.lnb[:, 2 * k, :G], src(c), [skeys(c)], [("lnb", 2 * k)])
            P.act(self.lnb[:, 2 * k + 1, :G], src(c), AF.Square, [skeys(c)], [("lnb", 2 * k + 1)])
            P.mm(ps1, self.ones[:], self.lnb[:, 2 * k, :G], c == 0, c == nch - 1, ["ones", ("lnb", 2 * k)], [("ps", s1)])
            P.mm(ps2, self.ones[:], self.lnb[:, 2 * k + 1, :G], c == 0, c == nch - 1, ["ones", ("lnb", 2 * k + 1)], [("ps", s2)])
        m = self.ln_m[:, :G]
        t = self.ln_t[:, :G]
        r = self.ln_r[:, :G]
        P.act(m, ps1, AF.Copy, [("ps", s1)], ["ln_m"], scale=1.0 / nfeat)
        P.tt("dve", t, m, m, ALU.mult, ["ln_m"], ["ln_t"])
        P.stt("dve", t, ps2, 1.0 / nfeat, t, ALU.mult, ALU.subtract, [("ps", s2), "ln_t"], ["ln_t"])
        P.act(t, t, AF.Sqrt, ["ln_t", "eps"], ["ln_t"], bias=eps_tile[:], scale=1.0)
        P.rcp(r, t, ["ln_t"], ["ln_r"])
        for c in range(nch):
            k = P.nxt("ln_x", 2)
            x = self.ln_x[:, k, :G]
            P.tt("dve", x, src(c), m, ALU.subtract, [skeys(c), "ln_m"], [("ln_x", k)])
            P.tt("dve", x, x, r, ALU.mult, [("ln_x", k), "ln_r"], [("ln_x", k)])
            P.act(dst(c), x, func, [("ln_x", k), "consts"], [dkeys(c)], bias=bias_fn(c), scale=scale_fn(c))


def build_mod():
    nc = bass.Bass("TRN2", target_bir_lowering=False)
    cT = nc.dram_tensor("cT", [128, 32], F32, kind="ExternalInput").ap()
    wm = nc.dram_tensor("wm", [24, 128, 2048], F32, kind="ExternalInput").ap()
    bm = nc.dram_tensor("bm", [128, 24], F32, kind="ExternalInput").ap()
    mo = nc.dram_tensor("mo", [128, 48], F32, kind="ExternalOutput").ap()
    P = Prog(nc)
    ca = nc.alloc_sbuf_tensor("ca", [128, 32], F32)
    bmt = nc.alloc_sbuf_tensor("bmt", [128, 24], F32)
    res = nc.alloc_sbuf_tensor("res", [128, 48], F32)
    wb = nc.alloc_sbuf_tensor("wb", [128, 2, 2048], F32)
    ps = nc.alloc_psum_tensor("ps", [128, 48], F32)
    P.dma("sp", ca[:], cT, [], ["ca"])
    P.dma("sp", bmt[:], bm, [], ["bmt"])
    P.act(ca[:], ca[:], AF.Silu, ["ca"], ["ca"])
    for j in range(24):
        k = j % 2
        P.dma("sp", wb[:, k, :], wm[j], [], [("wb", k)])
        for kc in range(16):
            P.mm(ps[:, 2 * j:2 * j + 2], wb[:, k, kc * 128:(kc + 1) * 128], ca[:, 2 * kc:2 * kc + 2],
                 kc == 0, kc == 15, [("wb", k), "ca"], ["ps"])
    for b in range(2):
        P.tt("dve", res[:, b::2], ps[:, b::2], bmt[:], ALU.add, ["ps", "bmt"], ["res"])
    P.dma("sp", mo, res[:], ["res"], ["mo"])
    P.dma("sp", mo[0:1, 0:1], res[0:1, 0:1], ["mo"], ["mo"])
    P.emit()
    return nc


def build_ffn(T):
    nc = bass.Bass("TRN2", target_bir_lowering=False)
    dt = nc.dram_tensor
    xm = dt("xm", [D, HALO + T], F32, kind="ExternalInput").ap()
    mod = dt("mod", [128, 96], F32, kind="ExternalInput").ap()
    wup = dt("wup", [88 * 128, 2048], F32, kind="ExternalInput").ap()
    wdn = dt("wdn", [16 * 128, 5632], F32, kind="ExternalInput").ap()
    wc = dt("wc", [128, 3 * NFF], F32, kind="ExternalInput").ap()
    bc = dt("bc", [128, NFF], F32, kind="ExternalInput").ap()
    pg = dt("pg", [128, 16], F32, kind="ExternalInput").ap()
    pb = dt("pb", [128, 16], F32, kind="ExternalInput").ap()
    flag = dt("flag", [128, 1], F32, kind="ExternalInput").ap()
    xo = dt("xo", [D, T], F32, kind="ExternalOutput").ap()
    wupb = dt("wupb", [88 * 128, 2048], BF16).ap()
    wdnb = dt("wdnb", [16 * 128, 5632], BF16).ap()

    C = Ctx(nc)
    P = C.P
    modt = C.load("modt", [128, 96], mod)
    wct = C.load("wct", [128, 3 * NFF], wc)
    bct = C.load("bct", [128, NFF], bc)
    pgt = C.load("pgt", [128, 16], pg)
    pbt = C.load("pbt", [128, 16], pb)
    flt = C.load("flt", [128, 1], flag)
    mod1 = nc.alloc_sbuf_tensor("mod1", [128, 96], F32)
    P.ts("dve", mod1[:], modt[:], 1.0, None, ALU.add, None, ["modt"], ["consts"])
    C.cast_weights(wupb, wup, 88 * 128, 2048)
    C.cast_weights(wdnb, wdn, 16 * 128, 5632)

    big = nc.alloc_sbuf_tensor("big", [128, NCH, 512], F32)
    hT = nc.alloc_sbuf_tensor("hT", [128, NCH, 512], BF16)
    ffT = nc.alloc_sbuf_tensor("ffT", [128, NFF, 512], BF16)
    wbuf = nc.alloc_sbuf_tensor("wbuf", [128, 4, 2048], BF16)
    wdbuf = nc.alloc_sbuf_tensor("wdbuf", [128, 2, 5632], BF16)
    gbuf = nc.alloc_sbuf_tensor("gbuf", [128, 2, 516], F32)
    acc = nc.alloc_sbuf_tensor("acc", [128, 2, 512], F32)
    carry = nc.alloc_sbuf_tensor("carry", [128, NFF, 2], F32)
    xres = nc.alloc_sbuf_tensor("xres", [128, 2, 512], F32)
    obuf = nc.alloc_sbuf_tensor("obuf", [128, 2, 512], F32)
    xm_v = xm.rearrange("(c p) t -> p c t", p=128)
    xo_v = xo.rearrange("(c p) t -> p c t", p=128)

    def group(t0, G, halo, o0):
        P.dma("sp", big[:, :, :G], xm_v[:, :, t0:t0 + G], [], ["big"])
        C.ln_fm(lambda c: big[:, c, :G], lambda c: "big", NCH, G, D, C.eps_ln,
                lambda c: mod1[:, 64 + c:65 + c], lambda c: modt[:, 48 + c:49 + c],
                lambda c: hT[:, c, :G], lambda c: "hT")
        for j in range(NFF):
            blocks = [j] if halo else [j, NFF + j]
            pss = []
            for blk in blocks:
                k = P.nxt("wbuf", 4)
                P.dma("sp", wbuf[:, k, :], wupb[blk * 128:(blk + 1) * 128, :], [("wdram", id(wupb))], [("wbuf", k)])
                s = C.psn(4, 8)
                for c in range(NCH):
                    P.mm(C.ps[s][:, :G], wbuf[:, k, c * 128:(c + 1) * 128], hT[:, c, :G], c == 0, c == NCH - 1,
                         [("wbuf", k), "hT"], [("ps", s)])
                pss.append(s)
            sa = pss[0]
            if halo:
                P.ts("dve", carry[:, j, :], C.ps[sa][:, G - 2:G], flt[:, 0:1], None, ALU.mult, None,
                     [("ps", sa), "flt"], [("carry", j)])
                continue
            sb_ = pss[1]
            k = P.nxt("gbuf", 2)
            gb = gbuf[:, k, :]
            P.cp("pool", gb[:, 0:2], carry[:, j, :], [("carry", j)], [("gbuf", k)])
            P.act(gb[:, 2:2 + G], C.ps[sa][:, :G], AF.Copy, [("ps", sa)], [("gbuf", k)])
            a = acc[:, k, :G]
            P.ts("dve", a, gb[:, 2:2 + G], wct[:, 2 * NFF + j:2 * NFF + j + 1], bct[:, j:j + 1], ALU.mult, ALU.add,
                 [("gbuf", k), "wct", "bct"], [("acc", k)])
            P.stt("dve", a, gb[:, 1:1 + G], wct[:, NFF + j:NFF + j + 1], a, ALU.mult, ALU.add, [("gbuf", k), ("acc", k)], [("acc", k)])
            P.stt("dve", a, gb[:, 0:G], wct[:, j:j + 1], a, ALU.mult, ALU.add, [("gbuf", k), ("acc", k)], [("acc", k)])
            P.cp("pool", carry[:, j, :], gb[:, G:G + 2], [("gbuf", k)], [("carry", j)])
            P.act(a, a, AF.Silu, [("acc", k)], [("acc", k)])
            P.tt("dve", ffT[:, j, :G], a, C.ps[sb_][:, :G], ALU.mult, [("acc", k), ("ps", sb_)], [("ffT", j)])
        if halo:
            return
        for ob in range(NCH):
            k = P.nxt("wdbuf", 2)
            P.dma("sp", wdbuf[:, k, :], wdnb[ob * 128:(ob + 1) * 128, :], [("wdram", id(wdnb))], [("wdbuf", k)])
            kx = P.nxt("xres", 2)
            P.dma("pool", xres[:, kx, :G], xm_v[:, ob, t0:t0 + G], [], [("xres", kx)])
            s = C.psn(4, 8)
            for c in range(NFF):
                P.mm(C.ps[s][:, :G], wdbuf[:, k, c * 128:(c + 1) * 128], ffT[:, c, :G], c == 0, c == NFF - 1,
                     [("wdbuf", k), ("ffT", c)], [("ps", s)])
            P.act(big[:, ob, :G], C.ps[s][:, :G], AF.Identity, [("ps", s), "consts"], ["big"], scale=mod1[:, 80 + ob:81 + ob], bias=0.0)
            P.stt("dve", big[:, ob, :G], xres[:, kx, :G], ALPHA, big[:, ob, :G], ALU.mult, ALU.add, [("xres", kx), "big"], ["big"])

        def dst(c):
            return obuf[:, c % 2, :G]

        C.ln_fm(lambda c: big[:, c, :G], lambda c: "big", NCH, G, D, C.eps_ln,
                lambda c: pgt[:, c:c + 1], lambda c: pbt[:, c:c + 1],
                dst, lambda c: ("obuf", c % 2))

    orig_act = P.act

    state = {"store": None}

    def act_hook(out, in_, func, reads, writes, bias=None, scale=None):
        orig_act(out, in_, func, reads, writes, bias=bias, scale=scale)
        st = state["store"]
        if st is not None and writes and isinstance(writes[0], tuple) and writes[0][0] == "obuf":
            st(out, writes[0])

    P.act = act_hook

    def make_store(o0, G):
        cnt = {"c": 0}

        def st(out_ap, key):
            c = cnt["c"]
            cnt["c"] += 1
            P.dma("pool", xo_v[:, c, o0:o0 + G], out_ap, [key], [("xo", c, o0)])
        return st

    group(0, HALO, True, 0)
    for g in range(T // 512):
        state["store"] = make_store(g * 512, 512)
        group(HALO + g * 512, 512, False, g * 512)
        state["store"] = None
    outs = [("xo", c, g * 512) for c in range(NCH) for g in range(T // 512)]
    P.dma("sp", wupb[0:1, 0:8], wupb[0:1, 8:16], outs, ["fin"])
    P.emit()
    return nc


def pl(v, n):
    return np.ascontiguousarray(np.asarray(v, np.float32).reshape(n, 128).T)


def mod_layout(modv):
    return np.ascontiguousarray(np.concatenate([pl(modv[j], 16) for j in range(6)], axis=1))


def tile_w(w, nk, cols_list):
    out = []
    for cols in cols_list:
        blk = w[:, cols].reshape(nk, 128, len(cols)).transpose(1, 0, 2).reshape(128, nk * len(cols))
        out.append(blk)
    return np.ascontiguousarray(np.concatenate(out, axis=0), dtype=np.float32)


def ffn_inputs(xmT_ext, modv, wup, wdn, wconv, bconv, pg, pb, flag):
    return {
        "xm": np.ascontiguousarray(xmT_ext, dtype=np.float32),
        "mod": mod_layout(modv),
        "wup": tile_w(wup, 16, [np.arange(b * 128, (b + 1) * 128) for b in range(88)]),
        "wdn": tile_w(wdn, 44, [np.arange(b * 128, (b + 1) * 128) for b in range(16)]),
        "wc": np.ascontiguousarray(np.concatenate([pl(wconv[k], 44) for k in range(3)], axis=1)),
        "bc": pl(bconv, 44),
        "pg": pl(pg, 16),
        "pb": pl(pb, 16),
        "flag": np.full((128, 1), flag, np.float32),
    }


def build_mix(T, kv_only):
    nc = bass.Bass("TRN2", target_bir_lowering=False)
    NT = T // 128
    NPREV = 3 * NT

    def din(name, shape, dtype=F32):
        return nc.dram_tensor(name, shape, dtype, kind="ExternalInput").ap()

    x = din("x", [D, HALO + T])
    mod = din("mod", [128, 96])
    winF = din("winF", [29 * 128, 2048])
    winkr = din("winkr", [2 * 128, 1024])
    wukvk = din("wukvk", [4 * 128, 256])
    wukvv = din("wukvv", [128, 1024])
    gkv = din("gkv", [128, 2])
    CCd = din("CC", [64, T])
    SSd = din("SS", [64, T])
    okind = "ExternalOutput" if kv_only else "Internal"
    KNo = nc.dram_tensor("KNo", [NT * 128, 512], BF16, kind=okind).ap()
    KRo = nc.dram_tensor("KRo", [NT * 64, 128], BF16, kind=okind).ap()
    Vo = nc.dram_tensor("Vo", [NT * 128, 512], BF16, kind=okind).ap()
    winFb = nc.dram_tensor("winFb", [29 * 128, 2048], BF16).ap()
    winkrb = nc.dram_tensor("winkrb", [2 * 128, 1024], BF16).ap()
    wukvkb = nc.dram_tensor("wukvkb", [4 * 128, 256], BF16).ap()
    wukvvb = nc.dram_tensor("wukvvb", [128, 1024], BF16).ap()
    if not kv_only:
        winv = din("winv", [128, 8192])
        wuqn = din("wuqn", [4 * 128, 384])
        wuqr = din("wuqr", [8 * 128, 192])
        gq = din("gq", [128, 3])
        wsT = din("wsT", [128, 512])
        bsrow = din("bsrow", [1, 512])
        lng = din("lng", [128, 4])
        lnbrow = din("lnbrow", [1, 512])
        scw = din("scw", [128, 12])
        cfw = din("cfw", [128, 124])
        cfb = din("cfb", [128, 4])
        cfg = din("cfg", [128, 4])
        cfbt = din("cfbt", [128, 4])
        wout = din("wout", [16 * 128, 2048])
        pmg = din("pmg", [128, 16])
        pmb = din("pmb", [128, 16])
        flag = din("flag", [128, 1])
        sbias = din("sbias", [128, 3])
        KNp = din("KNp", [NPREV * 128, 512], BF16)
        KRp = din("KRp", [NPREV * 64, 128], BF16)
        Vp = din("Vp", [NPREV * 128, 512], BF16)
        xmid = nc.dram_tensor("xmid", [D, T], F32, kind="ExternalOutput").ap()
        winvb = nc.dram_tensor("winvb", [128, 8192], BF16).ap()
        wuqnb = nc.dram_tensor("wuqnb", [4 * 128, 384], BF16).ap()
        wuqrb = nc.dram_tensor("wuqrb", [8 * 128, 192], BF16).ap()
        woutb = nc.dram_tensor("woutb", [16 * 128, 2048], BF16).ap()

    C = Ctx(nc)
    P = C.P
    al = nc.alloc_sbuf_tensor
    modt = C.load("modt", [128, 96], mod)
    gkvt = C.load("gkvt", [128, 2], gkv)
    mod1 = al("mod1", [128, 96], F32)
    P.ts("dve", mod1[:], modt[:], 1.0, None, ALU.add, None, ["modt"], ["consts"])
    C.cast_weights(winFb, winF, 29 * 128, 2048)
    C.cast_weights(winkrb, winkr, 2 * 128, 1024)
    C.cast_weights(wukvkb, wukvk, 4 * 128, 256)
    C.cast_weights(wukvvb, wukvv, 128, 1024)
    wvv = al("wvv", [128, 2, 512], BF16)
    P.dma("sp", wvv[:], wukvvb.rearrange("p (c n) -> p c n", c=2), [("wdram", id(wukvvb))], ["wvv"])

    big = al("big", [128, NCH, 512], F32)
    hT = al("hT", [128, NCH, 512], BF16)
    wbuf = al("wbuf", [128, 3, 2048], BF16)
    sqkv = al("sqkv", [128, 2, 512], BF16)
    kvcg = al("kvcg", [128, 2, 512], BF16)
    rkv = al("rkv", [128, 512], F32)
    kst = al("kst", [128, 4, 512], BF16)
    krs_full = al("krs", [128, 512], BF16)
    r1_full = al("r1", [128, 512], F32)
    r2_full = al("r2", [128, 512], F32)
    cct_full = al("cct", [128, 512], F32)
    sst_full = al("sst", [128, 512], F32)
    krs, r1, r2, cct, sst = (t_[0:64] for t_ in (krs_full, r1_full, r2_full, cct_full, sst_full))
    rt = al("rt", [128, 2, 1], F32)
    vst = al("vst", [128, 2, 512], BF16)
    x_v = x.rearrange("(c p) t -> p c t", p=128)

    if not kv_only:
        consts = {}
        for nm, ap_, shp in (("gqt", gq, [128, 3]), ("lngt", lng, [128, 4]), ("scwt", scw, [128, 12]),
                             ("cfwt", cfw, [128, 124]), ("cfbt_", cfb, [128, 4]), ("cfgt", cfg, [128, 4]),
                             ("cfbtt", cfbt, [128, 4]), ("pmgt", pmg, [128, 16]), ("pmbt", pmb, [128, 16]),
                             ("flt", flag, [128, 1]), ("sbt", sbias, [128, 3]), ("wsTf", wsT, [128, 512]),
                             ("bsr", bsrow, [1, 512]), ("lnbr", lnbrow, [1, 512])):
            consts[nm] = C.load(nm, shp, ap_)
        gqt, lngt, scwt, cfwt, cfbt_, cfgt, cfbtt = (consts[k] for k in ("gqt", "lngt", "scwt", "cfwt", "cfbt_", "cfgt", "cfbtt"))
        pmgt, pmbt, flt, sbt, wsTf, bsr, lnbr = (consts[k] for k in ("pmgt", "pmbt", "flt", "sbt", "wsTf", "bsr", "lnbr"))
        C.cast_weights(winvb, winv, 128, 8192)
        C.cast_weights(wuqnb, wuqn, 4 * 128, 384)
        C.cast_weights(wuqrb, wuqr, 8 * 128, 192)
        C.cast_weights(woutb, wout, 16 * 128, 2048)
        P.ms("pool", wsTf[64:128, :].rearrange("p (h i) -> p h i", h=4)[:, :, 0:64], 0.0, ["wsTf"])
        wsTb = al("wsTb", [128, 512], BF16)
        P.cp("dve", wsTb[:], wsTf[:], ["wsTf"], ["wsTb"])
        onesf = al("onesf", [128, 128], F32)
        P.ms("pool", onesf[:], 1.0, ["onesf"])
        rsw = al("rsw", [1, 512], F32)
        s = C.psn(4, 8)
        P.mm(C.ps[s][0:1, :], onesf[:, 0:1], wsTf[:], True, True, ["onesf", "wsTf"], [("ps", s)])
        P.cp("dve", rsw[:], C.ps[s][0:1, :], [("ps", s)], ["rsw"])
        Bm = al("Bm", [128, 512], F32)
        s = C.psn(4, 8)
        for h in range(4):
            sl = slice(h * 128, (h + 1) * 128)
            P.mm(C.ps[s][:, sl], lnbr[0:1, sl], rsw[0:1, sl], True, False, ["lnbr", "rsw"], [("ps", s)])
            P.mm(C.ps[s][:, sl], onesf[0:1, :], bsr[0:1, sl], False, True, ["onesf", "bsr"], [("ps", s)])
        P.cp("dve", Bm[:], C.ps[s][:], [("ps", s)], ["Bm"])

        yT = al("yT", [128, NCH, 512], BF16)
        uT = al("uT", [128, 4, 512], BF16)
        wvb = al("wvb", [128, 2, 512], BF16)
        vf = al("vf", [128, 512], F32)
        vn = al("vn", [128, 512], BF16)
        bst = al("bst", [128, 8], F32)
        gt = al("gt", [128, 512], F32)
        sqq = al("sqq", [128, 2, 512], BF16)
        qcg = al("qcg", [128, 3, 512], BF16)
        rq = al("rq", [128, 512], F32)
        QnT = al("QnT", [128, 4, 512], BF16)
        QrT = al("QrT", [128, 4, 512], BF16)[0:64]
        sbT = al("sbT", [128, 4, 512], BF16)
        scT = al("scT", [128, 512], F32)
        zbuf = al("zbuf", [128, 516], F32)
        ybuf = al("ybuf", [128, 544], F32)
        cz = al("cz", [128, 4, 2], F32)
        cy = al("cy", [128, 4, 30], F32)
        cacc = al("cacc", [128, 512], F32)
        cT = al("cT", [128, 4, 512], F32)
        knb = al("knb", [128, 2, 4, 128], BF16)
        vpb = al("vpb", [128, 2, 4, 128], BF16)
        krb = al("krb", [128, 2, 4, 128], BF16)[0:64]
        pT = al("pT", [128, 3, 512], BF16)
        rl = al("rl", [128, 512], F32)
        xres = al("xres", [128, 2, 512], F32)
        obuf = al("obuf", [128, 2, 512], F32)
        xmid_v = xmid.rearrange("(c p) t -> p c t", p=128)

    def wload(dram2d, blk, width):
        k = P.nxt("wbuf", 3)
        P.dma("sp", wbuf[:, k, :width], dram2d[blk * 128:(blk + 1) * 128, :], [("wdram", id(dram2d))], [("wbuf", k)])
        return k

    def proj(dram2d, blk, M, nk, rhs, rkeys, G):
        k = wload(dram2d, blk, nk * M)
        s = C.psn(4, 8)
        for c in range(nk):
            P.mm(C.ps[s][:M, :G], wbuf[:, k, c * M:(c + 1) * M], rhs(c), c == 0, c == nk - 1,
                 [("wbuf", k)] + rkeys, [("ps", s)])
        return s

    def group(t0, G, halo):
        o0 = t0 - HALO
        P.dma("sp", big[:, :, :G], x_v[:, :, t0:t0 + G], [], ["big"])
        C.ln_fm(lambda c: big[:, c, :G], lambda c: "big", NCH, G, D, C.eps_ln,
                lambda c: mod1[:, 16 + c:17 + c], lambda c: modt[:, c:c + 1],
                lambda c: hT[:, c, :G], lambda c: "hT")
        hrhs = lambda c: hT[:, c, :G]
        import os
        if int(os.environ.get("KSTOP", "9")) <= 0:
            return
        if not halo:
            P.dma("sp", cct[:, :G], CCd[:, o0:o0 + G], [], ["cct"])
            P.dma("sp", sst[:, :G], SSd[:, o0:o0 + G], [], ["sst"])
            KS2 = int(os.environ.get("KS2", "9"))
            if KS2 <= 1:
                return
            for j in range(2):
                s = proj(winFb, 7 + j, 128, NCH, hrhs, ["hT"], G)
                if KS2 <= 2:
                    continue
                P.act(sqkv[:, j, :G], C.ps[s][:, :G], AF.Square, [("ps", s)], [("sqkv", j)])
                P.ts("dve", kvcg[:, j, :G], C.ps[s][:, :G], gkvt[:, j:j + 1], None, ALU.mult, None,
                     [("ps", s), "gkvt"], [("kvcg", j)])
            if KS2 <= 3:
                return
            s = C.psn(4, 8)
            for j in range(2):
                P.mm(C.ps[s][:, :G], C.ones[:], sqkv[:, j, :G], j == 0, j == 1, ["ones", ("sqkv", j)], [("ps", s)])
            P.act(rkv[:, :G], C.ps[s][:, :G], AF.Sqrt, [("ps", s), "eps"], ["rkv"], bias=C.eps_rms[:], scale=1.0 / 256)
            P.rcp(rkv[:, :G], rkv[:, :G], ["rkv"], ["rkv"])
            import os
            KSTOP = int(os.environ.get("KSTOP", "9"))
            if KSTOP <= 1:
                return
            for h in range(4):
                s = proj(wukvkb, h, 128, 2, lambda c: kvcg[:, c, :G], [("kvcg", 0), ("kvcg", 1)], G)
                P.tt("dve", kst[:, h, :G], C.ps[s][:, :G], rkv[:, :G], ALU.mult, [("ps", s), "rkv"], ["kst"])
            if KSTOP <= 2:
                return
            sa = proj(winkrb, 0, 64, NCH, hrhs, ["hT"], G)
            sb_ = proj(winkrb, 1, 64, NCH, hrhs, ["hT"], G)
            P.tt("dve", r1[:, :G], C.ps[sa][:64, :G], cct[:, :G], ALU.mult, [("ps", sa), "cct"], ["r1"])
            P.tt("dve", r2[:, :G], C.ps[sb_][:64, :G], sst[:, :G], ALU.mult, [("ps", sb_), "sst"], ["r2"])
            P.tt("pool", krs[:, :G], r1[:, :G], r2[:, :G], ALU.add, ["r1", "r2"], ["krs"])
            if KSTOP <= 3:
                return
            for sub in range(G // 128):
                kt = o0 // 128 + sub
                cs = slice(sub * 128, (sub + 1) * 128)
                P.dma("pool", KNo[kt * 128:(kt + 1) * 128, :].rearrange("p (h k) -> p h k", h=4), kst[:, :, cs],
                      ["kst"], [("KNo", kt)])
                P.dma("pool", KRo[kt * 64:(kt + 1) * 64, :], krs[:, cs], ["krs"], [("KRo", kt)])
                if KSTOP <= 4:
                    continue
                s = C.psn(4, 8)
                for j in range(2):
                    P.mm(C.ps[s][:, 0:1], sqkv[:, j, cs], C.ones[:, 0:1], j == 0, j == 1, [("sqkv", j), "ones"], [("ps", s)])
                kk = P.nxt("rt", 2)
                P.act(rt[:, kk, :], C.ps[s][:, 0:1], AF.Sqrt, [("ps", s), "eps"], [("rt", kk)], bias=C.eps_rms[:], scale=1.0 / 256)
                P.rcp(rt[:, kk, :], rt[:, kk, :], [("rt", kk)], [("rt", kk)])
                s = C.psn(4, 8)
                for c in range(2):
                    P.mm(C.ps[s][:, :], kvcg[:, c, cs], wvv[:, c, :], c == 0, c == 1, [("kvcg", c), "wvv"], [("ps", s)])
                P.act(vst[:, kk, :], C.ps[s][:, :], AF.Copy, [("ps", s), ("rt", kk)], [("vst", kk)], scale=rt[:, kk, :])
                P.dma("pool", Vo[kt * 128:(kt + 1) * 128, :], vst[:, kk, :], [("vst", kk)], [("Vo", kt)])
            if kv_only:
                return
            for j in range(4):
                s = proj(winFb, j, 128, NCH, hrhs, ["hT"], G)
                P.act(uT[:, j, :G], C.ps[s][:, :G], AF.Gelu_apprx_tanh, [("ps", s)], [("uT", j)])
            nsub = G // 128
            for c in range(NCH):
                k = P.nxt("wvb", 2)
                P.dma("sp", wvb[:, k, :], winvb[:, c * 512:(c + 1) * 512], [("wdram", id(winvb))], [("wvb", k)])
                for sub in range(nsub):
                    P.mm(C.ps[sub][:, :], hT[:, c, sub * 128:(sub + 1) * 128], wvb[:, k, :], c == 0, c == NCH - 1,
                         [("wvb", k), "hT"], [("ps", sub)])
            for sub in range(nsub):
                cs = slice(sub * 128, (sub + 1) * 128)
                P.act(vf[:], C.ps[sub][:, :], AF.Gelu_apprx_tanh, [("ps", sub)], ["vf"])
                P.op("dve", lambda e: e.bn_stats(out=bst[:, 0:6], in_=vf[:]), ["vf"], ["bst"])
                P.op("dve", lambda e: e.bn_aggr(out=bst[:, 6:8], in_=bst[:, 0:6]), ["bst"], ["bst"])
                P.act(bst[:, 7:8], bst[:, 7:8], AF.Sqrt, ["bst", "eps"], ["bst"], bias=C.eps_ln[:], scale=1.0)
                P.rcp(bst[:, 7:8], bst[:, 7:8], ["bst"], ["bst"])
                P.ts("dve", vf[:], vf[:], bst[:, 6:7], bst[:, 7:8], ALU.subtract, ALU.mult, ["vf", "bst"], ["vf"])
                P.cp("pool", vn[:], vf[:], ["vf"], ["vn"])
                s = C.psn(4, 8)
                for h in range(4):
                    sl = slice(h * 128, (h + 1) * 128)
                    P.mm(C.ps[s][:, sl], vn[:, sl], wsTb[:, sl], True, True, ["vn", "wsTb"], [("ps", s)])
                for h in range(4):
                    sl = slice(h * 128, (h + 1) * 128)
                    P.stt("dve", gt[:, sl], C.ps[s][:, sl], lngt[:, h:h + 1], Bm[:, sl], ALU.mult, ALU.add,
                          [("ps", s), "lngt", "Bm"], ["gt"])
                    P.tt("dve", yT[:, h, cs], gt[:, sl], uT[:, h, cs], ALU.mult, ["gt", ("uT", h)], [("yT", h)])
            s2 = C.psn(0, 4)
            for j in range(3):
                s = proj(winFb, 4 + j, 128, NCH, hrhs, ["hT"], G)
                k = P.nxt("sqq", 2)
                P.act(sqq[:, k, :G], C.ps[s][:, :G], AF.Square, [("ps", s)], [("sqq", k)])
                P.ts("dve", qcg[:, j, :G], C.ps[s][:, :G], gqt[:, j:j + 1], None, ALU.mult, None, [("ps", s), "gqt"], [("qcg", j)])
                P.mm(C.ps[s2][:, :G], C.ones[:], sqq[:, k, :G], j == 0, j == 2, ["ones", ("sqq", k)], [("ps", s2)])
            P.act(rq[:, :G], C.ps[s2][:, :G], AF.Sqrt, [("ps", s2), "eps"], ["rq"], bias=C.eps_rms[:], scale=1.0 / 384)
            P.rcp(rq[:, :G], rq[:, :G], ["rq"], ["rq"])
            qrhs = lambda c: qcg[:, c, :G]
            qk = [("qcg", 0), ("qcg", 1), ("qcg", 2)]
            for h in range(4):
                s = proj(wuqnb, h, 128, 3, qrhs, qk, G)
                P.tt("dve", QnT[:, h, :G], C.ps[s][:, :G], rq[:, :G], ALU.mult, [("ps", s), "rq"], [("QnT", h)])
                sa = proj(wuqrb, 2 * h, 64, 3, qrhs, qk, G)
                sb_ = proj(wuqrb, 2 * h + 1, 64, 3, qrhs, qk, G)
                P.tt("dve", r1[:, :G], C.ps[sa][:64, :G], cct[:, :G], ALU.mult, [("ps", sa), "cct"], ["r1"])
                P.tt("dve", r2[:, :G], C.ps[sb_][:64, :G], sst[:, :G], ALU.mult, [("ps", sb_), "sst"], ["r2"])
                P.tt("pool", r1[:, :G], r1[:, :G], r2[:, :G], ALU.add, ["r1", "r2"], ["r1"])
                P.tt("dve", QrT[:, h, :G], r1[:, :G], rq[:64, :G], ALU.mult, ["r1", "rq"], [("QrT", h)])
        for j in range(4):
            if not halo:
                s = proj(winFb, 9 + j, 128, NCH, hrhs, ["hT"], G)
                P.act(sbT[:, j, :G], C.ps[s][:, :G], AF.Copy, [("ps", s)], [("sbT", j)])
            s = proj(winFb, 13 + j, 128, NCH, hrhs, ["hT"], G)
            P.act(scT[:, :G], C.ps[s][:, :G], AF.Copy, [("ps", s)], ["scT"])
            s = proj(winFb, 17 + j, 128, NCH, hrhs, ["hT"], G)
            if halo:
                P.tt("dve", scT[:, :G], C.ps[s][:, :G], scT[:, :G], ALU.mult, [("ps", s), "scT"], ["scT"])
                P.ts("dve", cz[:, j, :], scT[:, G - 2:G], flt[:, 0:1], None, ALU.mult, None, ["scT", "flt"], [("cz", j)])
            else:
                P.cp("pool", zbuf[:, 0:2], cz[:, j, :], [("cz", j)], ["zbuf"])
                P.tt("dve", zbuf[:, 2:2 + G], C.ps[s][:, :G], scT[:, :G], ALU.mult, [("ps", s), "scT"], ["zbuf"])
                P.cp("pool", cz[:, j, :], zbuf[:, G:G + 2], ["zbuf"], [("cz", j)])
                P.ts("dve", cacc[:, :G], zbuf[:, 2:2 + G], scwt[:, 8 + j:9 + j], None, ALU.mult, None, ["zbuf", "scwt"], ["cacc"])
                P.stt("dve", cacc[:, :G], zbuf[:, 1:1 + G], scwt[:, 4 + j:5 + j], cacc[:, :G], ALU.mult, ALU.add, ["zbuf", "cacc"], ["cacc"])
                P.stt("dve", cacc[:, :G], zbuf[:, 0:G], scwt[:, j:j + 1], cacc[:, :G], ALU.mult, ALU.add, ["zbuf", "cacc"], ["cacc"])
                P.tt("dve", yT[:, 8 + j, :G], cacc[:, :G], sbT[:, j, :G], ALU.mult, ["cacc", ("sbT", j)], [("yT", 8 + j)])
        for j in range(4):
            s = proj(winFb, 25 + j, 128, NCH, hrhs, ["hT"], G)
            P.act(scT[:, :G], C.ps[s][:, :G], AF.Sigmoid, [("ps", s)], ["scT"])
            s = proj(winFb, 21 + j, 128, NCH, hrhs, ["hT"], G)
            if halo:
                P.tt("dve", scT[:, :G], C.ps[s][:, :G], scT[:, :G], ALU.mult, [("ps", s), "scT"], ["scT"])
                P.ts("dve", cy[:, j, :], scT[:, G - 30:G], flt[:, 0:1], None, ALU.mult, None, ["scT", "flt"], [("cy", j)])
                continue
            P.cp("pool", ybuf[:, 0:30], cy[:, j, :], [("cy", j)], ["ybuf"])
            P.tt("dve", ybuf[:, 30:30 + G], C.ps[s][:, :G], scT[:, :G], ALU.mult, [("ps", s), "scT"], ["ybuf"])
            P.cp("pool", cy[:, j, :], ybuf[:, G:G + 30], ["ybuf"], [("cy", j)])
            P.ts("dve", cT[:, j, :G], ybuf[:, 30:30 + G], cfwt[:, 120 + j:121 + j], cfbt_[:, j:j + 1], ALU.mult, ALU.add,
                 ["ybuf", "cfwt", "cfbt_"], [("cT", j)])
            for k in range(30):
                P.stt("dve", cT[:, j, :G], ybuf[:, k:k + G], cfwt[:, 4 * k + j:4 * k + j + 1], cT[:, j, :G], ALU.mult, ALU.add,
                      ["ybuf", ("cT", j)], [("cT", j)])
        if halo:
            return
        C.ln_fm(lambda c: cT[:, c, :G], lambda c: ("cT", c), 4, G, 512, C.eps_ln,
                lambda c: cfgt[:, c:c + 1], lambda c: cfbtt[:, c:c + 1],
                lambda c: yT[:, 12 + c, :G], lambda c: ("yT", 12 + c), func=AF.Silu)
        g4 = o0 // 128
        tiles = [("p", kt) for kt in range(NPREV)] + [("o", kt) for kt in range(g4 + 4)]
        for h in range(4):
            first = True
            for i0 in range(0, len(tiles), 4):
                kind = tiles[i0][0]
                kt0 = tiles[i0][1]
                k = P.nxt("kvb", 2)
                srcs = (KNp, KRp, Vp) if kind == "p" else (KNo, KRo, Vo)
                if kind == "p":
                    rk = []
                else:
                    rk = None
                rdn = [] if kind == "p" else [("KNo", kt0 + i) for i in range(4)]
                rdr = [] if kind == "p" else [("KRo", kt0 + i) for i in range(4)]
                rdv = [] if kind == "p" else [("Vo", kt0 + i) for i in range(4)]
                hs = slice(h * 128, (h + 1) * 128)
                P.dma("sp", knb[:, k, :, 0:128], srcs[0][kt0 * 128:(kt0 + 4) * 128, hs].rearrange("(t p) k -> p t k", p=128),
                      rdn, [("knb", k)])
                P.dma("sp", krb[:, k, :, :], srcs[1][kt0 * 64:(kt0 + 4) * 64, :].rearrange("(t p) k -> p t k", p=64),
                      rdr, [("krb", k)])
                P.dma("sp", vpb[:, k, :, 0:128], srcs[2][kt0 * 128:(kt0 + 4) * 128, hs].rearrange("(t p) k -> p t k", p=128),
                      rdv, [("vpb", k)])
                for i in range(4):
                    kt = kt0 + i
                    if kind == "p":
                        c0 = 0
                        bias = sbt[:, kt // NT:kt // NT + 1]
                        diag = False
                    else:
                        d = kt - g4
                        c0 = max(0, d) * 128
                        bias = 0.0
                        diag = d >= 0
                    sS = C.psn(0, 2)
                    P.mm(C.ps[sS][:, c0:G], knb[:, k, i, 0:128], QnT[:, h, c0:G], True, False, [("knb", k), ("QnT", h)], [("ps", sS)])
                    P.mm(C.ps[sS][:, c0:G], krb[:, k, i, :], QrT[:, h, c0:G], False, True, [("krb", k), ("QrT", h)], [("ps", sS)])
                    kp = P.nxt("pT", 3)
                    P.act(pT[:, kp, c0:G], C.ps[sS][:, c0:G], AF.Exp, [("ps", sS), "sbt"], [("pT", kp)], bias=bias, scale=ATT_SCALE)
                    if diag:
                        P.ms("pool", pT[64:128, kp, c0:c0 + 64], 0.0, [("pT", kp)])
                    last = (i0 + i == len(tiles) - 1)
                    P.mm(C.ps[2][:, c0:G], vpb[:, k, i, 0:128], pT[:, kp, c0:G], first, last, [("vpb", k), ("pT", kp)], [("ps", 2)])
                    P.mm(C.ps[3][:, c0:G], C.ones[:], pT[:, kp, c0:G], first, last, ["ones", ("pT", kp)], [("ps", 3)])
                    first = False
            P.ts("dve", rl[:, :G], C.ps[3][:, :G], 1e-30, None, ALU.max, None, [("ps", 3)], ["rl"])
            P.rcp(rl[:, :G], rl[:, :G], ["rl"], ["rl"])
            P.tt("dve", yT[:, 4 + h, :G], C.ps[2][:, :G], rl[:, :G], ALU.mult, [("ps", 2), "rl"], [("yT", 4 + h)])
        for ob in range(NCH):
            kx = P.nxt("xres", 2)
            P.dma("pool", xres[:, kx, :G], x_v[:, ob, t0:t0 + G], [], [("xres", kx)])
            s = proj(woutb, ob, 128, NCH, lambda c: yT[:, c, :G], [("yT", c) for c in range(NCH)], G)
            P.act(big[:, ob, :G], C.ps[s][:, :G], AF.Identity, [("ps", s), "consts"], ["big"], scale=mod1[:, 32 + ob:33 + ob], bias=0.0)
            P.stt("dve", big[:, ob, :G], xres[:, kx, :G], ALPHA, big[:, ob, :G], ALU.mult, ALU.add, [("xres", kx), "big"], ["big"])
        cnt = {"c": 0}
        orig_act = P.act

        def act_hook(out, in_, func, reads, writes, bias=None, scale=None):
            orig_act(out, in_, func, reads, writes, bias=bias, scale=scale)
            if writes and isinstance(writes[0], tuple) and writes[0][0] == "obuf":
                c = cnt["c"]
                cnt["c"] += 1
                P.dma("pool", xmid_v[:, c, o0:o0 + G], out, [writes[0]], [("xmid", c, o0)])

        P.act = act_hook
        C.ln_fm(lambda c: big[:, c, :G], lambda c: "big", NCH, G, D, C.eps_ln,
                lambda c: pmgt[:, c:c + 1], lambda c: pmbt[:, c:c + 1],
                lambda c: obuf[:, c % 2, :G], lambda c: ("obuf", c % 2))
        P.act = orig_act

    if not kv_only:
        group(0, HALO, True)
    for g in range(T // 512):
        group(HALO + g * 512, 512, False)
    if kv_only:
        outs = [(n, kt) for n in ("KNo", "KRo", "Vo") for kt in range(NT)]
    else:
        outs = [("xmid", c, g * 512) for c in range(NCH) for g in range(T // 512)]
    P.dma("sp", winFb[0:1, 0:8], winFb[0:1, 8:16], outs, ["fin"])
    P.emit()
    return nc


def _bf(a):
    return np.asarray(a)


def kernel(x, c, w_mod, b_mod, w_in, gmlp_ln_g, gmlp_ln_b, gmlp_w_s, gmlp_b_s,
           mla_q_norm, mla_w_uq, mla_kv_norm, mla_w_ukv, sconv_w, conf_w_dw, conf_b_dw,
           conf_ln_g, conf_ln_b, w_out, post_mix_g, post_mix_b, ffn_w_up, ffn_w_conv,
           ffn_b_conv, ffn_w_down, post_ffn_g, post_ffn_b):
    f = lambda a: np.asarray(a, dtype=np.float32)
    x = f(x)
    B, S, _ = x.shape
    L = w_mod.shape[0]
    T = S // 4
    NT = T // 128
    cores = list(range(8))
    ar = np.arange

    c = f(c)
    cT = np.ascontiguousarray(c.T.reshape(16, 128, B).transpose(1, 0, 2).reshape(128, 16 * B))
    wcat = np.concatenate([f(w_mod[l]) for l in range(L)], axis=1)
    bcat = np.concatenate([f(b_mod[l]) for l in range(L)], axis=0)
    ncol = wcat.shape[1] // 8
    nblk = ncol // 128
    nc_mod = build_mod()
    maps = []
    for i in cores:
        cols = [ar(i * ncol + j * 128, i * ncol + (j + 1) * 128) for j in range(nblk)]
        maps.append({"cT": cT, "wm": tile_w(wcat, 16, cols).reshape(nblk, 128, 2048), "bm": pl(bcat[i * ncol:(i + 1) * ncol], nblk)})
    res = run_bass_kernel_spmd(nc_mod, maps, core_ids=cores)
    modall = np.zeros((B, wcat.shape[1]), np.float32)
    for i in cores:
        mo = np.asarray(res.results[i]["mo"]).reshape(128, nblk, B)
        for b in range(B):
            modall[b, i * ncol:(i + 1) * ncol] = mo[:, :, b].T.reshape(-1)
    del wcat, maps

    inv_freq = (10000.0 ** (-np.arange(0, 64, 2, dtype=np.float32) / 64)).astype(np.float32)
    nc_kv = build_mix(T, True)
    nc_mix = build_mix(T, False)
    nc_ffn = build_ffn(T)
    xcur = x
    for l in range(L):
        modv = [modall[b, l * 6 * D:(l + 1) * 6 * D].reshape(6, D) for b in range(B)]
        wi = f(w_in[l])
        wq = f(mla_w_uq[l])
        wkv = f(mla_w_ukv[l])
        blocks = ([ar(j * 128, (j + 1) * 128) for j in range(4)] + [ar(1024 + j * 128, 1024 + (j + 1) * 128) for j in range(3)]
                  + [ar(1408 + j * 128, 1408 + (j + 1) * 128) for j in range(2)]
                  + [ar(1728 + j * 128, 1728 + (j + 1) * 128) for j in range(20)])
        shared = {
            "winF": tile_w(wi, 16, blocks),
            "winkr": tile_w(wi, 16, [ar(1664, 1728), np.concatenate([ar(1696, 1728), ar(1664, 1696)])]),
            "wukvk": tile_w(wkv, 2, [ar(h * 256, h * 256 + 128) for h in range(4)]),
            "wukvv": tile_w(wkv, 2, [np.concatenate([ar(h * 256 + 128, h * 256 + 256) for h in range(4)])]),
            "gkv": pl(mla_kv_norm[l], 2),
        }
        rope_cols = []
        for h in range(4):
            rope_cols.append(ar(h * 192 + 128, h * 192 + 192))
            rope_cols.append(np.concatenate([ar(h * 192 + 160, h * 192 + 192), ar(h * 192 + 128, h * 192 + 160)]))
        shared_mix = {
            "winv": tile_w(wi, 16, [ar(512, 1024)]),
            "wuqn": tile_w(wq, 3, [ar(h * 192, h * 192 + 128) for h in range(4)]),
            "wuqr": tile_w(wq, 3, rope_cols),
            "gq": pl(mla_q_norm[l], 3),
            "wsT": np.ascontiguousarray(np.transpose(f(gmlp_w_s[l]), (2, 0, 1)).reshape(128, 512)),
            "bsrow": f(gmlp_b_s[l]).reshape(1, 512),
            "lng": pl(gmlp_ln_g[l], 4),
            "lnbrow": f(gmlp_ln_b[l]).reshape(1, 512),
            "scw": np.ascontiguousarray(np.concatenate([pl(f(sconv_w[l])[k], 4) for k in range(3)], axis=1)),
            "cfw": np.ascontiguousarray(np.concatenate([pl(f(conf_w_dw[l])[k], 4) for k in range(31)], axis=1)),
            "cfb": pl(conf_b_dw[l], 4),
            "cfg": pl(conf_ln_g[l], 4),
            "cfbt": pl(conf_ln_b[l], 4),
            "wout": tile_w(f(w_out[l]), 16, [ar(j * 128, (j + 1) * 128) for j in range(16)]),
            "pmg": pl(post_mix_g[l], 16),
            "pmb": pl(post_mix_b[l], 16),
        }
        percore = []
        for i in cores:
            b, q = divmod(i, 4)
            xe = np.zeros((D, HALO + T), np.float32)
            if q > 0:
                xe[:, :HALO] = xcur[b, q * T - HALO:q * T].T
            xe[:, HALO:] = xcur[b, q * T:(q + 1) * T].T
            pos = np.arange(q * T, (q + 1) * T, dtype=np.float32)
            ang = pos[:, None] * inv_freq[None, :]
            cs, sn = np.cos(ang).astype(np.float32), np.sin(ang).astype(np.float32)
            percore.append({
                "x": xe, "mod": mod_layout(modv[b]),
                "CC": np.ascontiguousarray(np.concatenate([cs, cs], axis=1).T),
                "SS": np.ascontiguousarray(np.concatenate([-sn, sn], axis=1).T),
            })
        res = run_bass_kernel_spmd(nc_kv, [dict(shared, **pc) for pc in percore], core_ids=cores)
        kvs = [{k: np.asarray(res.results[i][k]) for k in ("KNo", "KRo", "Vo")} for i in cores]
        maps = []
        for i in cores:
            b, q = divmod(i, 4)
            others = [j for j in range(4) if j != q]
            sb = np.zeros((128, 3), np.float32)
            for s_, j in enumerate(others):
                sb[:, s_] = 0.0 if j < q else NEG
            m = dict(shared, **shared_mix, **percore[i])
            m["KNp"] = np.ascontiguousarray(np.concatenate([kvs[b * 4 + j]["KNo"] for j in others], axis=0))
            m["KRp"] = np.ascontiguousarray(np.concatenate([kvs[b * 4 + j]["KRo"] for j in others], axis=0))
            m["Vp"] = np.ascontiguousarray(np.concatenate([kvs[b * 4 + j]["Vo"] for j in others], axis=0))
            m["sbias"] = sb
            m["flag"] = np.full((128, 1), 0.0 if q == 0 else 1.0, np.float32)
            maps.append(m)
        res = run_bass_kernel_spmd(nc_mix, maps, core_ids=cores)
        xmid = [np.asarray(res.results[i]["xmid"]) for i in cores]
        del maps, kvs, shared, shared_mix
        wupt = tile_w(f(ffn_w_up[l]), 16, [ar(j * 128, (j + 1) * 128) for j in range(88)])
        wdnt = tile_w(f(ffn_w_down[l]), 44, [ar(j * 128, (j + 1) * 128) for j in range(16)])
        wct = np.ascontiguousarray(np.concatenate([pl(f(ffn_w_conv[l])[k], 44) for k in range(3)], axis=1))
        maps = []
        for i in cores:
            b, q = divmod(i, 4)
            xe = np.zeros((D, HALO + T), np.float32)
            if q > 0:
                xe[:, :HALO] = xmid[i - 1][:, T - HALO:]
            xe[:, HALO:] = xmid[i]
            maps.append({"xm": xe, "mod": mod_layout(modv[b]), "wup": wupt, "wdn": wdnt, "wc": wct,
                         "bc": pl(ffn_b_conv[l], 44), "pg": pl(post_ffn_g[l], 16), "pb": pl(post_ffn_b[l], 16),
                         "flag": np.full((128, 1), 0.0 if q == 0 else 1.0, np.float32)})
        res = run_bass_kernel_spmd(nc_ffn, maps, core_ids=cores)
        xnew = np.empty((B, S, D), np.float32)
        for i in cores:
            b, q = divmod(i, 4)
            xnew[b, q * T:(q + 1) * T] = np.asarray(res.results[i]["xo"]).T
        xcur = xnew
        del maps, wupt, wdnt
    return xcur


from contextlib import ExitStack


class Phase:
    def __init__(self, nc, tag):
        self.nc, self.tag, self.st = nc, tag, ExitStack()

    def al(self, name, shape, dtype):
        return self.st.enter_context(self.nc.sbuf_tensor(f"{self.tag}_{name}", shape, dtype))

    def close(self):
        self.st.close()


def _barrier(C, scratch_dram):
    P = C.P
    start = getattr(P, "bar_start", 0)
    deps = set()
    last = {}
    for o in P.ops[start:]:
        if o.dma:
            deps.add(o.idx)
        else:
            last[o.eng] = o.idx
    for e in ENGS:
        for o in reversed(P.ops[:start]):
            if o.eng == e and e not in last:
                last[e] = o.idx
                break
    deps |= set(last.values())
    fns = {
        "pe": lambda e: e.matmul(C.ps[7][:, 0:1], lhsT=C.ones[:], rhs=C.ones[:, 0:1], start=True, stop=True),
        "act": lambda e: e.activation(out=C.bsc[:, 0:1], in_=C.eps_ln[:], func=AF.Copy),
        "dve": lambda e: e.tensor_copy(out=C.bsc[:, 1:2], in_=C.eps_ln[:]),
        "pool": lambda e: e.memset(C.bsc[:, 2:3], 0.0),
        "sp": lambda e: e.dma_start(out=scratch_dram[0:1, 0:8], in_=scratch_dram[0:1, 8:16]),
    }
    for e in ENGS:
        idx = len(P.ops)
        P.ops.append(Op(e, fns[e], set(deps), e == "sp", idx))
    P.lastw[("ps", 7)] = len(P.ops) - 5
    P.readers[("ps", 7)] = []
    P.bar_start = len(P.ops)


def mix_phase(C, tag, cfg):
    nc, P = C.nc, C.P
    ph = Phase(nc, tag)
    al = ph.al
    K = lambda *a: (tag,) + a
    kv_only = cfg["kv_only"]
    xv = cfg["xv"]
    modt, mod1 = cfg["modt"], cfg["mod1"]
    W = cfg["W"]
    S = cfg["S"]
    KNo, KRo, Vo = cfg["KN"], cfg["KR"], cfg["V"]
    prev = cfg.get("prev")
    halo_first = cfg.get("halo_first", False)

    def ld(name, shape, src):
        t = al(name, shape, F32)
        P.dma("sp", t[:], src, [], [K(name)])
        return t

    gkvt = ld("gkvt", [128, 2], S["gkv"])
    wvv = al("wvv", [128, 2, 512], BF16)
    P.dma("sp", wvv[:], W["wukvvb"].rearrange("p (c n) -> p c n", c=2), [("wdram", id(W["wukvvb"]))], [K("wvv")])
    big = al("big", [128, NCH, 512], F32)
    hT = al("hT", [128, NCH, 512], BF16)
    wbuf = al("wbuf", [128, 3, 2048], BF16)
    sqkv = al("sqkv", [128, 2, 512], BF16)
    kvcg = al("kvcg", [128, 2, 512], BF16)
    rkv = al("rkv", [128, 512], F32)
    rq = rkv
    kst = al("kst", [128, 4, 512], BF16)
    krs = al("krs", [128, 512], BF16)[0:64]
    r1 = al("r1", [128, 512], F32)[0:64]
    r2 = al("r2", [128, 512], F32)[0:64]
    cct = al("cct", [128, 512], F32)[0:64]
    sst = al("sst", [128, 512], F32)[0:64]
    rt = al("rt", [128, 2, 1], F32)
    vst = al("vst", [128, 2, 512], BF16)
    if not kv_only:
        gqt = ld("gqt", [128, 3], S["gq"])
        lngt = ld("lngt", [128, 4], S["lng"])
        scwt = ld("scwt", [128, 12], S["scw"])
        cfwt = ld("cfwt", [128, 124], S["cfw"])
        cfbt_ = ld("cfbt_", [128, 4], S["cfb"])
        cfgt = ld("cfgt", [128, 4], S["cfg"])
        cfbtt = ld("cfbtt", [128, 4], S["cfbt"])
        pmgt = ld("pmgt", [128, 16], S["pmg"])
        pmbt = ld("pmbt", [128, 16], S["pmb"])
        wsTf = ld("wsTf", [128, 512], S["wsT"])
        bsr = ld("bsr", [1, 512], S["bsrow"])
        lnbr = ld("lnbr", [1, 512], S["lnbrow"])
        if prev is not None:
            tbt = ld("tbt", [128, prev["ntiles"]], prev["tb"])
        P.ms("pool", wsTf[64:128, :].rearrange("p (h i) -> p h i", h=4)[:, :, 0:64], 0.0, [K("wsTf")])
        wsTb = al("wsTb", [128, 512], BF16)
        P.cp("dve", wsTb[:], wsTf[:], [K("wsTf")], [K("wsTb")])
        onesf = al("onesf", [128, 128], F32)
        P.ms("pool", onesf[:], 1.0, [K("onesf")])
        rsw = al("rsw", [1, 512], F32)
        s = C.psn(4, 8)
        P.mm(C.ps[s][0:1, :], onesf[:, 0:1], wsTf[:], True, True, [K("onesf"), K("wsTf")], [("ps", s)])
        P.cp("dve", rsw[:], C.ps[s][0:1, :], [("ps", s)], [K("rsw")])
        Bm = al("Bm", [128, 512], F32)
        s = C.psn(4, 8)
        for h in range(4):
            sl = slice(h * 128, (h + 1) * 128)
            P.mm(C.ps[s][:, sl], lnbr[0:1, sl], rsw[0:1, sl], True, False, [K("lnbr"), K("rsw")], [("ps", s)])
            P.mm(C.ps[s][:, sl], onesf[0:1, :], bsr[0:1, sl], False, True, [K("onesf"), K("bsr")], [("ps", s)])
        P.cp("dve", Bm[:], C.ps[s][:], [("ps", s)], [K("Bm")])
        yT = al("yT", [128, NCH, 512], BF16)
        uT = al("uT", [128, 4, 512], BF16)
        wvb = al("wvb", [128, 4, 512], BF16)
        vn = al("vn", [128, 512], BF16)
        bst = al("bst", [128, 8], F32)
        sqq = al("sqq", [128, 3, 512], BF16)
        qcg = al("qcg", [128, 3, 512], BF16)
        QnT = al("QnT", [128, 4, 512], BF16)
        QrT = al("QrT", [128, 4, 512], BF16)[0:64]
        sbT = al("sbT", [128, 4, 512], BF16)
        scT = al("scT", [128, 512], F32)
        vf = scT
        zbuf = al("zbuf", [128, 516], F32)
        ybuf4 = al("ybuf", [128, 4, 544], F32)
        cz = al("cz", [128, 4, 2], F32)
        cy = al("cy", [128, 4, 30], F32)
        cacc = al("cacc", [128, 512], F32)
        gt = cacc
        rl = cacc
        cT = al("cT", [128, 4, 512], F32)
        knb = al("knb", [128, 3, 4, 128], BF16)
        vpb = al("vpb", [128, 3, 4, 128], BF16)
        krb = al("krb", [128, 3, 4, 128], BF16)[0:64]
        pT = al("pT", [128, 4, 512], BF16)
        xres = al("xres", [128, 2, 512], F32)
        obuf = al("obuf", [128, 2, 512], F32)
        xmid_v = cfg["xmid_v"]
        P.ms("pool", cz[:], 0.0, [K("cz", j) for j in range(4)])
        P.ms("pool", cy[:], 0.0, [K("cy", j) for j in range(4)])

    bgc = [0.0]

    def proj(dram2d, blk, M, nk, rhs, rkeys, G):
        k = P.nxt(K("wbuf"), 3)
        P.dma("sp", wbuf[:, k, :nk * M], dram2d[blk * 128:(blk + 1) * 128, :], [("wdram", id(dram2d))], [K("wbuf", k)])
        s = C.psn(0, 8)
        for c in range(nk):
            P.mm(C.ps[s][:M, :G], wbuf[:, k, c * M:(c + 1) * M], rhs(c), c == 0, c == nk - 1,
                 [K("wbuf", k)] + rkeys, [("ps", s)])
        P.flush(4 if nk >= 16 else 0)
        if nk >= 16:
            bgc[0] += cfg.get("bgn", 0.6)
            while bgc[0] >= 1.0:
                P.bg_flush(1)
                bgc[0] -= 1.0
        return s

    def stageL(t0, G, banks=None):
        C.ln_stream(xv, cfg.get("xkeys", []), t0, NCH, G, D, C.eps_ln,
                    lambda c: mod1[:, 16 + c:17 + c], lambda c: modt[:, c:c + 1],
                    lambda c: hT[:, c, :G], lambda c: K("hT"), banks=banks)

    def group(t0, G, gi, nxt):
        g4 = t0 // 128
        if kv_only:
            P.dma("sp", big[:, :, :G], xv[:, :, t0:t0 + G], cfg.get("xkeys", []), [K("big")])
            C.ln_fm(lambda c: big[:, c, :G], lambda c: K("big"), NCH, G, D, C.eps_ln,
                    lambda c: mod1[:, 16 + c:17 + c], lambda c: modt[:, c:c + 1],
                    lambda c: hT[:, c, :G], lambda c: K("hT"))
        hrhs = lambda c: hT[:, c, :G]
        hk = [K("hT")]
        if not kv_only:
            for j in range(4):
                yb = ybuf4[:, j, :]
                s = proj(W["winFb"], 25 + j, 128, NCH, hrhs, hk, G)
                P.act(scT[:, :G], C.ps[s][:, :G], AF.Sigmoid, [("ps", s)], [K("scT")])
                s = proj(W["winFb"], 21 + j, 128, NCH, hrhs, hk, G)
                P.cp("pool", yb[:, 0:30], cy[:, j, :], [K("cy", j)], [K("ybuf", j)])
                P.tt("dve", yb[:, 30:30 + G], C.ps[s][:, :G], scT[:, :G], ALU.mult, [("ps", s), K("scT")], [K("ybuf", j)])
                P.cp("pool", cy[:, j, :], yb[:, G:G + 30], [K("ybuf", j)], [K("cy", j)])
                P.defer(lambda j=j, yb=yb: P.ts("dve", cT[:, j, :G], yb[:, 30:30 + G], cfwt[:, 120 + j:121 + j], cfbt_[:, j:j + 1],
                                                ALU.mult, ALU.add, [K("ybuf", j), K("cfwt"), K("cfbt_")], [K("cT", j)]))
                for k in range(30):
                    P.defer(lambda j=j, yb=yb, k=k: P.stt("dve", cT[:, j, :G], yb[:, k:k + G], cfwt[:, 4 * k + j:4 * k + j + 1],
                                                          cT[:, j, :G], ALU.mult, ALU.add, [K("ybuf", j), K("cT", j)], [K("cT", j)]))
            if halo_first and gi == 0:
                for j in range(4):
                    P.ts("dve", cy[:, j, :], cy[:, j, :], cfg["flt"][:, 0:1], None, ALU.mult, None, [K("cy", j)], [K("cy", j)])
        P.dma("sp", cct[:, :G], cfg["CC"][:, t0:t0 + G], [], [K("cct")])
        P.dma("sp", sst[:, :G], cfg["SS"][:, t0:t0 + G], [], [K("sst")])
        for j in range(2):
            s = proj(W["winFb"], 7 + j, 128, NCH, hrhs, hk, G)
            P.act(sqkv[:, j, :G], C.ps[s][:, :G], AF.Square, [("ps", s)], [K("sqkv", j)])
            P.ts("dve", kvcg[:, j, :G], C.ps[s][:, :G], gkvt[:, j:j + 1], None, ALU.mult, None,
                 [("ps", s), K("gkvt")], [K("kvcg", j)])
        s = C.psn(4, 8)
        for j in range(2):
            P.mm(C.ps[s][:, :G], C.ones[:], sqkv[:, j, :G], j == 0, j == 1, ["ones", K("sqkv", j)], [("ps", s)])
        P.act(rkv[:, :G], C.ps[s][:, :G], AF.Sqrt, [("ps", s), "eps"], [K("rkv")], bias=C.eps_rms[:], scale=1.0 / 256)
        P.rcp(rkv[:, :G], rkv[:, :G], [K("rkv")], [K("rkv")])
        for h in range(4):
            s = proj(W["wukvkb"], h, 128, 2, lambda c: kvcg[:, c, :G], [K("kvcg", 0), K("kvcg", 1)], G)
            P.tt("dve", kst[:, h, :G], C.ps[s][:, :G], rkv[:, :G], ALU.mult, [("ps", s), K("rkv")], [K("kst")])
        sa = proj(W["winkrb"], 0, 64, NCH, hrhs, hk, G)
        sb_ = proj(W["winkrb"], 1, 64, NCH, hrhs, hk, G)
        P.tt("dve", r1[:, :G], C.ps[sa][:64, :G], cct[:, :G], ALU.mult, [("ps", sa), K("cct")], [K("r1")])
        P.tt("dve", r2[:, :G], C.ps[sb_][:64, :G], sst[:, :G], ALU.mult, [("ps", sb_), K("sst")], [K("r2")])
        P.tt("pool", krs[:, :G], r1[:, :G], r2[:, :G], ALU.add, [K("r1"), K("r2")], [K("krs")])
        for sub in range(G // 128):
            kt = g4 + sub
            cs = slice(sub * 128, (sub + 1) * 128)
            P.dma("pool", KNo[kt * 128:(kt + 1) * 128, :].rearrange("p (h k) -> p h k", h=4), kst[:, :, cs],
                  [K("kst")], [K("KNo", kt)])
            P.dma("pool", KRo[kt * 64:(kt + 1) * 64, :], krs[:, cs], [K("krs")], [K("KRo", kt)])
            s = C.psn(4, 8)
            for j in range(2):
                P.mm(C.ps[s][:, 0:1], sqkv[:, j, cs], C.ones[:, 0:1], j == 0, j == 1, [K("sqkv", j), "ones"], [("ps", s)])
            kk = P.nxt(K("rt"), 2)
            P.act(rt[:, kk, :], C.ps[s][:, 0:1], AF.Sqrt, [("ps", s), "eps"], [K("rt", kk)], bias=C.eps_rms[:], scale=1.0 / 256)
            P.rcp(rt[:, kk, :], rt[:, kk, :], [K("rt", kk)], [K("rt", kk)])
            s = C.psn(4, 8)
            for c in range(2):
                P.mm(C.ps[s][:, :], kvcg[:, c, cs], wvv[:, c, :], c == 0, c == 1, [K("kvcg", c), K("wvv")], [("ps", s)])
            P.act(vst[:, kk, :], C.ps[s][:, :], AF.Copy, [("ps", s), K("rt", kk)], [K("vst", kk)], scale=rt[:, kk, :])
            P.dma("pool", Vo[kt * 128:(kt + 1) * 128, :], vst[:, kk, :], [K("vst", kk)], [K("Vo", kt)])
        if kv_only:
            return
        for j in range(4):
            s = proj(W["winFb"], j, 128, NCH, hrhs, hk, G)
            P.act(uT[:, j, :G], C.ps[s][:, :G], AF.Gelu_apprx_tanh, [("ps", s)], [K("uT", j)])
        nsub = G // 128
        for c in range(NCH):
            k = P.nxt(K("wvb"), 4)
            P.dma("sp", wvb[:, k, :], W["winvb"][:, c * 512:(c + 1) * 512], [("wdram", id(W["winvb"]))], [K("wvb", k)])
            for sub in range(nsub):
                P.mm(C.ps[sub][:, :], hT[:, c, sub * 128:(sub + 1) * 128], wvb[:, k, :], c == 0, c == NCH - 1,
                     [K("wvb", k), K("hT")], [("ps", sub)])
        for sub in range(nsub):
            cs = slice(sub * 128, (sub + 1) * 128)
            P.act(vf[:], C.ps[sub][:, :], AF.Gelu_apprx_tanh, [("ps", sub)], [K("scT")])
            P.op("dve", lambda e: e.bn_stats(out=bst[:, 0:6], in_=vf[:]), [K("scT")], [K("bst")])
            P.op("dve", lambda e: e.bn_aggr(out=bst[:, 6:8], in_=bst[:, 0:6]), [K("bst")], [K("bst")])
            P.act(bst[:, 7:8], bst[:, 7:8], AF.Sqrt, [K("bst"), "eps"], [K("bst")], bias=C.eps_ln[:], scale=1.0)
            P.rcp(bst[:, 7:8], bst[:, 7:8], [K("bst")], [K("bst")])
            P.ts("dve", vn[:], vf[:], bst[:, 6:7], bst[:, 7:8], ALU.subtract, ALU.mult, [K("scT"), K("bst")], [K("vn")])
            s = C.psn(4, 8)
            for h in range(4):
                sl = slice(h * 128, (h + 1) * 128)
                P.mm(C.ps[s][:, sl], vn[:, sl], wsTb[:, sl], True, True, [K("vn"), K("wsTb")], [("ps", s)])
            for h in range(4):
                sl = slice(h * 128, (h + 1) * 128)
                P.stt("dve", gt[:, sl], C.ps[s][:, sl], lngt[:, h:h + 1], Bm[:, sl], ALU.mult, ALU.add,
                      [("ps", s), K("lngt"), K("Bm")], [K("cacc")])
                P.tt("dve", yT[:, h, cs], gt[:, sl], uT[:, h, cs], ALU.mult, [K("cacc"), K("uT", h)], [K("yT", h)])
        for j in range(3):
            s = proj(W["winFb"], 4 + j, 128, NCH, hrhs, hk, G)
            P.act(sqq[:, j, :G], C.ps[s][:, :G], AF.Square, [("ps", s)], [K("sqq", j)])
            P.ts("dve", qcg[:, j, :G], C.ps[s][:, :G], gqt[:, j:j + 1], None, ALU.mult, None, [("ps", s), K("gqt")], [K("qcg", j)])
        s2 = C.psn(0, 4)
        for j in range(3):
            P.mm(C.ps[s2][:, :G], C.ones[:], sqq[:, j, :G], j == 0, j == 2, ["ones", K("sqq", j)], [("ps", s2)])
        P.act(rq[:, :G], C.ps[s2][:, :G], AF.Sqrt, [("ps", s2), "eps"], [K("rkv")], bias=C.eps_rms[:], scale=1.0 / 384)
        P.rcp(rq[:, :G], rq[:, :G], [K("rkv")], [K("rkv")])
        qrhs = lambda c: qcg[:, c, :G]
        qk = [K("qcg", 0), K("qcg", 1), K("qcg", 2)]
        for h in range(4):
            s = proj(W["wuqnb"], h, 128, 3, qrhs, qk, G)
            P.tt("dve", QnT[:, h, :G], C.ps[s][:, :G], rq[:, :G], ALU.mult, [("ps", s), K("rkv")], [K("QnT", h)])
            sa = proj(W["wuqrb"], 2 * h, 64, 3, qrhs, qk, G)
            sb_ = proj(W["wuqrb"], 2 * h + 1, 64, 3, qrhs, qk, G)
            P.tt("dve", r1[:, :G], C.ps[sa][:64, :G], cct[:, :G], ALU.mult, [("ps", sa), K("cct")], [K("r1")])
            P.tt("dve", r2[:, :G], C.ps[sb_][:64, :G], sst[:, :G], ALU.mult, [("ps", sb_), K("sst")], [K("r2")])
            P.tt("pool", r1[:, :G], r1[:, :G], r2[:, :G], ALU.add, [K("r1"), K("r2")], [K("r1")])
            P.tt("dve", QrT[:, h, :G], r1[:, :G], rq[:64, :G], ALU.mult, [K("r1"), K("rkv")], [K("QrT", h)])
        for j in range(4):
            s = proj(W["winFb"], 9 + j, 128, NCH, hrhs, hk, G)
            P.act(sbT[:, j, :G], C.ps[s][:, :G], AF.Copy, [("ps", s)], [K("sbT", j)])
            s = proj(W["winFb"], 13 + j, 128, NCH, hrhs, hk, G)
            P.act(scT[:, :G], C.ps[s][:, :G], AF.Copy, [("ps", s)], [K("scT")])
            s = proj(W["winFb"], 17 + j, 128, NCH, hrhs, hk, G)
            P.cp("pool", zbuf[:, 0:2], cz[:, j, :], [K("cz", j)], [K("zbuf")])
            P.tt("dve", zbuf[:, 2:2 + G], C.ps[s][:, :G], scT[:, :G], ALU.mult, [("ps", s), K("scT")], [K("zbuf")])
            P.cp("pool", cz[:, j, :], zbuf[:, G:G + 2], [K("zbuf")], [K("cz", j)])
            P.ts("dve", cacc[:, :G], zbuf[:, 2:2 + G], scwt[:, 8 + j:9 + j], None, ALU.mult, None, [K("zbuf"), K("scwt")], [K("cacc")])
            P.stt("dve", cacc[:, :G], zbuf[:, 1:1 + G], scwt[:, 4 + j:5 + j], cacc[:, :G], ALU.mult, ALU.add, [K("zbuf"), K("cacc")], [K("cacc")])
            P.stt("dve", cacc[:, :G], zbuf[:, 0:G], scwt[:, j:j + 1], cacc[:, :G], ALU.mult, ALU.add, [K("zbuf"), K("cacc")], [K("cacc")])
            P.tt("dve", yT[:, 8 + j, :G], cacc[:, :G], sbT[:, j, :G], ALU.mult, [K("cacc"), K("sbT", j)], [K("yT", 8 + j)])
        if halo_first and gi == 0:
            for j in range(4):
                P.ts("dve", cz[:, j, :], cz[:, j, :], cfg["flt"][:, 0:1], None, ALU.mult, None, [K("cz", j)], [K("cz", j)])
        if nxt is not None:
            P.deferring = True
            stageL(*nxt, banks=(6, 7))
            P.deferring = False
        tiles = []
        if prev is not None:
            tiles += [("p", kt) for kt in range(prev["ntiles"])]
        tiles += [("o", kt) for kt in range(g4 + G // 128)]
        SB = (0, 1, 4, 5)
        PD = 3
        for h in range(4):
            hs = slice(h * 128, (h + 1) * 128)
            tl = []
            i = 0
            ch = 0
            while i < len(tiles):
                kind, kt0 = tiles[i]
                n = 1
                while n < 4 and i + n < len(tiles) and tiles[i + n][0] == kind:
                    n += 1
                for a in range(n):
                    tl.append(dict(kind=kind, kt=kt0 + a, a=a, ch=ch, load=(kind, kt0, n) if a == 0 else None))
                ch += 1
                i += n
            ntl = len(tl)
            cbuf = {}
            st = {}

            def stageS(idx):
                r = tl[idx]
                if r["load"] is not None:
                    kind, kt0, n = r["load"]
                    k = P.nxt(K("kvb"), 3)
                    cbuf[r["ch"]] = k
                    if kind == "p":
                        srcs = (prev["KN"], prev["KR"], prev["V"])
                        rdn = rdr = rdv = []
                    else:
                        srcs = (KNo, KRo, Vo)
                        rdn = [K("KNo", kt0 + a) for a in range(n)]
                        rdr = [K("KRo", kt0 + a) for a in range(n)]
                        rdv = [K("Vo", kt0 + a) for a in range(n)]
                    P.dma("sp", knb[:, k, 0:n, :], srcs[0][kt0 * 128:(kt0 + n) * 128, hs].rearrange("(t p) k -> p t k", p=128),
                          rdn, [K("knb", k)])
                    P.dma("sp", krb[:, k, 0:n, :], srcs[1][kt0 * 64:(kt0 + n) * 64, :].rearrange("(t p) k -> p t k", p=64),
                          rdr, [K("krb", k)])
                    P.dma("sp", vpb[:, k, 0:n, :], srcs[2][kt0 * 128:(kt0 + n) * 128, hs].rearrange("(t p) k -> p t k", p=128),
                          rdv, [K("vpb", k)])
                k = cbuf[r["ch"]]
                kt, a = r["kt"], r["a"]
                if r["kind"] == "p":
                    c0, bias, diag = 0, tbt[:, kt:kt + 1], False
                else:
                    d = kt - g4
                    c0 = max(0, d) * 128
                    diag = d >= 0
                    bias = cfg["hbt"][:, 0:1] if (halo_first and kt == 0 and g4 > 0) else 0.0
                sS = SB[P.nxt(K("sb"), 4)]
                P.mm(C.ps[sS][:, c0:G], knb[:, k, a, :], QnT[:, h, c0:G], True, False, [K("knb", k), K("QnT", h)], [("ps", sS)])
                P.mm(C.ps[sS][:, c0:G], krb[:, k, a, :], QrT[:, h, c0:G], False, True, [K("krb", k), K("QrT", h)], [("ps", sS)])
                kp = P.nxt(K("pT"), 4)
                P.act(pT[:, kp, c0:G], C.ps[sS][:, c0:G], AF.Exp, [("ps", sS)], [K("pT", kp)], bias=bias, scale=ATT_SCALE)
                if diag:
                    P.ms("dve", pT[64:128, kp, c0:c0 + 64], 0.0, [K("pT", kp)])
                st[idx] = (k, kp, c0, a)

            def stagePV(idx):
                k, kp, c0, a = st[idx]
                first, last = idx == 0, idx == ntl - 1
                P.mm(C.ps[2][:, c0:G], vpb[:, k, a, :], pT[:, kp, c0:G], first, last, [K("vpb", k), K("pT", kp)], [("ps", 2)])
                P.mm(C.ps[3][:, c0:G], C.ones[:], pT[:, kp, c0:G], first, last, ["ones", K("pT", kp)], [("ps", 3)])
                P.flush(1)

            for idx in range(ntl + PD):
                if idx < ntl:
                    stageS(idx)
                if idx >= PD:
                    stagePV(idx - PD)
            P.ts("dve", rl[:, :G], C.ps[3][:, :G], 1e-30, None, ALU.max, None, [("ps", 3)], [K("cacc")])
            P.rcp(rl[:, :G], rl[:, :G], [K("cacc")], [K("cacc")])
            P.tt("dve", yT[:, 4 + h, :G], C.ps[2][:, :G], rl[:, :G], ALU.mult, [("ps", 2), K("cacc")], [K("yT", 4 + h)])
        P.flush()
        C.ln_fm(lambda c: cT[:, c, :G], lambda c: K("cT", c), 4, G, 512, C.eps_ln,
                lambda c: cfgt[:, c:c + 1], lambda c: cfbtt[:, c:c + 1],
                lambda c: yT[:, 12 + c, :G], lambda c: K("yT", 12 + c), func=AF.Silu)
        for ob in range(NCH):
            kx = P.nxt(K("xres"), 2)
            P.dma("pool", xres[:, kx, :G], xv[:, ob, t0:t0 + G], cfg.get("xkeys", []), [K("xres", kx)])
            s = proj(W["woutb"], ob, 128, NCH, lambda c: yT[:, c, :G], [K("yT", c) for c in range(NCH)], G)
            P.act(big[:, ob, :G], C.ps[s][:, :G], AF.Identity, [("ps", s)], [K("big")], scale=mod1[:, 32 + ob:33 + ob], bias=0.0)
            P.stt("dve", big[:, ob, :G], xres[:, kx, :G], ALPHA, big[:, ob, :G], ALU.mult, ALU.add, [K("xres", kx), K("big")], [K("big")])
        cnt = {"c": 0}
        orig_act = P.act

        def act_hook(out, in_, func, reads, writes, bias=None, scale=None):
            orig_act(out, in_, func, reads, writes, bias=bias, scale=scale)
            if writes and isinstance(writes[0], tuple) and len(writes[0]) > 1 and writes[0][1] == "obuf":
                c = cnt["c"]
                cnt["c"] += 1
                P.dma("pool", xmid_v[:, c, t0:t0 + G], out, [writes[0]], [K("xmid", c, t0)])

        P.act = act_hook
        C.ln_fm(lambda c: big[:, c, :G], lambda c: K("big"), NCH, G, D, C.eps_ln,
                lambda c: pmgt[:, c:c + 1], lambda c: pmbt[:, c:c + 1],
                lambda c: obuf[:, c % 2, :G], lambda c: K("obuf", c % 2))
        P.act = orig_act

    glist = cfg["groups"]
    if not kv_only:
        stageL(*glist[0])
    for gi, (t0, G) in enumerate(glist):
        group(t0, G, gi, glist[gi + 1] if (gi + 1 < len(glist) and not kv_only) else None)
    if cfg.get("bg_end", True):
        P.bg_flush()
    ph.close()


def ffn_phase(C, tag, cfg):
    nc, P = C.nc, C.P
    ph = Phase(nc, tag)
    al = ph.al
    K = lambda *a: (tag,) + a
    xm_v, xo_v, ooff = cfg["xm_v"], cfg["xo_v"], cfg["ooff"]
    modt, mod1 = cfg["modt"], cfg["mod1"]
    wupb, wdnb = cfg["wupb"], cfg["wdnb"]
    S = cfg["S"]

    def ld(name, shape, src):
        t = al(name, shape, F32)
        P.dma("sp", t[:], src, [], [K(name)])
        return t

    wct = ld("wct", [128, 3 * NFF], S["wc"])
    bct = ld("bct", [128, NFF], S["bc"])
    pgt = ld("pgt", [128, 16], S["pg"])
    pbt = ld("pbt", [128, 16], S["pb"])
    big = al("big", [128, NCH, 512], F32)
    xin = al("xin", [128, NCH, 512], F32)
    hT = al("hT", [128, NCH, 512], BF16)
    ffT = al("ffT", [128, NFF, 512], BF16)
    wbuf = al("wbuf", [128, 4, 2048], BF16)
    wdbuf = al("wdbuf", [128, 2, 5632], BF16)
    gbuf = al("gbuf", [128, 2, 516], F32)
    acc = al("acc", [128, 2, 512], F32)
    carry = al("carry", [128, NFF, 2], F32)
    xres = al("xres", [128, 2, 512], F32)
    obuf = al("obuf", [128, 2, 512], F32)
    P.ms("pool", carry[:], 0.0, [K("carry", j) for j in range(NFF)])

    def stageL(t0, G, halo):
        P.dma("sp", xin[:, :, :G], xm_v[:, :, t0:t0 + G], [], [K("xin")])
        C.ln_fm(lambda c: xin[:, c, :G], lambda c: K("xin"), NCH, G, D, C.eps_ln,
                lambda c: mod1[:, 64 + c:65 + c], lambda c: modt[:, 48 + c:49 + c],
                lambda c: hT[:, c, :G], lambda c: K("hT"))

    def group(t0, G, halo, nxt):
        for j in range(NFF):
            blocks = [j] if halo else [j, NFF + j]
            pss = []
            for blk in blocks:
                k = P.nxt(K("wbuf"), 4)
                P.dma("sp", wbuf[:, k, :], wupb[blk * 128:(blk + 1) * 128, :], [("wdram", id(wupb))], [K("wbuf", k)])
                if blk % 12 == 0:
                    P.bg_flush(1)
                s = C.psn(0, 8)
                for c in range(NCH):
                    P.mm(C.ps[s][:, :G], wbuf[:, k, c * 128:(c + 1) * 128], hT[:, c, :G], c == 0, c == NCH - 1,
                         [K("wbuf", k), K("hT")], [("ps", s)])
                pss.append(s)
            sa = pss[0]
            if halo:
                P.ts("dve", carry[:, j, :], C.ps[sa][:, G - 2:G], cfg["flt"][:, 0:1], None, ALU.mult, None,
                     [("ps", sa)], [K("carry", j)])
                continue
            sb_ = pss[1]
            k = P.nxt(K("gbuf"), 2)
            gb = gbuf[:, k, :]
            P.cp("pool", gb[:, 0:2], carry[:, j, :], [K("carry", j)], [K("gbuf", k)])
            P.act(gb[:, 2:2 + G], C.ps[sa][:, :G], AF.Copy, [("ps", sa)], [K("gbuf", k)])
            a = acc[:, k, :G]
            P.ts("dve", a, gb[:, 2:2 + G], wct[:, 2 * NFF + j:2 * NFF + j + 1], bct[:, j:j + 1], ALU.mult, ALU.add,
                 [K("gbuf", k), K("wct"), K("bct")], [K("acc", k)])
            P.stt("dve", a, gb[:, 1:1 + G], wct[:, NFF + j:NFF + j + 1], a, ALU.mult, ALU.add, [K("gbuf", k), K("acc", k)], [K("acc", k)])
            P.stt("dve", a, gb[:, 0:G], wct[:, j:j + 1], a, ALU.mult, ALU.add, [K("gbuf", k), K("acc", k)], [K("acc", k)])
            P.cp("pool", carry[:, j, :], gb[:, G:G + 2], [K("gbuf", k)], [K("carry", j)])
            P.act(a, a, AF.Silu, [K("acc", k)], [K("acc", k)])
            P.tt("dve", ffT[:, j, :G], a, C.ps[sb_][:, :G], ALU.mult, [K("acc", k), ("ps", sb_)], [K("ffT", j)])
        if nxt is not None:
            stageL(*nxt)
        if halo:
            return
        for ob in range(NCH):
            k = P.nxt(K("wdbuf"), 2)
            P.dma("sp", wdbuf[:, k, :], wdnb[ob * 128:(ob + 1) * 128, :], [("wdram", id(wdnb))], [K("wdbuf", k)])
            kx = P.nxt(K("xres"), 2)
            P.dma("pool", xres[:, kx, :G], xm_v[:, ob, t0:t0 + G], [], [K("xres", kx)])
            s = C.psn(0, 8)
            for c in range(NFF):
                P.mm(C.ps[s][:, :G], wdbuf[:, k, c * 128:(c + 1) * 128], ffT[:, c, :G], c == 0, c == NFF - 1,
                     [K("wdbuf", k), K("ffT", c)], [("ps", s)])
            P.act(big[:, ob, :G], C.ps[s][:, :G], AF.Identity, [("ps", s)], [K("big")], scale=mod1[:, 80 + ob:81 + ob], bias=0.0)
            P.stt("dve", big[:, ob, :G], xres[:, kx, :G], ALPHA, big[:, ob, :G], ALU.mult, ALU.add, [K("xres", kx), K("big")], [K("big")])
        cnt = {"c": 0}
        orig_act = P.act

        def act_hook(out, in_, func, reads, writes, bias=None, scale=None):
            orig_act(out, in_, func, reads, writes, bias=bias, scale=scale)
            if writes and isinstance(writes[0], tuple) and len(writes[0]) > 1 and writes[0][1] == "obuf":
                c = cnt["c"]
                cnt["c"] += 1
                P.dma("pool", xo_v[:, c, t0 - ooff:t0 - ooff + G], out, [writes[0]], [K("xo", c, t0)])

        P.act = act_hook
        C.ln_fm(lambda c: big[:, c, :G], lambda c: K("big"), NCH, G, D, C.eps_ln,
                lambda c: pgt[:, c:c + 1], lambda c: pbt[:, c:c + 1],
                lambda c: obuf[:, c % 2, :G], lambda c: K("obuf", c % 2))
        P.act = orig_act

    glist = cfg["groups"]
    stageL(*glist[0])
    for gi, (t0, G, halo) in enumerate(glist):
        group(t0, G, halo, glist[gi + 1] if gi + 1 < len(glist) else None)
    P.bg_flush()
    ph.close()


MIXW = (("winF", 29 * 128, 2048), ("winkr", 2 * 128, 1024), ("wukvk", 4 * 128, 256), ("wukvv", 128, 1024),
        ("winv", 128, 8192), ("wuqn", 4 * 128, 384), ("wuqr", 8 * 128, 192), ("wout", 16 * 128, 2048))
FFNW = (("wup", 88 * 128, 2048), ("wdn", 16 * 128, 5632))
MIXS = (("gkv", [128, 2]), ("gq", [128, 3]), ("wsT", [128, 512]), ("bsrow", [1, 512]), ("lng", [128, 4]),
        ("lnbrow", [1, 512]), ("scw", [128, 12]), ("cfw", [128, 124]), ("cfb", [128, 4]), ("cfg", [128, 4]),
        ("cfbt", [128, 4]), ("pmg", [128, 16]), ("pmb", [128, 16]))
FFNS = (("wc", [128, 3 * NFF]), ("bc", [128, NFF]), ("pg", [128, 16]), ("pb", [128, 16]))


def build_fused(TA, T, L=2):
    nc = bass.Bass("TRN2", target_bir_lowering=False)
    NTA = TA // 128
    TE = HALO + T

    def din(name, shape, dtype=F32):
        return nc.dram_tensor(name, shape, dtype, kind="ExternalInput").ap()

    def dsc(name, shape, dtype):
        return nc.dram_tensor(name, shape, dtype).ap()

    xT = din("xT", [D, TA])
    cT = din("cT", [128, 16])
    wm = din("wm", [L * 96 * 128, 2048])
    bm = din("bm", [128, L * 96])
    CCa, SSa = din("CCa", [64, TA]), din("SSa", [64, TA])
    CCo, SSo = din("CCo", [64, TE]), din("SSo", [64, TE])
    sel, flag, hb, tb = din("sel", [128, 4]), din("flag", [128, 1]), din("hb", [128, 1]), din("tb", [128, 3 * (T // 128)])
    Wf, Wb, Sm = [], [], []
    for l in range(L):
        wf, wb, sm = {}, {}, {}
        for nm, r, c_ in MIXW + FFNW:
            wf[nm] = din(f"{nm}{l}", [r, c_])
            wb[nm + "b"] = dsc(f"{nm}b{l}", [r, c_], BF16)
        for nm, shp in MIXS + FFNS:
            sm[nm] = din(f"{nm}{l}", shp)
        Wf.append(wf)
        Wb.append(wb)
        Sm.append(sm)
    xo = nc.dram_tensor("xo", [D, T], F32, kind="ExternalOutput").ap()
    KN0, KR0, V0 = dsc("KN0", [NTA * 128, 512], BF16), dsc("KR0", [NTA * 64, 128], BF16), dsc("V0", [NTA * 128, 512], BF16)
    KNp, KRp, Vp = dsc("KNp", [NTA * 128, 512], BF16), dsc("KRp", [NTA * 64, 128], BF16), dsc("Vp", [NTA * 128, 512], BF16)
    NTE = TE // 128
    KN1, KR1, V1 = dsc("KN1", [NTE * 128, 512], BF16), dsc("KR1", [NTE * 64, 128], BF16), dsc("V1", [NTE * 128, 512], BF16)
    xmid0, x1 = dsc("xmid0", [D, TA], F32), dsc("x1", [D, TA], F32)
    x1e, xmid1e = dsc("x1e", [D, TE], F32), dsc("xmid1e", [D, TE], F32)
    scr = dsc("scr", [1, 64], F32)
    fm = lambda ap: ap.rearrange("(c p) t -> p c t", p=128)

    C = Ctx(nc)
    P = C.P
    C.bsc = nc.alloc_sbuf_tensor("bsc", [128, 4], F32)
    modall = C.load("modall_", [128, L * 96], bm)
    mod1all = nc.alloc_sbuf_tensor("mod1all", [128, L * 96], F32)
    selt = C.load("selt", [128, 4], sel)
    flt = C.load("flt", [128, 1], flag)
    hbt = C.load("hbt", [128, 1], hb)

    def cast(l, table):
        for nm, r, c_ in table:
            C.cast_weights_bg(Wb[l][nm + "b"], Wf[l][nm], r, c_)

    cast(0, MIXW)
    ph = Phase(nc, "mod")
    ca = ph.al("ca", [128, 16], F32)
    wbm = ph.al("wbm", [128, 2, 2048], F32)
    P.dma("sp", ca[:], cT, [], ["ca"])
    P.act(ca[:], ca[:], AF.Silu, ["ca"], ["ca"])
    for j in range(L * 96):
        k = j % 2
        P.dma("sp", wbm[:, k, :], wm[j * 128:(j + 1) * 128, :], [], [("wbm", k)])
        P.bg_flush(2)
        for kc in range(16):
            P.mm(C.ps[4][:, j:j + 1], wbm[:, k, kc * 128:(kc + 1) * 128], ca[:, kc:kc + 1], kc == 0, kc == 15,
                 [("wbm", k), "ca"], [("ps", 4)])
    P.tt("dve", modall[:], C.ps[4][:, 0:L * 96], modall[:], ALU.add, [("ps", 4), "modall_"], ["modall_"])
    P.ts("dve", mod1all[:], modall[:], 1.0, None, ALU.add, None, ["modall_"], ["consts"])
    P.bg_flush()
    ph.close()
    _barrier(C, scr)

    def mods(l):
        return modall[:, l * 96:(l + 1) * 96], mod1all[:, l * 96:(l + 1) * 96]

    cast(0, FFNW)
    m0, m01 = mods(0)
    mix_phase(C, "m0", dict(kv_only=False, xv=fm(xT), modt=m0, mod1=m01, W=Wb[0], S=Sm[0], KN=KN0, KR=KR0, V=V0,
                            CC=CCa, SS=SSa, xmid_v=fm(xmid0), groups=[(g * 512, 512) for g in range(TA // 512)]))
    _barrier(C, scr)
    cast(1, MIXW)
    ffn_phase(C, "f0", dict(xm_v=fm(xmid0), xo_v=fm(x1), ooff=0, modt=m0, mod1=m01, wupb=Wb[0]["wupb"], wdnb=Wb[0]["wdnb"],
                            S=Sm[0], groups=[(g * 512, 512, False) for g in range(TA // 512)]))
    _barrier(C, scr)
    cast(1, FFNW)
    m1, m11 = mods(1)
    mix_phase(C, "k1", dict(kv_only=True, xv=fm(x1), modt=m1, mod1=m11, W=Wb[1], S=Sm[1], KN=KNp, KR=KRp, V=Vp,
                            CC=CCa, SS=SSa, bgn=2, bg_end=False, groups=[(g * 512, 512) for g in range(3 * T // 512)]))
    _barrier(C, scr)
    ph = Phase(nc, "sel")
    pc = [ph.al(f"pc{j}", [128, NCH, 512], F32) for j in range(4)]
    x1v, x1ev = fm(x1), fm(x1e)
    pieces = [(0, HALO, True)] + [(HALO + i * 512, 512, False) for i in range(T // 512)]
    for (e0, w, is_halo) in pieces:
        js = [1, 2, 3] if is_halo else [0, 1, 2, 3]
        for j in js:
            s0 = j * T - HALO + e0
            P.dma("sp", pc[j][:, :, :w], x1v[:, :, s0:s0 + w], [], [("pc", j)])
        j0 = js[0]
        P.ts("dve", pc[j0][:, :, :w], pc[j0][:, :, :w], selt[:, j0:j0 + 1], None, ALU.mult, None, [("pc", j0), "selt"], [("pc", j0)])
        for j in js[1:]:
            P.stt("dve", pc[j0][:, :, :w], pc[j][:, :, :w], selt[:, j:j + 1], pc[j0][:, :, :w], ALU.mult, ALU.add,
                  [("pc", j), ("pc", j0)], [("pc", j0)])
        P.dma("pool", x1ev[:, :, e0:e0 + w], pc[j0][:, :, :w], [("pc", j0)], [("x1e", e0)])
    ph.close()
    _barrier(C, scr)
    own_groups = [(0, HALO)] + [(HALO + g * 512, 512) for g in range(T // 512)]
    mix_phase(C, "m1", dict(kv_only=False, xv=fm(x1e), modt=m1, mod1=m11, W=Wb[1], S=Sm[1], KN=KN1, KR=KR1, V=V1,
                            CC=CCo, SS=SSo, xmid_v=fm(xmid1e), groups=own_groups, halo_first=True, flt=flt, hbt=hbt,
                            bgn=2, prev=dict(KN=KNp, KR=KRp, V=Vp, ntiles=3 * (T // 128), tb=tb)))
    _barrier(C, scr)
    ffn_phase(C, "f1", dict(xm_v=fm(xmid1e), xo_v=fm(xo), ooff=HALO, modt=m1, mod1=m11, wupb=Wb[1]["wupb"], wdnb=Wb[1]["wdnb"],
                            S=Sm[1], flt=flt, groups=[(0, HALO, True)] + [(HALO + g * 512, 512, False) for g in range(T // 512)]))
    _barrier(C, scr)
    P.dma("sp", scr[0:1, 32:40], scr[0:1, 40:48], [], ["fin"])
    P.emit()
    return nc


def layer_inputs(l, w_in, gmlp_ln_g, gmlp_ln_b, gmlp_w_s, gmlp_b_s, mla_q_norm, mla_w_uq, mla_kv_norm, mla_w_ukv,
                 sconv_w, conf_w_dw, conf_b_dw, conf_ln_g, conf_ln_b, w_out, post_mix_g, post_mix_b, ffn_w_up,
                 ffn_w_conv, ffn_b_conv, ffn_w_down, post_ffn_g, post_ffn_b):
    f = lambda a: np.asarray(a, dtype=np.float32)
    ar = np.arange
    wi, wq, wkv = f(w_in[l]), f(mla_w_uq[l]), f(mla_w_ukv[l])
    blocks = ([ar(j * 128, (j + 1) * 128) for j in range(4)] + [ar(1024 + j * 128, 1024 + (j + 1) * 128) for j in range(3)]
              + [ar(1408 + j * 128, 1408 + (j + 1) * 128) for j in range(2)]
              + [ar(1728 + j * 128, 1728 + (j + 1) * 128) for j in range(20)])
    rope_cols = []
    for h in range(4):
        rope_cols.append(ar(h * 192 + 128, h * 192 + 192))
        rope_cols.append(np.concatenate([ar(h * 192 + 160, h * 192 + 192), ar(h * 192 + 128, h * 192 + 160)]))
    d = {
        "winF": tile_w(wi, 16, blocks),
        "winkr": tile_w(wi, 16, [ar(1664, 1728), np.concatenate([ar(1696, 1728), ar(1664, 1696)])]),
        "wukvk": tile_w(wkv, 2, [ar(h * 256, h * 256 + 128) for h in range(4)]),
        "wukvv": tile_w(wkv, 2, [np.concatenate([ar(h * 256 + 128, h * 256 + 256) for h in range(4)])]),
        "gkv": pl(mla_kv_norm[l], 2),
        "winv": tile_w(wi, 16, [ar(512, 1024)]),
        "wuqn": tile_w(wq, 3, [ar(h * 192, h * 192 + 128) for h in range(4)]),
        "wuqr": tile_w(wq, 3, rope_cols),
        "gq": pl(mla_q_norm[l], 3),
        "wsT": np.ascontiguousarray(np.transpose(f(gmlp_w_s[l]), (2, 0, 1)).reshape(128, 512)),
        "bsrow": f(gmlp_b_s[l]).reshape(1, 512),
        "lng": pl(gmlp_ln_g[l], 4),
        "lnbrow": f(gmlp_ln_b[l]).reshape(1, 512),
        "scw": np.ascontiguousarray(np.concatenate([pl(f(sconv_w[l])[k], 4) for k in range(3)], axis=1)),
        "cfw": np.ascontiguousarray(np.concatenate([pl(f(conf_w_dw[l])[k], 4) for k in range(31)], axis=1)),
        "cfb": pl(conf_b_dw[l], 4),
        "cfg": pl(conf_ln_g[l], 4),
        "cfbt": pl(conf_ln_b[l], 4),
        "wout": tile_w(f(w_out[l]), 16, [ar(j * 128, (j + 1) * 128) for j in range(16)]),
        "pmg": pl(post_mix_g[l], 16),
        "pmb": pl(post_mix_b[l], 16),
        "wup": tile_w(f(ffn_w_up[l]), 16, [ar(j * 128, (j + 1) * 128) for j in range(88)]),
        "wdn": tile_w(f(ffn_w_down[l]), 44, [ar(j * 128, (j + 1) * 128) for j in range(16)]),
        "wc": np.ascontiguousarray(np.concatenate([pl(f(ffn_w_conv[l])[k], 44) for k in range(3)], axis=1)),
        "bc": pl(ffn_b_conv[l], 44),
        "pg": pl(post_ffn_g[l], 16),
        "pb": pl(post_ffn_b[l], 16),
    }
    return {f"{k}{l}": v for k, v in d.items()}


def rope_tables(pos):
    inv_freq = (10000.0 ** (-np.arange(0, 64, 2, dtype=np.float32) / 64)).astype(np.float32)
    ang = pos.astype(np.float32)[:, None] * inv_freq[None, :]
    cs, sn = np.cos(ang).astype(np.float32), np.sin(ang).astype(np.float32)
    return (np.ascontiguousarray(np.concatenate([cs, cs], axis=1).T),
            np.ascontiguousarray(np.concatenate([-sn, sn], axis=1).T))


kernel_unfused = kernel


def kernel(x, c, w_mod, b_mod, **kw):
    f = lambda a: np.asarray(a, dtype=np.float32)
    x = f(x)
    B, S, _ = x.shape
    L = w_mod.shape[0]
    T = S // 4
    cores = list(range(8))
    shared = {}
    for l in range(L):
        shared.update(layer_inputs(l, **kw))
    wcat = np.concatenate([f(w_mod[l]) for l in range(L)], axis=1)
    bcat = np.concatenate([f(b_mod[l]) for l in range(L)], axis=0)
    shared["wm"] = tile_w(wcat, 16, [np.arange(j * 128, (j + 1) * 128) for j in range(L * 96)])
    shared["bm"] = pl(bcat, L * 96)
    shared["CCa"], shared["SSa"] = rope_tables(np.arange(S))
    del wcat
    xTs = [np.ascontiguousarray(x[b].T) for b in range(B)]
    cTs = [pl(f(c)[b], 16) for b in range(B)]
    maps = []
    for i in cores:
        b, q = divmod(i, 4)
        pos = np.arange(q * T - HALO, (q + 1) * T)
        cco, sso = rope_tables(np.maximum(pos, 0))
        selv = np.zeros((128, 4), np.float32)
        selv[:, q] = 1.0
        tbv = np.full((128, 3 * (T // 128)), NEG, np.float32)
        tbv[:, :max(0, q * (T // 128) - 1)] = 0.0
        m = dict(shared)
        m.update({"xT": xTs[b], "cT": cTs[b], "CCo": cco, "SSo": sso, "sel": selv,
                  "flag": np.full((128, 1), 0.0 if q == 0 else 1.0, np.float32),
                  "hb": np.full((128, 1), NEG if q == 0 else 0.0, np.float32), "tb": tbv})
        maps.append(m)
    nc = build_fused(S, T, L)
    res = run_bass_kernel_spmd(nc, maps, core_ids=cores)
    out = np.empty((B, S, D), np.float32)
    for i in cores:
        b, q = divmod(i, 4)
        out[b, q * T:(q + 1) * T] = np.asarray(res.results[i]["xo"]).T
    return out
```
